# Optimizing a Trainium2 kernel written in Bass

```python
import math
import jax, jax.numpy as jnp
from jax import lax
import numpy as np

D_MODEL = 1024
BATCH = 8
SEQ = 4096
DEPTH = 4

N_MIXERS = 2
N_RET_LAYERS = (DEPTH + N_MIXERS - 1) // N_MIXERS
N_NSA_LAYERS = DEPTH // N_MIXERS
NORM_EPS = 1e-6
NEG = -1e30

RET_HEADS = 4
RET_QK_DIM = D_MODEL // RET_HEADS
RET_V_DIM = 2 * D_MODEL // RET_HEADS
RET_CHUNK = 128
RET_ROPE_THETA = 10000.0
RET_IN = 2 * D_MODEL + 2 * (2 * D_MODEL)

NSA_HEADS = 16
NSA_KV_HEADS = 4
NSA_GROUP = NSA_HEADS // NSA_KV_HEADS
NSA_HEAD_DIM = D_MODEL // NSA_HEADS
ROPE_THETA = 500000.0
ROPE_DIM = NSA_HEAD_DIM // 4
CMP_BLOCK = 32
CMP_STRIDE = 16
CMP_HIDDEN = 4 * NSA_HEAD_DIM
SLC_BLOCK = 64
SLC_TOPK = 16
WINDOW = 512
NSA_Q_BLOCK = 32
N_BRANCH = 3
FORCE = 1e4
NSA_QD = NSA_HEADS * NSA_HEAD_DIM
NSA_KVD = NSA_KV_HEADS * NSA_HEAD_DIM
NSA_IN = NSA_QD + N_BRANCH * 2 * NSA_KVD + NSA_HEADS * N_BRANCH

FFN_HIDDEN = ((8 * D_MODEL // 3 + 255) // 256) * 256

kernel_name = "hybrid_retention_nsa_swiglu"


def rms_norm(x, gain):
    xf = x.astype(jnp.float32)
    y = xf * lax.rsqrt(jnp.mean(xf * xf, axis=-1, keepdims=True) + NORM_EPS)
    return (y * gain.astype(jnp.float32)).astype(x.dtype)


def apply_rotary(x, positions, rot_dim, theta):
    half = rot_dim // 2
    inv_freq = theta ** (-2.0 * jnp.arange(half, dtype=jnp.float32) / rot_dim)
    ang = positions.astype(jnp.float32)[..., None] * inv_freq
    cos = jnp.cos(ang)[:, :, None, :]
    sin = jnp.sin(ang)[:, :, None, :]
    xr = x[..., :rot_dim].astype(jnp.float32)
    x1, x2 = xr[..., :half], xr[..., half:]
    rot = jnp.concatenate([x1 * cos - x2 * sin, x2 * cos + x1 * sin], axis=-1).astype(x.dtype)
    return jnp.concatenate([rot, x[..., rot_dim:]], axis=-1)


def chunkwise_retention(q, k, v):
    B, T, H, dk = q.shape
    dv = v.shape[-1]
    C = RET_CHUNK
    N = T // C
    log_g = jnp.log(1.0 - 2.0 ** (-5.0 - jnp.arange(H, dtype=jnp.float32)))
    idx = jnp.arange(C, dtype=jnp.float32)
    diff = idx[:, None] - idx[None, :]
    intra = jnp.where(diff >= 0, jnp.exp(log_g[:, None, None] * jnp.maximum(diff, 0.0)), 0.0)
    q_decay = jnp.exp(log_g[:, None] * (idx + 1.0))[None, :, :, None]
    k_decay = jnp.exp(log_g[:, None] * (C - 1.0 - idx))[None, :, :, None]
    chunk_decay = jnp.exp(log_g * C)[None, :, None, None]

    def to_chunks(a):
        return a.astype(jnp.float32).reshape(B, N, C, H, a.shape[-1]).transpose(1, 0, 3, 2, 4)

    def step(state, inp):
        qc, kc, vc = inp
        s = jnp.einsum('bhid,bhjd->bhij', qc, kc) * intra
        o = (jnp.einsum('bhij,bhje->bhie', s, vc)
             + jnp.einsum('bhid,bhde->bhie', qc * q_decay, state))
        state = state * chunk_decay + jnp.einsum('bhjd,bhje->bhde', kc * k_decay, vc)
        return state, o

    state0 = jnp.zeros((B, H, dk, dv), jnp.float32)
    _, o = lax.scan(step, state0, (to_chunks(q), to_chunks(k), to_chunks(v)))
    return o.transpose(1, 0, 3, 2, 4).reshape(B, T, H, dv)


def retention_mixer(h, positions, w_in, w_out):
    B, T, _ = h.shape
    proj = h @ w_in
    q, k, v, g = jnp.split(proj, [D_MODEL, 2 * D_MODEL, 4 * D_MODEL], axis=-1)
    q = q.reshape(B, T, RET_HEADS, RET_QK_DIM)
    k = k.reshape(B, T, RET_HEADS, RET_QK_DIM) * (RET_QK_DIM ** -0.5)
    v = v.reshape(B, T, RET_HEADS, RET_V_DIM)
    q = apply_rotary(q, positions, RET_QK_DIM, RET_ROPE_THETA)
    k = apply_rotary(k, positions, RET_QK_DIM, RET_ROPE_THETA)
    o = chunkwise_retention(q, k, v)
    o = o * lax.rsqrt(jnp.mean(o * o, axis=-1, keepdims=True) + NORM_EPS)
    o = o.astype(h.dtype).reshape(B, T, 2 * D_MODEL) * jax.nn.silu(g)
    return o @ w_out


def nsa_mixer(h, positions, w_in, cmp_pos, cmp_w1, cmp_w2, w_out):
    B, T, _ = h.shape
    G, HG, HD = NSA_KV_HEADS, NSA_GROUP, NSA_HEAD_DIM
    scale = HD ** -0.5
    proj = h @ w_in
    q = proj[..., :NSA_QD].reshape(B, T, NSA_HEADS, HD)
    kv = proj[..., NSA_QD:NSA_QD + N_BRANCH * 2 * NSA_KVD].reshape(B, T, N_BRANCH, 2, G, HD)
    gates = jax.nn.sigmoid(proj[..., NSA_QD + N_BRANCH * 2 * NSA_KVD:].astype(jnp.float32))
    gates = gates.astype(h.dtype).reshape(B, T, NSA_HEADS, N_BRANCH)

    q = apply_rotary(q, positions, ROPE_DIM, ROPE_THETA)
    k_cmp_tok = apply_rotary(kv[:, :, 0, 0], positions, ROPE_DIM, ROPE_THETA)
    v_cmp_tok = kv[:, :, 0, 1]
    k_slc = apply_rotary(kv[:, :, 1, 0], positions, ROPE_DIM, ROPE_THETA)
    v_slc = kv[:, :, 1, 1]
    k_win = apply_rotary(kv[:, :, 2, 0], positions, ROPE_DIM, ROPE_THETA)
    v_win = kv[:, :, 2, 1]

    n_cmp = (T - CMP_BLOCK) // CMP_STRIDE + 1
    tok_idx = jnp.arange(n_cmp)[:, None] * CMP_STRIDE + jnp.arange(CMP_BLOCK)[None, :]

    def compress(a, pos_emb, w1, w2):
        blocks = a[:, tok_idx] + pos_emb[None, None, :, None, :]
        blocks = blocks.transpose(0, 1, 3, 2, 4).reshape(B, n_cmp, G, CMP_BLOCK * HD)
        return jax.nn.silu(blocks @ w1) @ w2

    k_cmp = compress(k_cmp_tok, cmp_pos[0], cmp_w1[0], cmp_w2[0])
    v_cmp = compress(v_cmp_tok, cmp_pos[1], cmp_w1[1], cmp_w2[1])
    cmp_end = jnp.arange(n_cmp) * CMP_STRIDE + CMP_BLOCK - 1

    NB = T // SLC_BLOCK
    n_sel = min(SLC_TOPK, NB)
    cmp_start = jnp.arange(n_cmp) * CMP_STRIDE
    slc_start = jnp.arange(NB) * SLC_BLOCK
    overlap = ((cmp_start[:, None] < slc_start[None, :] + SLC_BLOCK)
               & (cmp_start[:, None] + CMP_BLOCK > slc_start[None, :])).astype(jnp.float32)
    k_blocks = k_slc.reshape(B, NB, SLC_BLOCK, G, HD).transpose(0, 3, 1, 2, 4)
    v_blocks = v_slc.reshape(B, NB, SLC_BLOCK, G, HD).transpose(0, 3, 1, 2, 4)
    b_ix = jnp.arange(B)[:, None, None, None]
    g_ix = jnp.arange(G)[None, :, None, None]
    blk = jnp.arange(NB)

    k_win_pad = jnp.pad(k_win, ((0, 0), (WINDOW, 0), (0, 0), (0, 0)))
    v_win_pad = jnp.pad(v_win, ((0, 0), (WINDOW, 0), (0, 0), (0, 0)))
    WL = WINDOW + NSA_Q_BLOCK

    def attend(scores, mask, vals, eq_out):
        p = jax.nn.softmax(jnp.where(mask, scores, NEG), axis=-1)
        return p, jnp.einsum(eq_out, p.astype(vals.dtype), vals, preferred_element_type=jnp.float32)

    def q_block(n):
        q0 = n * NSA_Q_BLOCK
        qb = lax.dynamic_slice_in_dim(q, q0, NSA_Q_BLOCK, axis=1).reshape(B, NSA_Q_BLOCK, G, HG, HD)
        gb = lax.dynamic_slice_in_dim(gates, q0, NSA_Q_BLOCK, axis=1).reshape(B, NSA_Q_BLOCK, G, HG, N_BRANCH)
        t_pos = q0 + jnp.arange(NSA_Q_BLOCK)

        s_c = jnp.einsum('bqghd,bkgd->bghqk', qb, k_cmp, preferred_element_type=jnp.float32) * scale
        valid_c = cmp_end[None, :] <= t_pos[:, None]
        p_c, o_cmp = attend(s_c, valid_c, v_cmp, 'bghqk,bkgd->bqghd')
        p_c = p_c * valid_c
        o_cmp = o_cmp * jnp.any(valid_c, axis=-1)[None, :, None, None, None]

        imp = jnp.einsum('bghqk,kn->bgqn', p_c, overlap)
        cur = t_pos // SLC_BLOCK
        forced = (blk[None, :] == 0) | (blk[None, :] == cur[:, None]) | (blk[None, :] == cur[:, None] - 1)
        future = blk[None, :] > cur[:, None]
        imp = jnp.where(future, NEG, jnp.where(forced, imp + FORCE, imp))
        _, sel = lax.top_k(imp, n_sel)
        ks = k_blocks[b_ix, g_ix, sel].reshape(B, G, NSA_Q_BLOCK, n_sel * SLC_BLOCK, HD)
        vs = v_blocks[b_ix, g_ix, sel].reshape(B, G, NSA_Q_BLOCK, n_sel * SLC_BLOCK, HD)
        tok = (sel[..., None] * SLC_BLOCK + jnp.arange(SLC_BLOCK)).reshape(B, G, NSA_Q_BLOCK, n_sel * SLC_BLOCK)
        tmask = (tok <= t_pos[None, None, :, None])[:, :, None]
        s_s = jnp.einsum('bqghd,bgqkd->bghqk', qb, ks, preferred_element_type=jnp.float32) * scale
        _, o_slc = attend(s_s, tmask, vs, 'bghqk,bgqkd->bqghd')

        kw = lax.dynamic_slice_in_dim(k_win_pad, q0, WL, axis=1)
        vw = lax.dynamic_slice_in_dim(v_win_pad, q0, WL, axis=1)
        kpos = q0 - WINDOW + jnp.arange(WL)
        wmask = ((kpos[None, :] <= t_pos[:, None]) & (kpos[None, :] > t_pos[:, None] - WINDOW)
                 & (kpos[None, :] >= 0))
        s_w = jnp.einsum('bqghd,bkgd->bghqk', qb, kw, preferred_element_type=jnp.float32) * scale
        _, o_win = attend(s_w, wmask, vw, 'bghqk,bkgd->bqghd')

        gf = gb.astype(jnp.float32)
        o = gf[..., 0:1] * o_cmp + gf[..., 1:2] * o_slc + gf[..., 2:3] * o_win
        return o.astype(h.dtype).reshape(B, NSA_Q_BLOCK, NSA_QD)

    out = lax.map(q_block, jnp.arange(T // NSA_Q_BLOCK))
    out = out.transpose(1, 0, 2, 3).reshape(B, T, NSA_QD)
    return out @ w_out


def swiglu(h, w_gu, w_down):
    a, b = jnp.split(h @ w_gu, 2, axis=-1)
    return (jax.nn.silu(a) * b) @ w_down


def setup_inputs(seed: int = 0) -> dict:
    key = jax.random.key(seed)
    ks = jax.random.split(key, 16)
    f32 = jnp.float32

    def nrm(k, shape, fan_in):
        return jax.random.normal(k, shape, f32) * (fan_in ** -0.5)

    x = jax.random.normal(ks[0], (BATCH, SEQ, D_MODEL), f32)
    offs = jax.random.randint(ks[1], (BATCH, 1), 0, 1024, dtype=jnp.int32)
    positions = offs + jnp.arange(SEQ, dtype=jnp.int32)[None, :]
    norm_mix = 1.0 + 0.01 * jax.random.normal(ks[2], (DEPTH, D_MODEL), f32)
    norm_ffn = 1.0 + 0.01 * jax.random.normal(ks[3], (DEPTH, D_MODEL), f32)
    norm_final = 1.0 + 0.01 * jax.random.normal(ks[4], (D_MODEL,), f32)
    ret_w_in = nrm(ks[5], (N_RET_LAYERS, D_MODEL, RET_IN), D_MODEL)
    ret_w_out = nrm(ks[6], (N_RET_LAYERS, 2 * D_MODEL, D_MODEL), 2 * D_MODEL)
    nsa_w_in = nrm(ks[7], (N_NSA_LAYERS, D_MODEL, NSA_IN), D_MODEL)
    nsa_cmp_pos = 0.1 * jax.random.normal(ks[8], (N_NSA_LAYERS, 2, CMP_BLOCK, NSA_HEAD_DIM), f32)
    nsa_cmp_w1 = nrm(ks[9], (N_NSA_LAYERS, 2, CMP_BLOCK * NSA_HEAD_DIM, CMP_HIDDEN), CMP_BLOCK * NSA_HEAD_DIM)
    nsa_cmp_w2 = nrm(ks[10], (N_NSA_LAYERS, 2, CMP_HIDDEN, NSA_HEAD_DIM), CMP_HIDDEN)
    nsa_w_out = nrm(ks[11], (N_NSA_LAYERS, NSA_QD, D_MODEL), NSA_QD)
    ffn_w_gu = nrm(ks[12], (DEPTH, D_MODEL, 2 * FFN_HIDDEN), D_MODEL)
    ffn_w_down = nrm(ks[13], (DEPTH, FFN_HIDDEN, D_MODEL), FFN_HIDDEN)
    return {"x": x, "positions": positions, "norm_mix": norm_mix, "norm_ffn": norm_ffn,
            "norm_final": norm_final, "ret_w_in": ret_w_in, "ret_w_out": ret_w_out,
            "nsa_w_in": nsa_w_in, "nsa_cmp_pos": nsa_cmp_pos, "nsa_cmp_w1": nsa_cmp_w1,
            "nsa_cmp_w2": nsa_cmp_w2, "nsa_w_out": nsa_w_out, "ffn_w_gu": ffn_w_gu,
            "ffn_w_down": ffn_w_down}


def reference(x, positions, norm_mix, norm_ffn, norm_final, ret_w_in, ret_w_out, nsa_w_in,
              nsa_cmp_pos, nsa_cmp_w1, nsa_cmp_w2, nsa_w_out, ffn_w_gu, ffn_w_down):
    for i in range(DEPTH):
        hn = rms_norm(x, norm_mix[i])
        j = i // N_MIXERS
        if i % N_MIXERS == 0:
            x = x + retention_mixer(hn, positions, ret_w_in[j], ret_w_out[j])
        else:
            x = x + nsa_mixer(hn, positions, nsa_w_in[j], nsa_cmp_pos[j], nsa_cmp_w1[j],
                              nsa_cmp_w2[j], nsa_w_out[j])
        x = x + swiglu(rms_norm(x, norm_ffn[i]), ffn_w_gu[i], ffn_w_down[i])
    return rms_norm(x, norm_final)
```

```python
import numpy as np
import concourse.bass as bass
import concourse.mybir as mybir
from concourse.bass_utils import run_bass_kernel_spmd
from contextlib import ExitStack

F32 = mybir.dt.float32
BF16 = mybir.dt.bfloat16
I32 = mybir.dt.int32
AF = mybir.ActivationFunctionType
ALU = mybir.AluOpType

D = 1024
KC = 8
FH = 2816
FC = 22
EPS = 1e-6
TB = 512
NDMA_SEM = 8

ENGS = ("pe", "act", "dve", "pool", "sp")


class Res:
    __slots__ = ("w", "r")

    def __init__(self):
        self.w = None
        self.r = {}


class Prog:
    def __init__(self, nc):
        self.nc = nc
        self.eng = {"pe": nc.tensor, "act": nc.scalar, "dve": nc.vector, "pool": nc.gpsimd, "sp": nc.sync}
        self.q = {e: [] for e in ENGS}
        self.cnt = {e: 0 for e in ENGS}
        self.dcnt = {e: 0 for e in ENGS}
        self.waited = {e: {} for e in ENGS}
        self.final = []
        self.pend = {e: {} for e in ENGS}
        self.ep = 0
        self.maxval = {}

    def barrier(self):
        cur = {}
        for e in ENGS:
            if self.cnt[e]:
                cur[(e, "c", self.ep)] = self.cnt[e]
            for j in range(min(self.dcnt[e], NDMA_SEM)):
                cur[(e, "d", j)] = 16 * ((self.dcnt[e] - 1 - j) // NDMA_SEM + 1)
        for e in ENGS:
            for k2, v in cur.items():
                if self.pend[e].get(k2, 0) < v:
                    self.pend[e][k2] = v
        self.ep += 1
        self.cnt = {e: 0 for e in ENGS}

    def op(self, eng, fn, reads=(), writes=(), dma=False):
        deps = []
        for r in reads:
            if r.w is not None:
                deps.append(r.w)
        for w in writes:
            if w.w is not None:
                deps.append(w.w)
            deps.extend(w.r.items())
        if dma:
            k = self.dcnt[eng]
            self.dcnt[eng] += 1
            sk = (eng, "d", k % NDMA_SEM)
            val = 16 * (k // NDMA_SEM + 1)
            if k >= NDMA_SEM:
                deps.append((sk, val - 16))
        else:
            self.cnt[eng] += 1
            sk = (eng, "c", self.ep)
            val = self.cnt[eng]
        waits = {}
        wd = self.waited[eng]
        if self.pend[eng]:
            deps.extend(self.pend[eng].items())
            self.pend[eng] = {}
        for (k2, v) in deps:
            if eng == "pe" and k2[0] == "pe" and k2[1] == "c":
                continue
            if wd.get(k2, 0) >= v:
                continue
            if waits.get(k2, 0) < v:
                waits[k2] = v
        for k2, v in waits.items():
            wd[k2] = v
        self.q[eng].append((tuple(waits.items()), fn, sk, 16 if dma else 1))
        ev = (sk, val)
        if self.maxval.get(sk, 0) < val:
            self.maxval[sk] = val
        for r in reads:
            if r.r.get(sk, 0) < val:
                r.r[sk] = val
        for w in writes:
            w.w = ev
            w.r = {}
        return ev

    def emit(self):
        nc = self.nc
        keys = set()
        for e in ENGS:
            for (waits, fn, sk, inc) in self.q[e]:
                keys.add(sk)
        with ExitStack() as es:
            sems = {}
            for sk in sorted(keys):
                sems[sk] = es.enter_context(nc.semaphore("s_%s_%s_%d" % sk))
            block = es.enter_context(nc.Block())
            fin = [(sems[k2], v) for k2, v in sorted(self.maxval.items())]

            def mk(e):
                def body(engine):
                    for (waits, fn, sk, inc) in self.q[e]:
                        for (k2, v) in waits:
                            engine.wait_ge(sems[k2], v)
                        fn(engine).then_inc(sems[sk], inc)
                    if e == "sp":
                        for (s, v) in fin:
                            engine.wait_ge(s, v)
                return body

            block.tensor(mk("pe"))
            block.scalar(mk("act"))
            block.vector(mk("dve"))
            block.gpsimd(mk("pool"))
            block.sync(mk("sp"))


def ap_of(t, offset, pat):
    return bass.AP(tensor=t, offset=offset, ap=[list(p) for p in pat])


class Ctx:
    def __init__(self, nc, es):
        self.nc = nc
        self.es = es
        self.P = Prog(nc)
        self.n = 0

    def sb(self, shape, dt, es=None, name=None):
        self.n += 1
        t = (es or self.es).enter_context(self.nc.sbuf_tensor("%s%d" % (name or "sb", self.n), list(shape), dt))
        return t

    def ps(self, es=None):
        self.n += 1
        t = (es or self.es).enter_context(self.nc.psum_tensor("ps%d" % self.n, [128, 512], F32))
        return t

    def dram(self, shape, dt, name=None):
        self.n += 1
        return self.nc.dram_tensor("%s%d" % (name or "scr", self.n), list(shape), dt, kind="Internal")


def dma(c, eng, out, in_, reads, writes):
    return c.P.op(eng, lambda e: e.dma_start(out=out, in_=in_), reads, writes, dma=True)


def mm(c, out, lhsT, rhs, start, stop, reads, writes, **kw):
    return c.P.op("pe", lambda e: e.matmul(out, lhsT, rhs, start=start, stop=stop, **kw), reads, writes)


def tr(c, out, in_, ident, reads, writes):
    return c.P.op("pe", lambda e: e.transpose(out, in_, ident), reads, writes)


def act(c, out, in_, func, reads, writes, **kw):
    return c.P.op("act", lambda e: e.activation(out, in_, func, **kw), reads, writes)


def tt(c, eng, out, in0, in1, op, reads, writes):
    return c.P.op(eng, lambda e: e.tensor_tensor(out, in0, in1, op), reads, writes)


def ts(c, eng, out, in0, s1, s2, op0, op1, reads, writes):
    if op1 is None:
        return c.P.op(eng, lambda e: e.tensor_scalar(out, in0, s1, None, op0), reads, writes)
    return c.P.op(eng, lambda e: e.tensor_scalar(out, in0, s1, s2, op0, op1), reads, writes)


def stt(c, out, in0, scalar, in1, op0, op1, reads, writes):
    return c.P.op("dve", lambda e: e.scalar_tensor_tensor(out, in0, scalar, in1, op0, op1), reads, writes)


def cp(c, eng, out, in_, reads, writes):
    if eng == "act":
        return c.P.op("act", lambda e: e.copy(out, in_), reads, writes)
    return c.P.op(eng, lambda e: e.tensor_copy(out, in_), reads, writes)


def convert_weights(c, items):
    CH = 4096
    with ExitStack() as es:
        NB = 3
        stg = [c.sb([128, CH], F32, es, "cvf") for _ in range(NB)]
        out = [c.sb([128, CH], BF16, es, "cvb") for _ in range(NB)]
        rs = [Res() for _ in range(NB)]
        ro = [Res() for _ in range(NB)]
        i = 0
        engs = ["pool", "dve", "act"]
        for (src, dst, N, dres) in items:
            lo = 0
            while lo < N:
                n = min(CH, N - lo)
                b = i % NB
                dma(c, "sp", stg[b][:, :n], src(lo, lo + n), [], [rs[b]])
                cp(c, engs[i % 3], out[b][:, :n], stg[b][:, :n], [rs[b]], [ro[b]])
                dma(c, "sp", dst(lo, lo + n), out[b][:, :n], [ro[b]], [dres])
                lo += n
                i += 1


class NormBufs:
    def __init__(self, c, es, nx=4):
        self.nx = nx
        self.xt = [c.sb([128, D], F32, es, "xt") for _ in range(nx)]
        self.rxt = [Res() for _ in range(nx)]
        self.xs = c.sb([128, D], BF16, es, "xs")
        self.rxs = Res()
        self.junk = c.sb([128, D], BF16, es, "junk")
        self.rjunk = Res()
        self.ss = c.sb([128, 4], F32, es, "ss")
        self.rs = c.sb([128, 4], F32, es, "rs")
        self.rstd = c.sb([128, 4], F32, es, "rstd")
        self.rss = [Res() for _ in range(4)]
        self.hT = c.sb([128, KC, TB], BF16, es, "hT")
        self.rhT = Res()
        self.tp = c.ps(es)
        self.rtp = Res()


def norm_block(c, nb, x_dram, rx_tiles, tok0, gain_ap, consts):
    ident, rconst = consts["ident"], consts["r"]
    for t4 in range(4):
        r0 = tok0 + t4 * 128
        dma(c, "sp", nb.xt[t4 % nb.nx][:, :], x_dram[r0:r0 + 128, :], [rx_tiles[r0 // 128]], [nb.rxt[t4 % nb.nx]])
        act(c, nb.junk[:, :], nb.xt[t4 % nb.nx][:, :], AF.Square, [nb.rxt[t4 % nb.nx]], [nb.rjunk, nb.rss[t4]],
            accum_out=nb.ss[:, t4:t4 + 1])
        act(c, nb.rs[:, t4:t4 + 1], nb.ss[:, t4:t4 + 1], AF.Sqrt, [nb.rss[t4]], [nb.rss[t4]],
            scale=1.0 / D, bias=consts["eps"][:, 0:1])
        c.P.op("dve", lambda e, t4=t4: e.reciprocal(nb.rstd[:, t4:t4 + 1], nb.rs[:, t4:t4 + 1]),
               [nb.rss[t4]], [nb.rss[t4]])
        ts(c, "dve", nb.xs[:, :], nb.xt[t4 % nb.nx][:, :], nb.rstd[:, t4:t4 + 1], None, ALU.mult, None,
           [nb.rxt[t4 % nb.nx], nb.rss[t4]], [nb.rxs])
        tpb = nb.tp[:, :].bitcast(BF16)
        for kc in range(KC):
            tr(c, tpb[:, kc * 128:(kc + 1) * 128], nb.xs[:, kc * 128:(kc + 1) * 128], ident[:, :],
               [nb.rxs, rconst], [nb.rtp])
        for kc in range(KC):
            ts(c, "dve", nb.hT[:, kc, t4 * 128:(t4 + 1) * 128],
               tpb[:, kc * 128:(kc + 1) * 128], gain_ap[:, kc:kc + 1], None, ALU.mult, None,
               [nb.rtp, rconst], [nb.rhT])


def ffn_pass(c, T, x_dram, rx_tiles, wgu_s, r_wgu, wd_s, r_wd, gain_ap, consts):
    with ExitStack() as es:
        nb = NormBufs(c, es)
        wd = c.sb([128, FC, D], BF16, es, "wd")
        rwd = Res()
        NWB = 3
        wg = [c.sb([128, KC, 2, 128], BF16, es, "wg") for _ in range(NWB)]
        rwg = [Res() for _ in range(NWB)]
        actT = c.sb([128, FC, TB], BF16, es, "actT")
        ractT = Res()
        sa = [c.sb([128, TB], F32, es, "sa") for _ in range(2)]
        rsa = [Res() for _ in range(2)]
        pa = [c.ps(es) for _ in range(2)]
        pb = [c.ps(es) for _ in range(2)]
        rpa = [Res() for _ in range(2)]
        rpb = [Res() for _ in range(2)]
        py = [c.ps(es) for _ in range(2)]
        rpy = [Res() for _ in range(2)]
        xo = [c.sb([128, TB], F32, es, "xo") for _ in range(2)]
        rxo = [Res() for _ in range(2)]
        half = FC // 2
        dma(c, "sp", wd[:, 0:half, :], wd_s[:, 0:half * D].rearrange("p (c n) -> p c n", n=D), [r_wd], [rwd])
        dma(c, "sp", wd[:, half:FC, :], wd_s[:, half * D:FC * D].rearrange("p (c n) -> p c n", n=D), [r_wd], [rwd])
        it = 0
        for blk in range(T // TB):
            tok0 = blk * TB
            norm_block(c, nb, x_dram, rx_tiles, tok0, gain_ap, consts)
            for cc in range(FC):
                b = it % NWB
                pp = it % 2
                it += 1
                dma(c, "sp", wg[b][:, :, :, :],
                    wgu_s[:, cc * 2048:(cc + 1) * 2048].rearrange("p (k a j) -> p k a j", k=KC, a=2),
                    [r_wgu], [rwg[b]])
                for kc in range(KC):
                    mm(c, pa[pp][:, :], wg[b][:, kc, 0, :], nb.hT[:, kc, :], kc == 0, kc == KC - 1,
                       [rwg[b], nb.rhT], [rpa[pp]])
                for kc in range(KC):
                    mm(c, pb[pp][:, :], wg[b][:, kc, 1, :], nb.hT[:, kc, :], kc == 0, kc == KC - 1,
                       [rwg[b], nb.rhT], [rpb[pp]])
                act(c, sa[pp][:, :], pa[pp][:, :], AF.Silu, [rpa[pp]], [rsa[pp]])
                tt(c, "dve", actT[:, cc, :], sa[pp][:, :], pb[pp][:, :], ALU.mult, [rsa[pp], rpb[pp]], [ractT])
            j = 0
            for t4 in range(4):
                for nh in range(2):
                    pp = j % 2
                    j += 1
                    for cc in range(FC):
                        mm(c, py[pp][:, :], actT[:, cc, t4 * 128:(t4 + 1) * 128], wd[:, cc, nh * 512:(nh + 1) * 512],
                           cc == 0, cc == FC - 1, [ractT, rwd], [rpy[pp]])
                    tt(c, "dve", xo[pp][:, :], py[pp][:, :], nb.xt[t4][:, nh * 512:(nh + 1) * 512], ALU.add,
                       [rpy[pp], nb.rxt[t4]], [rxo[pp]])
                    r0 = tok0 + t4 * 128
                    dma(c, "pool", x_dram[r0:r0 + 128, nh * 512:(nh + 1) * 512], xo[pp][:, :], [rxo[pp]],
                        [rx_tiles[r0 // 128]])


def final_pass(c, T, x_dram, rx_tiles, out_dram, gfin_in, rconst, consts):
    with ExitStack() as es:
        gain_rep = c.sb([128, D], F32, es, "gfin")
        rconst = Res()
        dma(c, "sp", gain_rep[:, :], gfin_in.ap().partition_broadcast(128), [], [rconst])
        NBF = 2
        xt = [c.sb([128, D], F32, es, "fx") for _ in range(NBF)]
        rxt = [Res() for _ in range(NBF)]
        yo = [c.sb([128, D], F32, es, "fy") for _ in range(NBF)]
        ryo = [Res() for _ in range(NBF)]
        junk = c.sb([128, D], BF16, es, "fj")
        rj = Res()
        st = [c.sb([128, 4], F32, es, "fs") for _ in range(NBF)]
        rst = [Res() for _ in range(NBF)]
        rout = Res()
        for i in range(T // 128):
            b = i % NBF
            dma(c, "sp", xt[b][:, :], x_dram[i * 128:(i + 1) * 128, :], [rx_tiles[i]], [rxt[b]])
            act(c, junk[:, :], xt[b][:, :], AF.Square, [rxt[b]], [rj, rst[b]], accum_out=st[b][:, 0:1])
            act(c, st[b][:, 1:2], st[b][:, 0:1], AF.Sqrt, [rst[b]], [rst[b]], scale=1.0 / D,
                bias=consts["eps"][:, 0:1])
            c.P.op("dve", lambda e, b=b: e.reciprocal(st[b][:, 2:3], st[b][:, 1:2]), [rst[b]], [rst[b]])
            stt(c, yo[b][:, :], xt[b][:, :], st[b][:, 2:3], gain_rep[:, :], ALU.mult, ALU.mult,
                [rxt[b], rst[b], rconst], [ryo[b]])
            dma(c, "pool", out_dram[i * 128:(i + 1) * 128, :], yo[b][:, :], [ryo[b]], [rout])


TWO_PI = 6.283185307179586
C1 = 6.28125
C2 = TWO_PI - C1
PI_SAFE = 3.1415925


class TrigBufs:
    def __init__(self, c, es, n):
        self.u = c.sb([128, n], F32, es, "tg_u")
        self.ki = c.sb([128, n], I32, es, "tg_k")
        self.kf = c.sb([128, n], F32, es, "tg_kf")
        self.r = Res()


def trig(c, tb, ang, rang, out, rout, phase, scale=None):
    n = ang.shape[-1]
    ts(c, "dve", tb.u[:, :n], ang, 1.0 / TWO_PI, 0.5 + phase / TWO_PI, ALU.mult, ALU.add, [rang], [tb.r])
    cp(c, "dve", tb.ki[:, :n], tb.u[:, :n], [tb.r], [tb.r])
    cp(c, "dve", tb.kf[:, :n], tb.ki[:, :n], [tb.r], [tb.r])
    stt(c, tb.u[:, :n], tb.kf[:, :n], -C1, ang, ALU.mult, ALU.add, [tb.r, rang], [tb.r])
    stt(c, tb.u[:, :n], tb.kf[:, :n], -C2, tb.u[:, :n], ALU.mult, ALU.add, [tb.r], [tb.r])
    if phase != 0.0:
        ts(c, "dve", tb.u[:, :n], tb.u[:, :n], float(phase), None, ALU.add, None, [tb.r], [tb.r])
    ts(c, "dve", tb.kf[:, :n], tb.u[:, :n], -PI_SAFE, TWO_PI, ALU.is_lt, ALU.mult, [tb.r], [tb.r])
    tt(c, "dve", tb.u[:, :n], tb.u[:, :n], tb.kf[:, :n], ALU.add, [tb.r], [tb.r])
    ts(c, "dve", tb.u[:, :n], tb.u[:, :n], PI_SAFE, -PI_SAFE, ALU.min, ALU.max, [tb.r], [tb.r])
    if scale is None:
        act(c, out, tb.u[:, :n], AF.Sin, [tb.r], [rout])
    else:
        act(c, out, tb.u[:, :n], AF.Sin, [tb.r], [rout], scale=scale)


RH = 4
RC = 128


def ret_consts_host():
    h = np.arange(RH, dtype=np.float64)
    log_g = np.log(1.0 - 2.0 ** (-5.0 - h))
    idx = np.arange(RC, dtype=np.float64)
    diff = idx[None, :] - idx[:, None]
    m = np.where(diff[:, None, :] >= 0, np.exp(log_g[None, :, None] * np.maximum(diff[:, None, :], 0.0)), 0.0)
    qdec = np.exp(log_g[None, :] * (idx[:, None] + 1.0))
    kdec = np.exp(log_g[None, :] * (RC - 1.0 - idx[:, None]))
    cdec = np.exp(log_g * RC)
    half = 128
    invf = (np.float32(10000.0) ** (np.float32(-2.0) * np.arange(half, dtype=np.float32) / np.float32(256.0))).astype(np.float32)
    arr = np.concatenate([m.reshape(128, RH * 128), qdec, kdec, invf[:, None]], axis=1).astype(np.float32)
    return arr, [float(x) for x in cdec]


def ret_pass(c, T, x_dram, rx_tiles, pos_dram, wqk_s, wvg_s, wo_s, r_w, gain_ap, consts, rc_sb, cdec):
    rconst = consts["r"]
    ident = consts["ident"]
    maskT = rc_sb[:, 0:512]
    qdec = rc_sb[:, 512:516]
    kdec = rc_sb[:, 516:520]
    invf = rc_sb[:, 520:521]
    with ExitStack() as es:
        nb = NormBufs(c, es)
        wo = c.sb([128, 16, D], BF16, es, "wo")
        rwo = Res()
        st32 = c.sb([128, 2, RH, 512], F32, es, "st32")
        stbf = c.sb([128, 2, RH, 512], BF16, es, "stbf")
        rst32 = [[Res() for _ in range(RH)] for _ in range(2)]
        rstbf = [[Res() for _ in range(RH)] for _ in range(2)]
        qT = c.sb([128, KC, TB], BF16, es, "qT")
        kT = c.sb([128, KC, TB], BF16, es, "kT")
        rqT = [Res() for _ in range(KC)]
        rkT = [Res() for _ in range(KC)]
        ktok = [c.sb([128, D], BF16, es, "ktok") for _ in range(4)]
        rktok = [Res() for _ in range(4)]
        v = [c.sb([128, 2048], BF16, es, "v") for _ in range(4)]
        rv = [Res() for _ in range(4)]
        sg = [c.sb([128, 2048], BF16, es, "sg") for _ in range(4)]
        rsg = [Res() for _ in range(4)]
        og = c.sb([128, 2048], BF16, es, "og")
        rog = Res()
        ogT = c.sb([128, 16, 128], BF16, es, "ogT")
        rogT = Res()
        wqk = [c.sb([128, KC, 128], BF16, es, "wqk") for _ in range(2)]
        rwqk = [Res() for _ in range(2)]
        wvg = [c.sb([128, KC, 512], BF16, es, "wvg") for _ in range(2)]
        rwvg = [Res() for _ in range(2)]
        posi = c.sb([128, TB], I32, es, "posi")
        ang = c.sb([128, TB], F32, es, "ang")
        rang = Res()
        cosT = c.sb([128, TB], F32, es, "cosT")
        sinT = c.sb([128, TB], F32, es, "sinT")
        rcos = Res()
        rsin = Res()
        tb = TrigBufs(c, es, TB)
        t1 = [c.sb([128, TB], F32, es, "t1") for _ in range(2)]
        t2 = [c.sb([128, TB], F32, es, "t2") for _ in range(2)]
        rt1 = [Res() for _ in range(2)]
        rt2 = [Res() for _ in range(2)]
        sm = c.sb([128, 128], BF16, es, "sm")
        rsm = Res()
        ocs = c.sb([128, 512], F32, es, "ocs")
        rocs = Res()
        ofl = c.sb([128, 512], F32, es, "ofl")
        rofl = Res()
        oss = c.sb([128, 4], F32, es, "oss")
        ross = Res()
        xo = [c.sb([128, 512], F32, es, "rxo") for _ in range(2)]
        rxo = [Res() for _ in range(2)]
        ojunk = nb.junk
        rojunk = nb.rjunk
        pA = c.ps(es)
        pB = c.ps(es)
        rpA = Res()
        rpB = Res()
        pS = c.ps(es)
        rpS = Res()
        pO = c.ps(es)
        rpO = Res()
        pC = c.ps(es)
        rpC = Res()
        pU = [c.ps(es) for _ in range(2)]
        rpU = [Res() for _ in range(2)]
        tp = nb.tp
        rtp = nb.rtp

        c.P.op("pool", lambda e: e.memset(st32[:, :, :, :], 0.0), [], [r for rr in rst32 for r in rr])
        c.P.op("pool", lambda e: e.memset(stbf[:, :, :, :], 0.0), [], [r for rr in rstbf for r in rr])
        dma(c, "sp", wo[:, 0:8, :], wo_s[:, 0:8 * D].rearrange("p (c n) -> p c n", n=D), [r_w], [rwo])
        dma(c, "sp", wo[:, 8:16, :], wo_s[:, 8 * D:16 * D].rearrange("p (c n) -> p c n", n=D), [r_w], [rwo])
        iq = 0
        ivg = 0
        for blk in range(T // TB):
            tok0 = blk * TB
            norm_block(c, nb, x_dram, rx_tiles, tok0, gain_ap, consts)
            dma(c, "sp", posi[:, :], pos_dram[tok0:tok0 + TB].partition_broadcast(128), [], [rang])
            cp(c, "dve", ang[:, :], posi[:, :], [rang], [rang])
            ts(c, "dve", ang[:, :], ang[:, :], invf, None, ALU.mult, None, [rang, rconst], [rang])
            trig(c, tb, ang[:, :], rang, sinT[:, :], rsin, 0.0)
            trig(c, tb, ang[:, :], rang, cosT[:, :], rcos, np.pi / 2)
            for qk in range(2):
                dstT = qT if qk == 0 else kT
                rdst = rqT if qk == 0 else rkT
                for h in range(RH):
                    for a, (pp, rpp) in enumerate(((pA, rpA), (pB, rpB))):
                        m = qk * 8 + 2 * h + a
                        b = iq % 2
                        iq += 1
                        dma(c, "sp", wqk[b][:, :, :],
                            wqk_s[:, m * 1024:(m + 1) * 1024].rearrange("p (k j) -> p k j", k=KC), [r_w], [rwqk[b]])
                        for kc in range(KC):
                            mm(c, pp[:, :], wqk[b][:, kc, :], nb.hT[:, kc, :], kc == 0, kc == KC - 1,
                               [rwqk[b], nb.rhT], [rpp])
                    sc = 1.0 if qk == 0 else 1.0 / 16.0
                    stt(c, t1[0][:, :], pA[:, :], sc, cosT[:, :], ALU.mult, ALU.mult, [rpA, rcos], [rt1[0]])
                    stt(c, t2[0][:, :], pB[:, :], sc, sinT[:, :], ALU.mult, ALU.mult, [rpB, rsin], [rt2[0]])
                    tt(c, "pool", dstT[:, 2 * h, :], t1[0][:, :], t2[0][:, :], ALU.subtract, [rt1[0], rt2[0]],
                       [rdst[2 * h]])
                    stt(c, t1[1][:, :], pB[:, :], sc, cosT[:, :], ALU.mult, ALU.mult, [rpB, rcos], [rt1[1]])
                    stt(c, t2[1][:, :], pA[:, :], sc, sinT[:, :], ALU.mult, ALU.mult, [rpA, rsin], [rt2[1]])
                    tt(c, "pool", dstT[:, 2 * h + 1, :], t1[1][:, :], t2[1][:, :], ALU.add, [rt1[1], rt2[1]],
                       [rdst[2 * h + 1]])
            tpb = tp[:, :].bitcast(BF16)
            for t4 in range(4):
                for m in range(KC):
                    tr(c, tpb[:, m * 128:(m + 1) * 128], kT[:, m, t4 * 128:(t4 + 1) * 128], ident[:, :],
                       [rkT[m], rconst], [rtp])
                for h in range(RH):
                    ts(c, "dve", ktok[t4][:, h * 256:(h + 1) * 256], tpb[:, h * 256:(h + 1) * 256],
                       kdec[:, h:h + 1], None, ALU.mult, None, [rtp, rconst], [rktok[t4]])
            for grp in range(8):
                b = ivg % 2
                ivg += 1
                dma(c, "sp", wvg[b][:, :, :],
                    wvg_s[:, grp * 4096:(grp + 1) * 4096].rearrange("p (k j) -> p k j", k=KC), [r_w], [rwvg[b]])
                for t4 in range(4):
                    pp, rpp = (pA, rpA) if t4 % 2 == 0 else (pB, rpB)
                    for kc in range(KC):
                        mm(c, pp[:, :], nb.hT[:, kc, t4 * 128:(t4 + 1) * 128], wvg[b][:, kc, :], kc == 0,
                           kc == KC - 1, [nb.rhT, rwvg[b]], [rpp])
                    if grp < 4:
                        cp(c, "act", v[t4][:, grp * 512:(grp + 1) * 512], pp[:, :], [rpp], [rv[t4]])
                    else:
                        act(c, sg[t4][:, (grp - 4) * 512:(grp - 3) * 512], pp[:, :], AF.Silu, [rpp], [rsg[t4]])
            for t4 in range(4):
                cs = slice(t4 * 128, (t4 + 1) * 128)
                for h in range(RH):
                    hs = slice(h * 512, (h + 1) * 512)
                    for a in range(2):
                        mm(c, pS[:, 0:128], kT[:, 2 * h + a, cs], qT[:, 2 * h + a, cs], a == 0, a == 1,
                           [rkT[2 * h + a], rqT[2 * h + a]], [rpS])
                    tt(c, "dve", sm[:, :], pS[:, 0:128], maskT[:, h * 128:(h + 1) * 128], ALU.mult,
                       [rpS, rconst], [rsm])
                    mm(c, pO[:, :], sm[:, :], v[t4][:, hs], True, True, [rsm, rv[t4]], [rpO])
                    for a in range(2):
                        mm(c, pC[:, :], qT[:, 2 * h + a, cs], stbf[:, a, h, :], a == 0, a == 1,
                           [rqT[2 * h + a], rstbf[a][h]], [rpC])
                    for a in range(2):
                        mm(c, pU[a][:, :], ktok[t4][:, (2 * h + a) * 128:(2 * h + a + 1) * 128], v[t4][:, hs],
                           True, True, [rktok[t4], rv[t4]], [rpU[a]])
                    act(c, ocs[:, :], pC[:, :], AF.Identity, [rpC, rconst], [rocs], scale=qdec[:, h:h + 1])
                    tt(c, "dve", ofl[:, :], pO[:, :], ocs[:, :], ALU.add, [rpO, rocs], [rofl])
                    for a in range(2):
                        stt(c, st32[:, a, h, :], st32[:, a, h, :], cdec[h], pU[a][:, :], ALU.mult, ALU.add,
                            [rst32[a][h], rpU[a]], [rst32[a][h]])
                        cp(c, "pool", stbf[:, a, h, :], st32[:, a, h, :], [rst32[a][h]], [rstbf[a][h]])
                    act(c, ojunk[:, 0:512], ofl[:, :], AF.Square, [rofl], [rojunk, ross], accum_out=oss[:, 0:1])
                    act(c, oss[:, 1:2], oss[:, 0:1], AF.Sqrt, [ross], [ross], scale=1.0 / 512.0,
                        bias=consts["eps"][:, 0:1])
                    c.P.op("dve", lambda e: e.reciprocal(oss[:, 2:3], oss[:, 1:2]), [ross], [ross])
                    stt(c, og[:, hs], ofl[:, :], oss[:, 2:3], sg[t4][:, hs], ALU.mult, ALU.mult,
                        [rofl, ross, rsg[t4]], [rog])
                for half in range(2):
                    for f in range(8):
                        fc = half * 8 + f
                        tr(c, tpb[:, f * 128:(f + 1) * 128], og[:, fc * 128:(fc + 1) * 128], ident[:, :],
                           [rog, rconst], [rtp])
                    cp(c, "act", ogT[:, half * 8:(half + 1) * 8, :],
                       tpb[:, :].rearrange("p (f t) -> p f t", f=8), [rtp], [rogT])
                for nh in range(2):
                    pp, rpp = (pA, rpA) if nh == 0 else (pB, rpB)
                    for fc in range(16):
                        mm(c, pp[:, :], ogT[:, fc, :], wo[:, fc, nh * 512:(nh + 1) * 512], fc == 0, fc == 15,
                           [rogT, rwo], [rpp])
                    tt(c, "dve", xo[nh][:, :], pp[:, :], nb.xt[t4][:, nh * 512:(nh + 1) * 512], ALU.add,
                       [rpp, nb.rxt[t4]], [rxo[nh]])
                    r0 = tok0 + t4 * 128
                    dma(c, "pool", x_dram[r0:r0 + 128, nh * 512:(nh + 1) * 512], xo[nh][:, :], [rxo[nh]],
                        [rx_tiles[r0 // 128]])


def lay_ret(w_in, w_out):
    a = w_in[:, :2048].reshape(KC, 128, 16, 128)
    wqk = np.ascontiguousarray(a.transpose(1, 2, 0, 3).reshape(128, -1))
    b = w_in[:, 2048:].reshape(KC, 128, 8, 512)
    wvg = np.ascontiguousarray(b.transpose(1, 2, 0, 3).reshape(128, -1))
    wo = np.ascontiguousarray(w_out.reshape(16, 128, D).transpose(1, 0, 2).reshape(128, -1))
    return wqk, wvg, wo


NG = 4
HD = 64
NEGB = -30000.0


def nsa_consts_host(T):
    NQ = T // 128
    NB = T // 64
    n_cmp = (T - 32) // 16 + 1
    p = np.arange(128)
    pm = p % 64
    f = (pm % 8).astype(np.float32)
    invf = np.where(pm < 16, np.float32(500000.0) ** (np.float32(-2.0) * f / np.float32(16.0)), 0.0).astype(np.float32)
    sgn = np.where(pm < 8, -1.0, 1.0).astype(np.float32)
    partner = np.where(pm < 8, p + 8, np.where(pm < 16, p - 8, p))
    Pm = np.zeros((128, 128), np.float32)
    Pm[partner, p] = 1.0
    j = np.arange(128)[:, None]
    q = np.arange(128)[None, :]
    tri = (j <= q).astype(np.float32)
    upper = (j > q).astype(np.float32)
    ci = np.arange(256)
    nb = np.arange(64)
    ovl = ((ci[:, None] * 16 < nb[None, :] * 64 + 64) & (ci[:, None] * 16 + 32 > nb[None, :] * 64)).astype(np.float32)
    ovl[n_cmp:] = 0.0
    ovl2 = ovl.reshape(2, 128, 64).transpose(1, 0, 2).reshape(128, 128)
    maskc = np.zeros((128, 17, 128), np.float32)
    for n in range(17):
        maskc[:, n, :] = (16 * j + 31 - q <= 128 * n)
    small = np.concatenate([invf[:, None], sgn[:, None], Pm, tri, upper, ovl2, maskc.reshape(128, -1)], axis=1)
    eall = np.zeros((128, T), np.float32)
    key = np.arange(T)
    for b in range(min(64, NB)):
        eall[64 + b] = (key // 64 == b)
    addm = np.zeros((128, NQ, 64), np.float32)
    for n in range(NQ):
        t = 128 * n + np.arange(128)
        cur = t // 64
        blk = np.arange(64)[None, :]
        forced = (blk == 0) | (blk == cur[:, None]) | (blk == cur[:, None] - 1)
        future = blk > cur[:, None]
        addm[:, n, :] = np.where(future, -1e30, np.where(forced, 1e4, 0.0))
    return (np.ascontiguousarray(small, np.float32), eall, np.ascontiguousarray(addm.reshape(128, -1), np.float32))


NSMALL = 2 + 128 * 4 + 17 * 128


def lay_nsa(w_in, cmp_pos, cmp_w1, cmp_w2, w_out):
    cols = []
    for m in range(8):
        cols.append(np.arange(m * 128, (m + 1) * 128))
    base = 1024
    for (br, kv) in ((0, 0), (0, 1), (1, 0), (2, 0)):
        for gp in range(2):
            cols.append(base + br * 512 + kv * 256 + gp * 128 + np.arange(128))
    cols = np.concatenate(cols)
    a = w_in[:, cols].reshape(KC, 128, 16, 128)
    wfm = np.ascontiguousarray(a.transpose(1, 2, 0, 3).reshape(128, -1))
    tcols = np.concatenate([base + 1 * 512 + 256 + np.arange(256), base + 2 * 512 + 256 + np.arange(256),
                            2560 + np.arange(48)])
    b = w_in[:, tcols].reshape(KC, 128, 560)
    wtm = np.ascontiguousarray(b.transpose(1, 0, 2).reshape(128, -1))
    w1 = cmp_w1.reshape(2, 32, 64, 256).transpose(0, 2, 1, 3).reshape(128, 32 * 256)
    posc = cmp_pos.transpose(0, 2, 1).reshape(128, 32)
    w2 = cmp_w2.reshape(2, 2, 128, 64).transpose(2, 0, 1, 3).reshape(128, 256)
    wo = w_out.reshape(8, 128, D).transpose(1, 0, 2).reshape(128, -1)
    pack = np.concatenate([wfm, wtm, w1, posc, w2, wo], axis=1)
    return np.ascontiguousarray(pack, np.float32)


NSA_OFF = {}
_o = 0
for _k, _n in (("wfm", 16 * 1024), ("wtm", KC * 560), ("w1", 32 * 256), ("posc", 32), ("w2", 256), ("wo", 8 * D)):
    NSA_OFF[_k] = (_o, _o + _n)
    _o += _n
NSA_WTOT = _o


def bc_mid(ap2d, n):
    return ap2d.unsqueeze(1).broadcast_to([ap2d.shape[0], n, ap2d.shape[1]])


def nsa_layer(c, T, x_dram, rx_tiles, pos_dram, w_s, r_w, gain_ap, consts, ncs_in, eall_in, addm_in, qS):
    rconst = consts["r"]
    ident = consts["ident"]
    NQ = T // 128
    n_cmp = (T - 32) // 16 + 1

    def W(k):
        return w_s[:, NSA_OFF[k][0]:NSA_OFF[k][1]]

    with ExitStack() as eo:
        Lslc = c.sb([128, NG, T], BF16, eo, "Lslc")
        Lwin = c.sb([64, NG, T], BF16, eo, "Lwin")
        Vslc = c.sb([128, NQ, NG, 65], BF16, eo, "Vslc")
        Vwin = c.sb([128, NQ, NG, 65], BF16, eo, "Vwin")
        KcT = c.sb([64, NG, 256], BF16, eo, "KcT")
        Vc = c.sb([128, 2, NG, 65], BF16, eo, "Vc")
        gates = c.sb([128, NQ, 48], F32, eo, "gates")
        cb = c.sb([128, 128 * 4 + 17 * 128], BF16, eo, "cb")
        rL = Res()
        rV = Res()
        rKc = Res()
        rG = Res()
        rqS = Res()
        Pm = cb[:, 0:128]
        tri = cb[:, 128:256]
        upper = cb[:, 256:384]
        ovl = cb[:, 384:512]
        maskc = cb[:, 512:512 + 17 * 128]
        isg = c.sb([128, 2], F32, eo, "isg")
        invf = isg[:, 0:1]
        sgn = isg[:, 1:2]
        with ExitStack() as et:
            ncs = c.sb([128, NSMALL], F32, et, "ncs")
            rn = Res()
            dma(c, "sp", ncs[:, :], ncs_in[:, :], [], [rn])
            cp(c, "dve", cb[:, :], ncs[:, 2:2 + 128 * 4 + 17 * 128], [rn], [rconst])
            cp(c, "dve", isg[:, :], ncs[:, 0:2], [rn], [rconst])
        c.P.barrier()
        c.P.op("pool", lambda e: e.memset(Vslc[:, :, :, :], 1.0), [], [rV])
        c.P.op("pool", lambda e: e.memset(Vwin[:, :, :, :], 1.0), [], [rV])
        c.P.op("pool", lambda e: e.memset(Vc[:, :, :, :], 0.0), [], [rKc])
        c.P.op("pool", lambda e: e.memset(Vc[:, :, :, 64:65], 1.0), [], [rKc])
        c.P.op("pool", lambda e: e.memset(KcT[:, :, :], 0.0), [], [rKc])

        with ExitStack() as ex:
            Acmp = c.sb([128, NG, T], BF16, ex, "Acmp")
            rA = Res()
            with ExitStack() as es:
                nb = NormBufs(c, es, nx=2)
                est = c.sb([128, 512], F32, es, "est")
                rest = Res()
                for ec in range(T // 512):
                    dma(c, "sp", est[64:128, :], eall_in[64:128, ec * 512:(ec + 1) * 512], [], [rest])
                    for g in range(NG):
                        cp(c, "pool", Lslc[64:128, g, ec * 512:(ec + 1) * 512], est[64:128, :], [rest], [rL])
                wtm = c.sb([128, KC, 560], BF16, es, "wtm")
                rwtm = Res()
                dma(c, "sp", wtm[:, :, :], W("wtm").rearrange("p (k j) -> p k j", k=KC), [r_w], [rwtm])
                wfm = [c.sb([128, KC, 128], BF16, es, "wfm") for _ in range(2)]
                rwfm = [Res() for _ in range(2)]
                posi = c.sb([128, TB], I32, es, "nposi")
                ang = c.sb([128, TB], F32, es, "nang")
                rang = Res()
                Ct = c.sb([128, TB], F32, es, "nC")
                St = c.sb([128, TB], F32, es, "nS")
                rC = Res()
                rS = Res()
                tb = TrigBufs(c, es, TB)
                xb = c.sb([128, TB], BF16, es, "xb")
                rxb = Res()
                t1 = c.sb([128, TB], F32, es, "nt1")
                t2 = c.sb([128, TB], F32, es, "nt2")
                rt1 = Res()
                rt2 = Res()
                kr = [c.sb([128, TB], BF16, es, "kr") for _ in range(2)]
                rkr = [Res() for _ in range(2)]
                pX = [c.ps(es) for _ in range(2)]
                rpX = [Res() for _ in range(2)]
                pP = c.ps(es)
                rpP = Res()
                ik = 0
                for blk in range(T // TB):
                    tok0 = blk * TB
                    norm_block(c, nb, x_dram, rx_tiles, tok0, gain_ap, consts)
                    dma(c, "sp", posi[:, :], pos_dram[tok0:tok0 + TB].partition_broadcast(128), [], [rang])
                    cp(c, "dve", ang[:, :], posi[:, :], [rang], [rang])
                    ts(c, "dve", ang[:, :], ang[:, :], invf, None, ALU.mult, None, [rang, rconst], [rang])
                    trig(c, tb, ang[:, :], rang, St[:, :], rS, 0.0, scale=sgn)
                    trig(c, tb, ang[:, :], rang, Ct[:, :], rC, np.pi / 2)
                    for m in range(16):
                        b = ik % 2
                        ik += 1
                        dma(c, "sp", wfm[b][:, :, :],
                            W("wfm")[:, m * 1024:(m + 1) * 1024].rearrange("p (k j) -> p k j", k=KC), [r_w], [rwfm[b]])
                        for kc in range(KC):
                            mm(c, pX[b][:, :], wfm[b][:, kc, :], nb.hT[:, kc, :], kc == 0, kc == KC - 1,
                               [rwfm[b], nb.rhT], [rpX[b]])
                        rot = not (m in (10, 11))
                        if rot:
                            cp(c, "act", xb[:, :], pX[b][:, :], [rpX[b]], [rxb])
                            mm(c, pP[:, :], Pm, xb[:, :], True, True, [rconst, rxb], [rpP])
                            tt(c, "dve", t1[:, :], pP[:, :], St[:, :], ALU.mult, [rpP, rS], [rt1])
                            tt(c, "pool", t2[:, :], xb[:, :], Ct[:, :], ALU.mult, [rxb, rC], [rt2])
                            tt(c, "pool", kr[b][:, :], t1[:, :], t2[:, :], ALU.add, [rt1, rt2], [rkr[b]])
                        else:
                            cp(c, "act", kr[b][:, :], pX[b][:, :], [rpX[b]], [rkr[b]])
                        ts_ = slice(tok0, tok0 + TB)
                        if m < 8:
                            dma(c, "pool", qS[:, 2 * m, ts_], kr[b][0:64, :], [rkr[b]], [rqS])
                            dma(c, "pool", qS[:, 2 * m + 1, ts_], kr[b][64:128, :], [rkr[b]], [rqS])
                        else:
                            gp = (m - 8) % 2
                            kind = (m - 8) // 2
                            for hh in range(2):
                                g = gp * 2 + hh
                                src = kr[b][hh * 64:(hh + 1) * 64, :]
                                if kind == 0:
                                    dma(c, "pool", Acmp[0:64, g, ts_], src, [rkr[b]], [rA])
                                elif kind == 1:
                                    dma(c, "pool", Acmp[64:128, g, ts_], src, [rkr[b]], [rA])
                                elif kind == 2:
                                    dma(c, "pool", Lslc[0:64, g, ts_], src, [rkr[b]], [rL])
                                else:
                                    dma(c, "pool", Lwin[0:64, g, ts_], src, [rkr[b]], [rL])
                    for t4 in range(4):
                        n = (tok0 // 128) + t4
                        b = t4 % 2
                        for kc in range(KC):
                            mm(c, pX[b][:, :], nb.hT[:, kc, t4 * 128:(t4 + 1) * 128], wtm[:, kc, 0:512], kc == 0,
                               kc == KC - 1, [nb.rhT, rwtm], [rpX[b]])
                        cp(c, "act", Vslc[:, n, :, 0:64], pX[b][:, 0:256].rearrange("p (g d) -> p g d", d=64),
                           [rpX[b]], [rV])
                        cp(c, "act", Vwin[:, n, :, 0:64], pX[b][:, 256:512].rearrange("p (g d) -> p g d", d=64),
                           [rpX[b]], [rV])
                        for kc in range(KC):
                            mm(c, pP[:, 0:48], nb.hT[:, kc, t4 * 128:(t4 + 1) * 128], wtm[:, kc, 512:560], kc == 0,
                               kc == KC - 1, [nb.rhT, rwtm], [rpP])
                        act(c, gates[:, n, :], pP[:, 0:48], AF.Sigmoid, [rpP], [rG])
            c.P.barrier()
            with ExitStack() as es:
                w1 = c.sb([128, 32, 256], BF16, es, "w1")
                posc = c.sb([128, 32], BF16, es, "posc")
                w2 = c.sb([128, 2, 2, 64], BF16, es, "w2")
                rw = Res()
                dma(c, "sp", w1[:, :, :], W("w1").rearrange("p (l m) -> p l m", m=256), [r_w], [rw])
                dma(c, "sp", posc[:, :], W("posc"), [r_w], [rw])
                dma(c, "sp", w2[:, :, :, :], W("w2").rearrange("p (a b d) -> p a b d", a=2, b=2), [r_w], [rw])
                hid = c.sb([128, 2, 1024], BF16, es, "hid")
                rhid = Res()
                c.P.op("pool", lambda e: e.memset(hid[:, :, :], 0.0), [], [rhid])
                bias = c.sb([128, 4], F32, es, "cbias")
                rb = Res()
                pH = [c.ps(es) for _ in range(2)]
                rpH = [Res() for _ in range(2)]
                pb = c.ps(es)
                rpb = Res()
                i2 = 0
                for kv in range(2):
                    rows = slice(kv * 64, kv * 64 + 64)
                    for mc in range(2):
                        for l in range(32):
                            mm(c, pb[:, 0:1], w1[rows, l, mc * 128:(mc + 1) * 128], posc[rows, l:l + 1], l == 0, l == 31,
                               [rw], [rpb])
                        cp(c, "dve", bias[:, kv * 2 + mc:kv * 2 + mc + 1], pb[:, 0:1], [rpb], [rb])
                    for mc in range(2):
                        for gp in range(2):
                            b = i2 % 2
                            i2 += 1
                            for l in range(32):
                                rhs = Acmp[rows, 2 * gp:2 * gp + 2, l:l + 16 * (n_cmp - 1) + 1:16]
                                mm(c, pH[b][:, 0:2 * n_cmp].rearrange("p (g i) -> p g i", g=2),
                                   w1[rows, l, mc * 128:(mc + 1) * 128], rhs, l == 0, l == 31, [rw, rA], [rpH[b]])
                            act(c, hid[:, mc, gp * 512:gp * 512 + 512].rearrange("p (g i) -> p g i", g=2)[:, :, 0:n_cmp],
                                pH[b][:, 0:2 * n_cmp].rearrange("p (g i) -> p g i", g=2), AF.Silu, [rpH[b], rb],
                                [rhid], bias=bias[:, kv * 2 + mc:kv * 2 + mc + 1])
                    if kv == 0:
                        for gp in range(2):
                            b = i2 % 2
                            i2 += 1
                            for mc in range(2):
                                mm(c, pH[b][0:64, :], w2[:, 0, mc, :], hid[:, mc, gp * 512:(gp + 1) * 512], mc == 0,
                                   mc == 1, [rw, rhid], [rpH[b]])
                            cp(c, "dve", KcT[:, 2 * gp:2 * gp + 2, :],
                               pH[b][0:64, :].rearrange("p (g i) -> p g i", g=2), [rpH[b]], [rKc])
                        c.P.op("dve", lambda e: e.memset(KcT[:, :, n_cmp:256], 0.0), [rKc], [rKc])
                    else:
                        for g in range(NG):
                            for it in range(2):
                                b = i2 % 2
                                i2 += 1
                                for mc in range(2):
                                    mm(c, pH[b][:, 0:64], hid[:, mc, g * 256 + it * 128:g * 256 + (it + 1) * 128],
                                       w2[:, 1, mc, :], mc == 0, mc == 1, [rhid, rw], [rpH[b]])
                                cp(c, "dve", Vc[:, it, g, 0:64], pH[b][:, 0:64], [rpH[b]], [rKc])
            c.P.barrier()
        with ExitStack() as es:
            wo = c.sb([128, 8, D], BF16, es, "nwo")
            rwo = Res()
            dma(c, "sp", wo[:, :, :], W("wo").rearrange("p (f n) -> p f n", n=D), [r_w], [rwo])
            R_ = [c.sb([128, 4, 128], BF16, es, "R") for _ in range(2)]
            rR = [Res() for _ in range(2)]
            Ec = [c.sb([128, 512], BF16, es, "Ec") for _ in range(2)]
            rEc = [Res() for _ in range(2)]
            Es = [c.sb([128, 512], BF16, es, "Es") for _ in range(2)]
            rEs = [Res() for _ in range(2)]
            imp = c.sb([128, 64], F32, es, "imp")
            imp2 = c.sb([128, 64], F32, es, "imp2")
            m8 = c.sb([128, 16], F32, es, "m8")
            rimp = Res()
            bsel = c.sb([128, 128], BF16, es, "bsel")
            rbsel = Res()
            c.P.op("pool", lambda e: e.memset(bsel[:, :], 0.0), [], [rbsel])
            den = c.sb([128, 16], F32, es, "den")
            rden = Res()
            coef = c.sb([128, 12], F32, es, "coef")
            rcoef = Res()
            oacc = c.sb([128, 4, 64], F32, es, "oacc")
            roacc = Res()
            Otok = c.sb([128, D], BF16, es, "Otok")
            rOtok = Res()
            OT = c.sb([128, 8, 128], BF16, es, "OT")
            rOT = Res()
            xt = c.sb([128, D], F32, es, "nxt")
            rxt = Res()
            xo = [c.sb([128, 512], F32, es, "nxo") for _ in range(2)]
            rxo = [Res() for _ in range(2)]
            pS = [c.ps(es) for _ in range(2)]
            rpS = [Res() for _ in range(2)]
            pOc = c.ps(es)
            pI = c.ps(es)
            pOs = c.ps(es)
            pOw = c.ps(es)
            pT = c.ps(es)
            pY = c.ps(es)
            rpOc, rpI, rpOs, rpOw, rpT, rpY = Res(), Res(), Res(), Res(), Res(), Res()
            iS = 0
            iR = 0

            def heads3(ps):
                return ps[:, 0:260].rearrange("p (h e) -> p h e", e=65)

            addm_t = [c.sb([128, 64], F32, es, "addm") for _ in range(2)]
            raddm = [Res() for _ in range(2)]
            for n in range(NQ):
                qs = slice(n * 128, (n + 1) * 128)
                dma(c, "sp", xt[:, :], x_dram[n * 128:(n + 1) * 128, :], [rx_tiles[n]], [rxt])
                dma(c, "sp", addm_t[n % 2][:, :], addm_in[:, n * 64:(n + 1) * 64], [], [raddm[n % 2]])
                for g in range(NG):
                    rb_ = iR % 2
                    iR += 1
                    R = R_[rb_]
                    rRr = rR[rb_]
                    dma(c, "sp", R[0:64, :, :], qS[:, 4 * g:4 * g + 4, qs], [rqS], [rRr])
                    Rq = R[0:64, :, :]
                    ntile = 1 if (8 * n + 6) < 128 else 2
                    first = True
                    for it in range(ntile):
                        sb_ = iS % 2
                        iS += 1
                        mm(c, pS[sb_][:, :].rearrange("p (h q) -> p h q", h=4), KcT[0:64, g, it * 128:(it + 1) * 128],
                           Rq, True, True, [rKc, rRr], [rpS[sb_]])
                        act(c, Ec[it][:, :], pS[sb_][:, :], AF.Exp, [rpS[sb_]], [rEc[it]], scale=0.125)
                        mi = min(n, 16) if it == 0 else n - 16
                        if not (it == 0 and n >= 17):
                            e3 = Ec[it][:, :].rearrange("p (h q) -> p h q", h=4)
                            tt(c, "dve", e3, e3, bc_mid(maskc[:, mi * 128:(mi + 1) * 128], 4), ALU.mult,
                               [rEc[it], rconst], [rEc[it]])
                        for h in range(4):
                            mm(c, pOc[:, h * 65:(h + 1) * 65], Ec[it][:, h * 128:(h + 1) * 128], Vc[:, it, g, :],
                               first, it == ntile - 1 and h == 3, [rEc[it], rKc], [rpOc], skip_group_check=True)
                            mm(c, pI[:, h * 64:(h + 1) * 64], Ec[it][:, h * 128:(h + 1) * 128],
                               ovl[:, it * 64:(it + 1) * 64], first, it == ntile - 1 and h == 3, [rEc[it], rconst],
                               [rpI], skip_group_check=True)
                            first = False
                    ts(c, "dve", den[:, 0:4], heads3(pOc)[:, :, 64], 1e-30, None, ALU.max, None, [rpOc], [rden])
                    c.P.op("dve", lambda e: e.reciprocal(den[:, 4:8], den[:, 0:4]), [rden], [rden])
                    stt(c, imp[:, :], pI[:, 0:64], den[:, 4:5], addm_t[n % 2][:, :], ALU.mult, ALU.add,
                        [rpI, rden, raddm[n % 2]], [rimp])
                    for h in range(1, 4):
                        stt(c, imp[:, :], pI[:, h * 64:(h + 1) * 64], den[:, 4 + h:5 + h], imp[:, :], ALU.mult,
                            ALU.add, [rpI, rden, rimp], [rimp])
                    c.P.op("dve", lambda e: e.max(m8[:, 0:8], imp[:, :]), [rimp], [rimp])
                    c.P.op("dve", lambda e: e.match_replace(imp2[:, :], m8[:, 0:8], imp[:, :], -3.0e38), [rimp], [rimp])
                    c.P.op("dve", lambda e: e.max(m8[:, 8:16], imp2[:, :]), [rimp], [rimp])
                    ts(c, "dve", bsel[:, 64:128], imp[:, :], m8[:, 15:16], NEGB, ALU.is_lt, ALU.mult, [rimp], [rbsel])
                    tr(c, pT[:, :].bitcast(BF16)[:, 0:128], bsel[:, :], ident[:, :], [rbsel, rconst], [rpT])
                    cp(c, "act", R[64:128, :, :], bc_mid(pT[:, :].bitcast(BF16)[64:128, 0:128], 4), [rpT], [rRr])
                    Rf = R[:, :, :]
                    for kt in range(n + 1):
                        sb_ = iS % 2
                        iS += 1
                        mm(c, pS[sb_][:, :].rearrange("p (h q) -> p h q", h=4), Lslc[:, g, kt * 128:(kt + 1) * 128], Rf,
                           True, True, [rL, rRr], [rpS[sb_]])
                        act(c, Es[sb_][:, :], pS[sb_][:, :], AF.Exp, [rpS[sb_]], [rEs[sb_]], scale=0.125)
                        if kt == n:
                            e3 = Es[sb_][:, :].rearrange("p (h q) -> p h q", h=4)
                            tt(c, "dve", e3, e3, bc_mid(tri, 4), ALU.mult, [rEs[sb_], rconst], [rEs[sb_]])
                        for h in range(4):
                            mm(c, pOs[:, h * 65:(h + 1) * 65], Es[sb_][:, h * 128:(h + 1) * 128], Vslc[:, kt, g, :],
                               kt == 0 and h == 0, kt == n and h == 3, [rEs[sb_], rV], [rpOs], skip_group_check=True)
                    k0 = max(0, n - 4)
                    for kt in range(k0, n + 1):
                        sb_ = iS % 2
                        iS += 1
                        mm(c, pS[sb_][:, :].rearrange("p (h q) -> p h q", h=4), Lwin[0:64, g, kt * 128:(kt + 1) * 128],
                           Rq, True, True, [rL, rRr], [rpS[sb_]])
                        act(c, Es[sb_][:, :], pS[sb_][:, :], AF.Exp, [rpS[sb_]], [rEs[sb_]], scale=0.125)
                        e3 = Es[sb_][:, :].rearrange("p (h q) -> p h q", h=4)
                        if kt == n:
                            tt(c, "dve", e3, e3, bc_mid(tri, 4), ALU.mult, [rEs[sb_], rconst], [rEs[sb_]])
                        if kt == n - 4:
                            tt(c, "dve", e3, e3, bc_mid(upper, 4), ALU.mult, [rEs[sb_], rconst], [rEs[sb_]])
                        for h in range(4):
                            mm(c, pOw[:, h * 65:(h + 1) * 65], Es[sb_][:, h * 128:(h + 1) * 128], Vwin[:, kt, g, :],
                               kt == k0 and h == 0, kt == n and h == 3, [rEs[sb_], rV], [rpOw], skip_group_check=True)
                    for br, (po, rpo) in enumerate(((pOc, rpOc), (pOs, rpOs), (pOw, rpOw))):
                        ts(c, "dve", den[:, 8:12], heads3(po)[:, :, 64], 1e-30, None, ALU.max, None, [rpo], [rden])
                        c.P.op("dve", lambda e: e.reciprocal(den[:, 12:16], den[:, 8:12]), [rden], [rden])
                        gsl = gates[:, n, (4 * g) * 3 + br:(4 * g + 4) * 3:3]
                        tt(c, "dve", coef[:, br * 4:(br + 1) * 4], den[:, 12:16], gsl, ALU.mult, [rden, rG], [rcoef])
                    for h in range(4):
                        ts(c, "dve", oacc[:, h, :], heads3(pOc)[:, h, 0:64], coef[:, h:h + 1], None, ALU.mult, None,
                           [rpOc, rcoef], [roacc])
                        stt(c, oacc[:, h, :], heads3(pOs)[:, h, 0:64], coef[:, 4 + h:5 + h], oacc[:, h, :], ALU.mult,
                            ALU.add, [rpOs, rcoef, roacc], [roacc])
                        stt(c, Otok[:, (4 * g + h) * 64:(4 * g + h + 1) * 64], heads3(pOw)[:, h, 0:64],
                            coef[:, 8 + h:9 + h], oacc[:, h, :], ALU.mult, ALU.add, [rpOw, rcoef, roacc], [rOtok])
                tpb = pT[:, :].bitcast(BF16)
                for f in range(8):
                    tr(c, tpb[:, f * 128:(f + 1) * 128], Otok[:, f * 128:(f + 1) * 128], ident[:, :], [rOtok, rconst],
                       [rpT])
                cp(c, "act", OT[:, :, :], tpb[:, :].rearrange("p (f t) -> p f t", f=8), [rpT], [rOT])
                for nh in range(2):
                    for f in range(8):
                        mm(c, pY[:, :], OT[:, f, :], wo[:, f, nh * 512:(nh + 1) * 512], f == 0, f == 7, [rOT, rwo],
                           [rpY])
                    tt(c, "dve", xo[nh][:, :], pY[:, :], xt[:, nh * 512:(nh + 1) * 512], ALU.add, [rpY, rxt],
                       [rxo[nh]])
                    dma(c, "pool", x_dram[n * 128:(n + 1) * 128, nh * 512:(nh + 1) * 512], xo[nh][:, :], [rxo[nh]],
                        [rx_tiles[n]])


def lay_gain(g):
    return np.ascontiguousarray(g.reshape(KC, 128).T)


def lay_wgu(w):
    a = w.reshape(KC, 128, 2, FC, 128)
    return np.ascontiguousarray(a.transpose(1, 3, 0, 2, 4).reshape(128, -1))


def lay_wd(w):
    return np.ascontiguousarray(w.reshape(FC, 128, D).transpose(1, 0, 2).reshape(128, -1))


def build(T, plan):
    nc = bass.Bass("TRN2", target_bir_lowering=False)
    nlay = 4
    x_in = nc.dram_tensor("x", [T, D], F32, kind="ExternalInput")
    gains_in = nc.dram_tensor("gains", [128, 8 * KC], F32, kind="ExternalInput")
    gfin_in = nc.dram_tensor("gfin", [D], F32, kind="ExternalInput")
    wgu_in = nc.dram_tensor("wgu", [nlay, 128, FC * 2048], F32, kind="ExternalInput")
    wd_in = nc.dram_tensor("wd", [nlay, 128, FC * D], F32, kind="ExternalInput")
    ident_in = nc.dram_tensor("ident", [128, 128], F32, kind="ExternalInput")
    pos_in = nc.dram_tensor("pos", [T], I32, kind="ExternalInput")
    rwqk_in = nc.dram_tensor("rwqk", [2, 128, 16 * 1024], F32, kind="ExternalInput")
    rwvg_in = nc.dram_tensor("rwvg", [2, 128, 8 * 4096], F32, kind="ExternalInput")
    rwo_in = nc.dram_tensor("rwo", [2, 128, 16 * D], F32, kind="ExternalInput")
    rc_in = nc.dram_tensor("rc", [128, 521], F32, kind="ExternalInput")
    nsaw_in = nc.dram_tensor("nsaw", [2, 128, NSA_WTOT], F32, kind="ExternalInput")
    ncs_in = nc.dram_tensor("ncs", [128, NSMALL], F32, kind="ExternalInput")
    eall_in = nc.dram_tensor("eall", [128, T], F32, kind="ExternalInput")
    addm_in = nc.dram_tensor("addm", [128, (T // 128) * 64], F32, kind="ExternalInput")
    out = nc.dram_tensor("out", [T, D], F32, kind="ExternalOutput")

    with ExitStack() as es:
        c = Ctx(nc, es)
        xs = c.dram([T, D], F32, "xres")
        rx_tiles = [Res() for _ in range(T // 128)]
        consts = {}
        rconst = Res()
        consts["r"] = rconst
        identf = c.sb([128, 128], F32, None, "identf")
        ident = c.sb([128, 128], BF16, None, "ident")
        gains = c.sb([128, 8 * KC], F32, None, "gains")
        eps = c.sb([128, 1], F32, None, "eps")
        consts["ident"] = ident
        consts["eps"] = eps
        rtmp = Res()
        dma(c, "sp", identf[:, :], ident_in[:, :], [], [rtmp])
        cp(c, "dve", ident[:, :], identf[:, :], [rtmp], [rconst])
        dma(c, "sp", gains[:, :], gains_in[:, :], [], [rconst])
        c.P.op("dve", lambda e: e.memset(eps[:, :], EPS), [], [rconst])
        rc_sb = c.sb([128, 521], F32, None, "rc_sb")
        dma(c, "sp", rc_sb[:, :], rc_in[:, :], [], [rconst])
        _, cdec = ret_consts_host()
        qS = c.dram([64, 16, T], BF16, "qS")
        for i in range(T // 128):
            dma(c, "sp", xs[i * 128:(i + 1) * 128, :], x_in[i * 128:(i + 1) * 128, :], [], [rx_tiles[i]])
        used_ffn = sorted(set(p[1] for p in plan if p[0] == "ffn"))
        wgu_s = {}
        wd_s = {}
        r_w = {}
        items = []
        for l in used_ffn:
            wgu_s[l] = c.dram([128, FC * 2048], BF16, "wgus")
            wd_s[l] = c.dram([128, FC * D], BF16, "wds")
            r_w[("gu", l)] = Res()
            r_w[("d", l)] = Res()
            items.append((lambda lo, hi, l=l: wgu_in[l, :, lo:hi], lambda lo, hi, l=l: wgu_s[l][:, lo:hi],
                          FC * 2048, r_w[("gu", l)]))
            items.append((lambda lo, hi, l=l: wd_in[l, :, lo:hi], lambda lo, hi, l=l: wd_s[l][:, lo:hi],
                          FC * D, r_w[("d", l)]))
        used_ret = sorted(set(p[1] for p in plan if p[0] == "ret"))
        rw_s = {}
        for j in used_ret:
            rw_s[j] = (c.dram([128, 16 * 1024], BF16, "rwqks"), c.dram([128, 8 * 4096], BF16, "rwvgs"),
                       c.dram([128, 16 * D], BF16, "rwos"))
            r_w[("ret", j)] = Res()
            for (src_t, dst_t, N) in ((rwqk_in, rw_s[j][0], 16 * 1024), (rwvg_in, rw_s[j][1], 8 * 4096),
                                      (rwo_in, rw_s[j][2], 16 * D)):
                items.append((lambda lo, hi, j=j, src_t=src_t: src_t[j, :, lo:hi],
                              lambda lo, hi, dst_t=dst_t: dst_t[:, lo:hi], N, r_w[("ret", j)]))
        used_nsa = sorted(set(p[1] for p in plan if p[0] == "nsa"))
        nw_s = {}
        for j in used_nsa:
            nw_s[j] = c.dram([128, NSA_WTOT], BF16, "nsaws")
            r_w[("nsa", j)] = Res()
            items.append((lambda lo, hi, j=j: nsaw_in[j, :, lo:hi], lambda lo, hi, j=j: nw_s[j][:, lo:hi], NSA_WTOT,
                          r_w[("nsa", j)]))
        convert_weights(c, items)
        for p in plan:
            c.P.barrier()
            if p[0] == "nsa":
                j, li = p[1], p[2]
                nsa_layer(c, T, xs, rx_tiles, pos_in.ap(), nw_s[j], r_w[("nsa", j)], gains[:, li * KC:(li + 1) * KC],
                          consts, ncs_in, eall_in, addm_in, qS)
                continue
            if p[0] == "ret":
                j, li = p[1], p[2]
                ret_pass(c, T, xs, rx_tiles, pos_in.ap(), rw_s[j][0], rw_s[j][1], rw_s[j][2], r_w[("ret", j)],
                         gains[:, li * KC:(li + 1) * KC], consts, rc_sb, cdec)
                continue
            if p[0] == "ffn":
                l = p[1]
                ffn_pass(c, T, xs, rx_tiles, wgu_s[l], r_w[("gu", l)], wd_s[l], r_w[("d", l)],
                         gains[:, (4 + l) * KC:(5 + l) * KC], consts)
            elif p[0] == "final":
                final_pass(c, T, xs, rx_tiles, out, gfin_in, rconst, consts)
        c.P.emit()
    return nc


def prep_common(inputs):
    g = np.concatenate([lay_gain(inputs["norm_mix"][i]) for i in range(4)] +
                       [lay_gain(inputs["norm_ffn"][i]) for i in range(4)], axis=1)
    m = {
        "gains": np.ascontiguousarray(g, dtype=np.float32),
        "gfin": np.ascontiguousarray(inputs["norm_final"], dtype=np.float32),
        "wgu": np.stack([lay_wgu(inputs["ffn_w_gu"][i]) for i in range(4)]),
        "wd": np.stack([lay_wd(inputs["ffn_w_down"][i]) for i in range(4)]),
        "ident": np.eye(128, dtype=np.float32),
    }
    lr = [lay_ret(inputs["ret_w_in"][j], inputs["ret_w_out"][j]) for j in range(2)]
    m["rwqk"] = np.stack([l[0] for l in lr])
    m["rwvg"] = np.stack([l[1] for l in lr])
    m["rwo"] = np.stack([l[2] for l in lr])
    m["rc"] = ret_consts_host()[0]
    m["nsaw"] = np.stack([lay_nsa(inputs["nsa_w_in"][j], inputs["nsa_cmp_pos"][j], inputs["nsa_cmp_w1"][j],
                                  inputs["nsa_cmp_w2"][j], inputs["nsa_w_out"][j]) for j in range(2)])
    return m


def run(inputs, T, plan, ncores, trace=False):
    nc = build(T, plan)
    common = prep_common(inputs)
    common["ncs"], common["eall"], common["addm"] = nsa_consts_host(T)
    in_maps = []
    for b in range(ncores):
        m = dict(common)
        m["x"] = np.ascontiguousarray(inputs["x"][b, :T], dtype=np.float32)
        m["pos"] = np.ascontiguousarray(inputs["positions"][b, :T], dtype=np.int32)
        in_maps.append(m)
    res = run_bass_kernel_spmd(nc, in_maps, core_ids=list(range(ncores)), trace=trace)
    return np.stack([r["out"] for r in res.results], axis=0), res


def kernel(**inputs):
    inputs = {k: np.asarray(v) for k, v in inputs.items()}
    plan = [("ret", 0, 0), ("ffn", 0), ("nsa", 0, 1), ("ffn", 1), ("ret", 1, 2), ("ffn", 2), ("nsa", 1, 3),
            ("ffn", 3), ("final",)]
    out, _ = run(inputs, 4096, plan, 8)
    return out.astype(np.float32)
```

```python
import numpy as np
import concourse.bass as bass
import concourse.mybir as mybir
from concourse.bass_utils import run_bass_kernel_spmd
from contextlib import ExitStack

F32 = mybir.dt.float32
BF16 = mybir.dt.bfloat16
I32 = mybir.dt.int32
AF = mybir.ActivationFunctionType
ALU = mybir.AluOpType

D = 1024
KC = 8
FH = 2816
FC = 22
EPS = 1e-6
TB = 512
NDMA_SEM = 8

ENGS = ("pe", "act", "dve", "pool", "sp")


class Res:
    __slots__ = ("w", "r")

    def __init__(self):
        self.w = None
        self.r = {}


class Prog:
    def __init__(self, nc):
        self.nc = nc
        self.eng = {"pe": nc.tensor, "act": nc.scalar, "dve": nc.vector, "pool": nc.gpsimd, "sp": nc.sync}
        self.q = {e: [] for e in ENGS}
        self.cnt = {e: 0 for e in ENGS}
        self.dcnt = {e: 0 for e in ENGS}
        self.waited = {e: {} for e in ENGS}
        self.final = []
        self.pend = {e: {} for e in ENGS}
        self.ep = 0
        self.maxval = {}

    def barrier(self):
        cur = {}
        for e in ENGS:
            if self.cnt[e]:
                cur[(e, "c", self.ep)] = self.cnt[e]
            for j in range(min(self.dcnt[e], NDMA_SEM)):
                cur[(e, "d", j)] = 16 * ((self.dcnt[e] - 1 - j) // NDMA_SEM + 1)
        for e in ENGS:
            for k2, v in cur.items():
                if self.pend[e].get(k2, 0) < v:
                    self.pend[e][k2] = v
        self.ep += 1
        self.cnt = {e: 0 for e in ENGS}

    def op(self, eng, fn, reads=(), writes=(), dma=False):
        deps = []
        for r in reads:
            if r.w is not None:
                deps.append(r.w)
        for w in writes:
            if w.w is not None:
                deps.append(w.w)
            deps.extend(w.r.items())
        if dma:
            k = self.dcnt[eng]
            self.dcnt[eng] += 1
            sk = (eng, "d", k % NDMA_SEM)
            val = 16 * (k // NDMA_SEM + 1)
            if k >= NDMA_SEM:
                deps.append((sk, val - 16))
        else:
            self.cnt[eng] += 1
            sk = (eng, "c", self.ep)
            val = self.cnt[eng]
        waits = {}
        wd = self.waited[eng]
        if self.pend[eng]:
            deps.extend(self.pend[eng].items())
            self.pend[eng] = {}
        for (k2, v) in deps:
            if eng == "pe" and k2[0] == "pe" and k2[1] == "c":
                continue
            if wd.get(k2, 0) >= v:
                continue
            if waits.get(k2, 0) < v:
                waits[k2] = v
        for k2, v in waits.items():
            wd[k2] = v
        self.q[eng].append((tuple(waits.items()), fn, sk, 16 if dma else 1))
        ev = (sk, val)
        if self.maxval.get(sk, 0) < val:
            self.maxval[sk] = val
        for r in reads:
            if r.r.get(sk, 0) < val:
                r.r[sk] = val
        for w in writes:
            w.w = ev
            w.r = {}
        return ev

    def emit(self):
        nc = self.nc
        keys = set()
        for e in ENGS:
            for (waits, fn, sk, inc) in self.q[e]:
                keys.add(sk)
        with ExitStack() as es:
            sems = {}
            for sk in sorted(keys):
                sems[sk] = es.enter_context(nc.semaphore("s_%s_%s_%d" % sk))
            block = es.enter_context(nc.Block())
            fin = [(sems[k2], v) for k2, v in sorted(self.maxval.items())]

            def mk(e):
                def body(engine):
                    for (waits, fn, sk, inc) in self.q[e]:
                        for (k2, v) in waits:
                            engine.wait_ge(sems[k2], v)
                        fn(engine).then_inc(sems[sk], inc)
                    if e == "sp":
                        for (s, v) in fin:
                            engine.wait_ge(s, v)
                return body

            block.tensor(mk("pe"))
            block.scalar(mk("act"))
            block.vector(mk("dve"))
            block.gpsimd(mk("pool"))
            block.sync(mk("sp"))


def ap_of(t, offset, pat):
    return bass.AP(tensor=t, offset=offset, ap=[list(p) for p in pat])


class Ctx:
    def __init__(self, nc, es):
        self.nc = nc
        self.es = es
        self.P = Prog(nc)
        self.n = 0

    def sb(self, shape, dt, es=None, name=None):
        self.n += 1
        t = (es or self.es).enter_context(self.nc.sbuf_tensor("%s%d" % (name or "sb", self.n), list(shape), dt))
        return t

    def ps(self, es=None):
        self.n += 1
        t = (es or self.es).enter_context(self.nc.psum_tensor("ps%d" % self.n, [128, 512], F32))
        return t

    def dram(self, shape, dt, name=None):
        self.n += 1
        return self.nc.dram_tensor("%s%d" % (name or "scr", self.n), list(shape), dt, kind="Internal")


def dma(c, eng, out, in_, reads, writes):
    return c.P.op(eng, lambda e: e.dma_start(out=out, in_=in_), reads, writes, dma=True)


def mm(c, out, lhsT, rhs, start, stop, reads, writes, **kw):
    return c.P.op("pe", lambda e: e.matmul(out, lhsT, rhs, start=start, stop=stop, **kw), reads, writes)


def tr(c, out, in_, ident, reads, writes):
    return c.P.op("pe", lambda e: e.transpose(out, in_, ident), reads, writes)


def act(c, out, in_, func, reads, writes, **kw):
    return c.P.op("act", lambda e: e.activation(out, in_, func, **kw), reads, writes)


def tt(c, eng, out, in0, in1, op, reads, writes):
    return c.P.op(eng, lambda e: e.tensor_tensor(out, in0, in1, op), reads, writes)


def ts(c, eng, out, in0, s1, s2, op0, op1, reads, writes):
    if op1 is None:
        return c.P.op(eng, lambda e: e.tensor_scalar(out, in0, s1, None, op0), reads, writes)
    return c.P.op(eng, lambda e: e.tensor_scalar(out, in0, s1, s2, op0, op1), reads, writes)


def stt(c, out, in0, scalar, in1, op0, op1, reads, writes):
    return c.P.op("dve", lambda e: e.scalar_tensor_tensor(out, in0, scalar, in1, op0, op1), reads, writes)


def cp(c, eng, out, in_, reads, writes):
    if eng == "act":
        return c.P.op("act", lambda e: e.copy(out, in_), reads, writes)
    return c.P.op(eng, lambda e: e.tensor_copy(out, in_), reads, writes)


def convert_weights(c, items):
    CH = 4096
    with ExitStack() as es:
        NB = 3
        stg = [c.sb([128, CH], F32, es, "cvf") for _ in range(NB)]
        out = [c.sb([128, CH], BF16, es, "cvb") for _ in range(NB)]
        rs = [Res() for _ in range(NB)]
        ro = [Res() for _ in range(NB)]
        i = 0
        engs = ["pool", "dve", "act"]
        for (src, dst, N, dres) in items:
            lo = 0
            while lo < N:
                n = min(CH, N - lo)
                b = i % NB
                dma(c, "sp", stg[b][:, :n], src(lo, lo + n), [], [rs[b]])
                cp(c, engs[i % 3], out[b][:, :n], stg[b][:, :n], [rs[b]], [ro[b]])
                dma(c, "sp", dst(lo, lo + n), out[b][:, :n], [ro[b]], [dres])
                lo += n
                i += 1


class NormBufs:
    def __init__(self, c, es, nx=4):
        self.nx = nx
        self.xt = [c.sb([128, D], F32, es, "xt") for _ in range(nx)]
        self.rxt = [Res() for _ in range(nx)]
        self.xs = c.sb([128, D], BF16, es, "xs")
        self.rxs = Res()
        self.junk = c.sb([128, D], BF16, es, "junk")
        self.rjunk = Res()
        self.ss = c.sb([128, 4], F32, es, "ss")
        self.rs = c.sb([128, 4], F32, es, "rs")
        self.rstd = c.sb([128, 4], F32, es, "rstd")
        self.rss = [Res() for _ in range(4)]
        self.hT = c.sb([128, KC, TB], BF16, es, "hT")
        self.rhT = Res()
        self.tp = c.ps(es)
        self.rtp = Res()


def norm_block(c, nb, x_dram, rx_tiles, tok0, gain_ap, consts):
    ident, rconst = consts["ident"], consts["r"]
    for t4 in range(4):
        r0 = tok0 + t4 * 128
        dma(c, "sp", nb.xt[t4 % nb.nx][:, :], x_dram[r0:r0 + 128, :], [rx_tiles[r0 // 128]], [nb.rxt[t4 % nb.nx]])
        act(c, nb.junk[:, :], nb.xt[t4 % nb.nx][:, :], AF.Square, [nb.rxt[t4 % nb.nx]], [nb.rjunk, nb.rss[t4]],
            accum_out=nb.ss[:, t4:t4 + 1])
        act(c, nb.rs[:, t4:t4 + 1], nb.ss[:, t4:t4 + 1], AF.Sqrt, [nb.rss[t4]], [nb.rss[t4]],
            scale=1.0 / D, bias=consts["eps"][:, 0:1])
        c.P.op("dve", lambda e, t4=t4: e.reciprocal(nb.rstd[:, t4:t4 + 1], nb.rs[:, t4:t4 + 1]),
               [nb.rss[t4]], [nb.rss[t4]])
        ts(c, "dve", nb.xs[:, :], nb.xt[t4 % nb.nx][:, :], nb.rstd[:, t4:t4 + 1], None, ALU.mult, None,
           [nb.rxt[t4 % nb.nx], nb.rss[t4]], [nb.rxs])
        tpb = nb.tp[:, :].bitcast(BF16)
        for kc in range(KC):
            tr(c, tpb[:, kc * 128:(kc + 1) * 128], nb.xs[:, kc * 128:(kc + 1) * 128], ident[:, :],
               [nb.rxs, rconst], [nb.rtp])
        for kc in range(KC):
            ts(c, "dve", nb.hT[:, kc, t4 * 128:(t4 + 1) * 128],
               tpb[:, kc * 128:(kc + 1) * 128], gain_ap[:, kc:kc + 1], None, ALU.mult, None,
               [nb.rtp, rconst], [nb.rhT])


def ffn_pass(c, T, x_dram, rx_tiles, wgu_s, r_wgu, wd_s, r_wd, gain_ap, consts):
    with ExitStack() as es:
        nb = NormBufs(c, es)
        wd = c.sb([128, FC, D], BF16, es, "wd")
        rwd = Res()
        NWB = 3
        wg = [c.sb([128, KC, 2, 128], BF16, es, "wg") for _ in range(NWB)]
        rwg = [Res() for _ in range(NWB)]
        actT = c.sb([128, FC, TB], BF16, es, "actT")
        ractT = Res()
        sa = [c.sb([128, TB], F32, es, "sa") for _ in range(2)]
        rsa = [Res() for _ in range(2)]
        pa = [c.ps(es) for _ in range(2)]
        pb = [c.ps(es) for _ in range(2)]
        rpa = [Res() for _ in range(2)]
        rpb = [Res() for _ in range(2)]
        py = [c.ps(es) for _ in range(2)]
        rpy = [Res() for _ in range(2)]
        xo = [c.sb([128, TB], F32, es, "xo") for _ in range(2)]
        rxo = [Res() for _ in range(2)]
        half = FC // 2
        dma(c, "sp", wd[:, 0:half, :], wd_s[:, 0:half * D].rearrange("p (c n) -> p c n", n=D), [r_wd], [rwd])
        dma(c, "sp", wd[:, half:FC, :], wd_s[:, half * D:FC * D].rearrange("p (c n) -> p c n", n=D), [r_wd], [rwd])
        it = 0
        for blk in range(T // TB):
            tok0 = blk * TB
            norm_block(c, nb, x_dram, rx_tiles, tok0, gain_ap, consts)
            for cc in range(FC):
                b = it % NWB
                pp = it % 2
                it += 1
                dma(c, "sp", wg[b][:, :, :, :],
                    wgu_s[:, cc * 2048:(cc + 1) * 2048].rearrange("p (k a j) -> p k a j", k=KC, a=2),
                    [r_wgu], [rwg[b]])
                for kc in range(KC):
                    mm(c, pa[pp][:, :], wg[b][:, kc, 0, :], nb.hT[:, kc, :], kc == 0, kc == KC - 1,
                       [rwg[b], nb.rhT], [rpa[pp]])
                for kc in range(KC):
                    mm(c, pb[pp][:, :], wg[b][:, kc, 1, :], nb.hT[:, kc, :], kc == 0, kc == KC - 1,
                       [rwg[b], nb.rhT], [rpb[pp]])
                act(c, sa[pp][:, :], pa[pp][:, :], AF.Silu, [rpa[pp]], [rsa[pp]])
                tt(c, "dve", actT[:, cc, :], sa[pp][:, :], pb[pp][:, :], ALU.mult, [rsa[pp], rpb[pp]], [ractT])
            j = 0
            for t4 in range(4):
                for nh in range(2):
                    pp = j % 2
                    j += 1
                    for cc in range(FC):
                        mm(c, py[pp][:, :], actT[:, cc, t4 * 128:(t4 + 1) * 128], wd[:, cc, nh * 512:(nh + 1) * 512],
                           cc == 0, cc == FC - 1, [ractT, rwd], [rpy[pp]])
                    tt(c, "dve", xo[pp][:, :], py[pp][:, :], nb.xt[t4][:, nh * 512:(nh + 1) * 512], ALU.add,
                       [rpy[pp], nb.rxt[t4]], [rxo[pp]])
                    r0 = tok0 + t4 * 128
                    dma(c, "pool", x_dram[r0:r0 + 128, nh * 512:(nh + 1) * 512], xo[pp][:, :], [rxo[pp]],
                        [rx_tiles[r0 // 128]])


def final_pass(c, T, x_dram, rx_tiles, out_dram, gfin_in, rconst, consts):
    with ExitStack() as es:
        gain_rep = c.sb([128, D], F32, es, "gfin")
        rconst = Res()
        dma(c, "sp", gain_rep[:, :], gfin_in.ap().partition_broadcast(128), [], [rconst])
        NBF = 2
        xt = [c.sb([128, D], F32, es, "fx") for _ in range(NBF)]
        rxt = [Res() for _ in range(NBF)]
        yo = [c.sb([128, D], F32, es, "fy") for _ in range(NBF)]
        ryo = [Res() for _ in range(NBF)]
        junk = c.sb([128, D], BF16, es, "fj")
        rj = Res()
        st = [c.sb([128, 4], F32, es, "fs") for _ in range(NBF)]
        rst = [Res() for _ in range(NBF)]
        rout = Res()
        for i in range(T // 128):
            b = i % NBF
            dma(c, "sp", xt[b][:, :], x_dram[i * 128:(i + 1) * 128, :], [rx_tiles[i]], [rxt[b]])
            act(c, junk[:, :], xt[b][:, :], AF.Square, [rxt[b]], [rj, rst[b]], accum_out=st[b][:, 0:1])
            act(c, st[b][:, 1:2], st[b][:, 0:1], AF.Sqrt, [rst[b]], [rst[b]], scale=1.0 / D,
                bias=consts["eps"][:, 0:1])
            c.P.op("dve", lambda e, b=b: e.reciprocal(st[b][:, 2:3], st[b][:, 1:2]), [rst[b]], [rst[b]])
            stt(c, yo[b][:, :], xt[b][:, :], st[b][:, 2:3], gain_rep[:, :], ALU.mult, ALU.mult,
                [rxt[b], rst[b], rconst], [ryo[b]])
            dma(c, "pool", out_dram[i * 128:(i + 1) * 128, :], yo[b][:, :], [ryo[b]], [rout])


TWO_PI = 6.283185307179586
C1 = 6.28125
C2 = TWO_PI - C1
PI_SAFE = 3.1415925


class TrigBufs:
    def __init__(self, c, es, n):
        self.u = c.sb([128, n], F32, es, "tg_u")
        self.ki = c.sb([128, n], I32, es, "tg_k")
        self.kf = c.sb([128, n], F32, es, "tg_kf")
        self.r = Res()


def trig(c, tb, ang, rang, out, rout, phase, scale=None):
    n = ang.shape[-1]
    ts(c, "dve", tb.u[:, :n], ang, 1.0 / TWO_PI, 0.5 + phase / TWO_PI, ALU.mult, ALU.add, [rang], [tb.r])
    cp(c, "dve", tb.ki[:, :n], tb.u[:, :n], [tb.r], [tb.r])
    cp(c, "dve", tb.kf[:, :n], tb.ki[:, :n], [tb.r], [tb.r])
    stt(c, tb.u[:, :n], tb.kf[:, :n], -C1, ang, ALU.mult, ALU.add, [tb.r, rang], [tb.r])
    stt(c, tb.u[:, :n], tb.kf[:, :n], -C2, tb.u[:, :n], ALU.mult, ALU.add, [tb.r], [tb.r])
    if phase != 0.0:
        ts(c, "dve", tb.u[:, :n], tb.u[:, :n], float(phase), None, ALU.add, None, [tb.r], [tb.r])
    ts(c, "dve", tb.kf[:, :n], tb.u[:, :n], -PI_SAFE, TWO_PI, ALU.is_lt, ALU.mult, [tb.r], [tb.r])
    tt(c, "dve", tb.u[:, :n], tb.u[:, :n], tb.kf[:, :n], ALU.add, [tb.r], [tb.r])
    ts(c, "dve", tb.u[:, :n], tb.u[:, :n], PI_SAFE, -PI_SAFE, ALU.min, ALU.max, [tb.r], [tb.r])
    if scale is None:
        act(c, out, tb.u[:, :n], AF.Sin, [tb.r], [rout])
    else:
        act(c, out, tb.u[:, :n], AF.Sin, [tb.r], [rout], scale=scale)


RH = 4
RC = 128


def ret_consts_host():
    h = np.arange(RH, dtype=np.float64)
    log_g = np.log(1.0 - 2.0 ** (-5.0 - h))
    idx = np.arange(RC, dtype=np.float64)
    diff = idx[None, :] - idx[:, None]
    m = np.where(diff[:, None, :] >= 0, np.exp(log_g[None, :, None] * np.maximum(diff[:, None, :], 0.0)), 0.0)
    qdec = np.exp(log_g[None, :] * (idx[:, None] + 1.0))
    kdec = np.exp(log_g[None, :] * (RC - 1.0 - idx[:, None]))
    cdec = np.exp(log_g * RC)
    half = 128
    invf = (np.float32(10000.0) ** (np.float32(-2.0) * np.arange(half, dtype=np.float32) / np.float32(256.0))).astype(np.float32)
    arr = np.concatenate([m.reshape(128, RH * 128), qdec, kdec, invf[:, None]], axis=1).astype(np.float32)
    return arr, [float(x) for x in cdec]


def ret_pass(c, T, x_dram, rx_tiles, pos_dram, wqk_s, wvg_s, wo_s, r_w, gain_ap, consts, rc_sb, cdec):
    rconst = consts["r"]
    ident = consts["ident"]
    maskT = rc_sb[:, 0:512]
    qdec = rc_sb[:, 512:516]
    kdec = rc_sb[:, 516:520]
    invf = rc_sb[:, 520:521]
    with ExitStack() as es:
        nb = NormBufs(c, es)
        wo = c.sb([128, 16, D], BF16, es, "wo")
        rwo = Res()
        st32 = c.sb([128, 2, RH, 512], F32, es, "st32")
        stbf = c.sb([128, 2, RH, 512], BF16, es, "stbf")
        rst32 = [[Res() for _ in range(RH)] for _ in range(2)]
        rstbf = [[Res() for _ in range(RH)] for _ in range(2)]
        qT = c.sb([128, KC, TB], BF16, es, "qT")
        kT = c.sb([128, KC, TB], BF16, es, "kT")
        rqT = [Res() for _ in range(KC)]
        rkT = [Res() for _ in range(KC)]
        ktok = [c.sb([128, D], BF16, es, "ktok") for _ in range(4)]
        rktok = [Res() for _ in range(4)]
        v = [c.sb([128, 2048], BF16, es, "v") for _ in range(4)]
        rv = [Res() for _ in range(4)]
        sg = [c.sb([128, 2048], BF16, es, "sg") for _ in range(4)]
        rsg = [Res() for _ in range(4)]
        og = c.sb([128, 2048], BF16, es, "og")
        rog = Res()
        ogT = c.sb([128, 16, 128], BF16, es, "ogT")
        rogT = Res()
        wqk = [c.sb([128, KC, 128], BF16, es, "wqk") for _ in range(2)]
        rwqk = [Res() for _ in range(2)]
        wvg = [c.sb([128, KC, 512], BF16, es, "wvg") for _ in range(2)]
        rwvg = [Res() for _ in range(2)]
        posi = c.sb([128, TB], I32, es, "posi")
        ang = c.sb([128, TB], F32, es, "ang")
        rang = Res()
        cosT = c.sb([128, TB], F32, es, "cosT")
        sinT = c.sb([128, TB], F32, es, "sinT")
        rcos = Res()
        rsin = Res()
        tb = TrigBufs(c, es, TB)
        t1 = [c.sb([128, TB], F32, es, "t1") for _ in range(2)]
        t2 = [c.sb([128, TB], F32, es, "t2") for _ in range(2)]
        rt1 = [Res() for _ in range(2)]
        rt2 = [Res() for _ in range(2)]
        sm = c.sb([128, 128], BF16, es, "sm")
        rsm = Res()
        ocs = c.sb([128, 512], F32, es, "ocs")
        rocs = Res()
        ofl = c.sb([128, 512], F32, es, "ofl")
        rofl = Res()
        oss = c.sb([128, 4], F32, es, "oss")
        ross = Res()
        xo = [c.sb([128, 512], F32, es, "rxo") for _ in range(2)]
        rxo = [Res() for _ in range(2)]
        ojunk = nb.junk
        rojunk = nb.rjunk
        pA = c.ps(es)
        pB = c.ps(es)
        rpA = Res()
        rpB = Res()
        pS = c.ps(es)
        rpS = Res()
        pO = c.ps(es)
        rpO = Res()
        pC = c.ps(es)
        rpC = Res()
        pU = [c.ps(es) for _ in range(2)]
        rpU = [Res() for _ in range(2)]
        tp = nb.tp
        rtp = nb.rtp

        c.P.op("pool", lambda e: e.memset(st32[:, :, :, :], 0.0), [], [r for rr in rst32 for r in rr])
        c.P.op("pool", lambda e: e.memset(stbf[:, :, :, :], 0.0), [], [r for rr in rstbf for r in rr])
        dma(c, "sp", wo[:, 0:8, :], wo_s[:, 0:8 * D].rearrange("p (c n) -> p c n", n=D), [r_w], [rwo])
        dma(c, "sp", wo[:, 8:16, :], wo_s[:, 8 * D:16 * D].rearrange("p (c n) -> p c n", n=D), [r_w], [rwo])
        iq = 0
        ivg = 0
        for blk in range(T // TB):
            tok0 = blk * TB
            norm_block(c, nb, x_dram, rx_tiles, tok0, gain_ap, consts)
            dma(c, "sp", posi[:, :], pos_dram[tok0:tok0 + TB].partition_broadcast(128), [], [rang])
            cp(c, "dve", ang[:, :], posi[:, :], [rang], [rang])
            ts(c, "dve", ang[:, :], ang[:, :], invf, None, ALU.mult, None, [rang, rconst], [rang])
            trig(c, tb, ang[:, :], rang, sinT[:, :], rsin, 0.0)
            trig(c, tb, ang[:, :], rang, cosT[:, :], rcos, np.pi / 2)
            for qk in range(2):
                dstT = qT if qk == 0 else kT
                rdst = rqT if qk == 0 else rkT
                for h in range(RH):
                    PA, rPA, PB, rPB = (pA, rpA, pB, rpB) if h % 2 == 0 else (pS, rpS, pO, rpO)
                    for a, (pp, rpp) in enumerate(((PA, rPA), (PB, rPB))):
                        m = qk * 8 + 2 * h + a
                        b = iq % 2
                        iq += 1
                        dma(c, "sp", wqk[b][:, :, :],
                            wqk_s[:, m * 1024:(m + 1) * 1024].rearrange("p (k j) -> p k j", k=KC), [r_w], [rwqk[b]])
                        for kc in range(KC):
                            mm(c, pp[:, :], wqk[b][:, kc, :], nb.hT[:, kc, :], kc == 0, kc == KC - 1,
                               [rwqk[b], nb.rhT], [rpp])
                    sc = 1.0 if qk == 0 else 1.0 / 16.0
                    stt(c, t1[0][:, :], PA[:, :], sc, cosT[:, :], ALU.mult, ALU.mult, [rPA, rcos], [rt1[0]])
                    stt(c, t2[0][:, :], PB[:, :], sc, sinT[:, :], ALU.mult, ALU.mult, [rPB, rsin], [rt2[0]])
                    tt(c, "pool", dstT[:, 2 * h, :], t1[0][:, :], t2[0][:, :], ALU.subtract, [rt1[0], rt2[0]],
                       [rdst[2 * h]])
                    stt(c, t1[1][:, :], PB[:, :], sc, cosT[:, :], ALU.mult, ALU.mult, [rPB, rcos], [rt1[1]])
                    stt(c, t2[1][:, :], PA[:, :], sc, sinT[:, :], ALU.mult, ALU.mult, [rPA, rsin], [rt2[1]])
                    tt(c, "pool", dstT[:, 2 * h + 1, :], t1[1][:, :], t2[1][:, :], ALU.add, [rt1[1], rt2[1]],
                       [rdst[2 * h + 1]])
            tpb = tp[:, :].bitcast(BF16)
            for t4 in range(4):
                for m in range(KC):
                    tr(c, tpb[:, m * 128:(m + 1) * 128], kT[:, m, t4 * 128:(t4 + 1) * 128], ident[:, :],
                       [rkT[m], rconst], [rtp])
                for h in range(RH):
                    ts(c, "dve", ktok[t4][:, h * 256:(h + 1) * 256], tpb[:, h * 256:(h + 1) * 256],
                       kdec[:, h:h + 1], None, ALU.mult, None, [rtp, rconst], [rktok[t4]])
            for grp in range(8):
                b = ivg % 2
                ivg += 1
                dma(c, "sp", wvg[b][:, :, :],
                    wvg_s[:, grp * 4096:(grp + 1) * 4096].rearrange("p (k j) -> p k j", k=KC), [r_w], [rwvg[b]])
                for t4 in range(4):
                    pp, rpp = ((pA, rpA), (pB, rpB), (pC, rpC), (pU[0], rpU[0]))[t4]
                    for kc in range(KC):
                        mm(c, pp[:, :], nb.hT[:, kc, t4 * 128:(t4 + 1) * 128], wvg[b][:, kc, :], kc == 0,
                           kc == KC - 1, [nb.rhT, rwvg[b]], [rpp])
                    if grp < 4:
                        cp(c, "act", v[t4][:, grp * 512:(grp + 1) * 512], pp[:, :], [rpp], [rv[t4]])
                    else:
                        act(c, sg[t4][:, (grp - 4) * 512:(grp - 3) * 512], pp[:, :], AF.Silu, [rpp], [rsg[t4]])
            for t4 in range(4):
                cs = slice(t4 * 128, (t4 + 1) * 128)
                for h in range(RH):
                    hs = slice(h * 512, (h + 1) * 512)
                    for a in range(2):
                        mm(c, pS[:, 0:128], kT[:, 2 * h + a, cs], qT[:, 2 * h + a, cs], a == 0, a == 1,
                           [rkT[2 * h + a], rqT[2 * h + a]], [rpS])
                    tt(c, "dve", sm[:, :], pS[:, 0:128], maskT[:, h * 128:(h + 1) * 128], ALU.mult,
                       [rpS, rconst], [rsm])
                    mm(c, pO[:, :], sm[:, :], v[t4][:, hs], True, True, [rsm, rv[t4]], [rpO])
                    for a in range(2):
                        mm(c, pC[:, :], qT[:, 2 * h + a, cs], stbf[:, a, h, :], a == 0, a == 1,
                           [rqT[2 * h + a], rstbf[a][h]], [rpC])
                    for a in range(2):
                        mm(c, pU[a][:, :], ktok[t4][:, (2 * h + a) * 128:(2 * h + a + 1) * 128], v[t4][:, hs],
                           True, True, [rktok[t4], rv[t4]], [rpU[a]])
                    act(c, ocs[:, :], pC[:, :], AF.Identity, [rpC, rconst], [rocs], scale=qdec[:, h:h + 1])
                    tt(c, "dve", ofl[:, :], pO[:, :], ocs[:, :], ALU.add, [rpO, rocs], [rofl])
                    for a in range(2):
                        stt(c, st32[:, a, h, :], st32[:, a, h, :], cdec[h], pU[a][:, :], ALU.mult, ALU.add,
                            [rst32[a][h], rpU[a]], [rst32[a][h]])
                        cp(c, "pool", stbf[:, a, h, :], st32[:, a, h, :], [rst32[a][h]], [rstbf[a][h]])
                    act(c, ojunk[:, 0:512], ofl[:, :], AF.Square, [rofl], [rojunk, ross], accum_out=oss[:, 0:1])
                    act(c, oss[:, 1:2], oss[:, 0:1], AF.Sqrt, [ross], [ross], scale=1.0 / 512.0,
                        bias=consts["eps"][:, 0:1])
                    c.P.op("dve", lambda e: e.reciprocal(oss[:, 2:3], oss[:, 1:2]), [ross], [ross])
                    stt(c, og[:, hs], ofl[:, :], oss[:, 2:3], sg[t4][:, hs], ALU.mult, ALU.mult,
                        [rofl, ross, rsg[t4]], [rog])
                for half in range(2):
                    for f in range(8):
                        fc = half * 8 + f
                        tr(c, tpb[:, f * 128:(f + 1) * 128], og[:, fc * 128:(fc + 1) * 128], ident[:, :],
                           [rog, rconst], [rtp])
                    cp(c, "act", ogT[:, half * 8:(half + 1) * 8, :],
                       tpb[:, :].rearrange("p (f t) -> p f t", f=8), [rtp], [rogT])
                for nh in range(2):
                    pp, rpp = (pA, rpA) if nh == 0 else (pB, rpB)
                    for fc in range(16):
                        mm(c, pp[:, :], ogT[:, fc, :], wo[:, fc, nh * 512:(nh + 1) * 512], fc == 0, fc == 15,
                           [rogT, rwo], [rpp])
                    tt(c, "dve", xo[nh][:, :], pp[:, :], nb.xt[t4][:, nh * 512:(nh + 1) * 512], ALU.add,
                       [rpp, nb.rxt[t4]], [rxo[nh]])
                    r0 = tok0 + t4 * 128
                    dma(c, "pool", x_dram[r0:r0 + 128, nh * 512:(nh + 1) * 512], xo[nh][:, :], [rxo[nh]],
                        [rx_tiles[r0 // 128]])


def lay_ret(w_in, w_out):
    a = w_in[:, :2048].reshape(KC, 128, 16, 128)
    wqk = np.ascontiguousarray(a.transpose(1, 2, 0, 3).reshape(128, -1))
    b = w_in[:, 2048:].reshape(KC, 128, 8, 512)
    wvg = np.ascontiguousarray(b.transpose(1, 2, 0, 3).reshape(128, -1))
    wo = np.ascontiguousarray(w_out.reshape(16, 128, D).transpose(1, 0, 2).reshape(128, -1))
    return wqk, wvg, wo


NG = 4
HD = 64
NEGB = -30000.0


def nsa_consts_host(T):
    NQ = T // 128
    NB = T // 64
    n_cmp = (T - 32) // 16 + 1
    p = np.arange(128)
    pm = p % 64
    f = (pm % 8).astype(np.float32)
    invf = np.where(pm < 16, np.float32(500000.0) ** (np.float32(-2.0) * f / np.float32(16.0)), 0.0).astype(np.float32)
    sgn = np.where(pm < 8, -1.0, 1.0).astype(np.float32)
    partner = np.where(pm < 8, p + 8, np.where(pm < 16, p - 8, p))
    Pm = np.zeros((128, 128), np.float32)
    Pm[partner, p] = 1.0
    j = np.arange(128)[:, None]
    q = np.arange(128)[None, :]
    tri = (j <= q).astype(np.float32)
    upper = (j > q).astype(np.float32)
    ci = np.arange(256)
    nb = np.arange(64)
    ovl = ((ci[:, None] * 16 < nb[None, :] * 64 + 64) & (ci[:, None] * 16 + 32 > nb[None, :] * 64)).astype(np.float32)
    ovl[n_cmp:] = 0.0
    ovl2 = ovl.reshape(2, 128, 64).transpose(1, 0, 2).reshape(128, 128)
    maskc = np.zeros((128, 17, 128), np.float32)
    for n in range(17):
        maskc[:, n, :] = (16 * j + 31 - q <= 128 * n)
    small = np.concatenate([invf[:, None], sgn[:, None], Pm, tri, upper, ovl2, maskc.reshape(128, -1)], axis=1)
    eall = np.zeros((128, T), np.float32)
    key = np.arange(T)
    for b in range(min(64, NB)):
        eall[64 + b] = (key // 64 == b)
    addm = np.zeros((128, NQ, 64), np.float32)
    for n in range(NQ):
        t = 128 * n + np.arange(128)
        cur = t // 64
        blk = np.arange(64)[None, :]
        forced = (blk == 0) | (blk == cur[:, None]) | (blk == cur[:, None] - 1)
        future = blk > cur[:, None]
        addm[:, n, :] = np.where(future, -1e30, np.where(forced, 1e4, 0.0))
    return (np.ascontiguousarray(small, np.float32), eall, np.ascontiguousarray(addm.reshape(128, -1), np.float32))


NSMALL = 2 + 128 * 4 + 17 * 128


def lay_nsa(w_in, cmp_pos, cmp_w1, cmp_w2, w_out):
    cols = []
    for m in range(8):
        cols.append(np.arange(m * 128, (m + 1) * 128))
    base = 1024
    for (br, kv) in ((0, 0), (0, 1), (1, 0), (2, 0)):
        for gp in range(2):
            cols.append(base + br * 512 + kv * 256 + gp * 128 + np.arange(128))
    cols = np.concatenate(cols)
    a = w_in[:, cols].reshape(KC, 128, 16, 128)
    wfm = np.ascontiguousarray(a.transpose(1, 2, 0, 3).reshape(128, -1))
    tcols = np.concatenate([base + 1 * 512 + 256 + np.arange(256), base + 2 * 512 + 256 + np.arange(256),
                            2560 + np.arange(48)])
    b = w_in[:, tcols].reshape(KC, 128, 560)
    wtm = np.ascontiguousarray(b.transpose(1, 0, 2).reshape(128, -1))
    w1 = cmp_w1.reshape(2, 32, 64, 256).transpose(0, 2, 1, 3).reshape(128, 32 * 256)
    posc = cmp_pos.transpose(0, 2, 1).reshape(128, 32)
    w2 = cmp_w2.reshape(2, 2, 128, 64).transpose(2, 0, 1, 3).reshape(128, 256)
    wo = w_out.reshape(8, 128, D).transpose(1, 0, 2).reshape(128, -1)
    pack = np.concatenate([wfm, wtm, w1, posc, w2, wo], axis=1)
    return np.ascontiguousarray(pack, np.float32)


NSA_OFF = {}
_o = 0
for _k, _n in (("wfm", 16 * 1024), ("wtm", KC * 560), ("w1", 32 * 256), ("posc", 32), ("w2", 256), ("wo", 8 * D)):
    NSA_OFF[_k] = (_o, _o + _n)
    _o += _n
NSA_WTOT = _o


def bc_mid(ap2d, n):
    return ap2d.unsqueeze(1).broadcast_to([ap2d.shape[0], n, ap2d.shape[1]])


def nsa_layer(c, T, x_dram, rx_tiles, pos_dram, w_s, r_w, gain_ap, consts, ncs_in, eall_in, addm_in, qS):
    rconst = consts["r"]
    ident = consts["ident"]
    NQ = T // 128
    n_cmp = (T - 32) // 16 + 1

    def W(k):
        return w_s[:, NSA_OFF[k][0]:NSA_OFF[k][1]]

    with ExitStack() as eo:
        Lslc = c.sb([128, NG, T], BF16, eo, "Lslc")
        Lwin = c.sb([64, NG, T], BF16, eo, "Lwin")
        Vslc = c.sb([128, NQ, NG, 65], BF16, eo, "Vslc")
        Vwin = c.sb([128, NQ, NG, 65], BF16, eo, "Vwin")
        KcT = c.sb([64, NG, 256], BF16, eo, "KcT")
        Vc = c.sb([128, 2, NG, 65], BF16, eo, "Vc")
        gates = c.sb([128, NQ, 48], F32, eo, "gates")
        cb = c.sb([128, 128 * 4 + 17 * 128], BF16, eo, "cb")
        rL = Res()
        rV = Res()
        rKc = Res()
        rG = Res()
        rqS = Res()
        Pm = cb[:, 0:128]
        tri = cb[:, 128:256]
        upper = cb[:, 256:384]
        ovl = cb[:, 384:512]
        maskc = cb[:, 512:512 + 17 * 128]
        isg = c.sb([128, 2], F32, eo, "isg")
        invf = isg[:, 0:1]
        sgn = isg[:, 1:2]
        with ExitStack() as et:
            ncs = c.sb([128, NSMALL], F32, et, "ncs")
            rn = Res()
            dma(c, "sp", ncs[:, :], ncs_in[:, :], [], [rn])
            cp(c, "dve", cb[:, :], ncs[:, 2:2 + 128 * 4 + 17 * 128], [rn], [rconst])
            cp(c, "dve", isg[:, :], ncs[:, 0:2], [rn], [rconst])
        c.P.barrier()
        c.P.op("pool", lambda e: e.memset(Vslc[:, :, :, :], 1.0), [], [rV])
        c.P.op("pool", lambda e: e.memset(Vwin[:, :, :, :], 1.0), [], [rV])
        c.P.op("pool", lambda e: e.memset(Vc[:, :, :, :], 0.0), [], [rKc])
        c.P.op("pool", lambda e: e.memset(Vc[:, :, :, 64:65], 1.0), [], [rKc])
        c.P.op("pool", lambda e: e.memset(KcT[:, :, :], 0.0), [], [rKc])

        with ExitStack() as ex:
            Acmp = c.sb([128, NG, T], BF16, ex, "Acmp")
            rA = Res()
            with ExitStack() as es:
                nb = NormBufs(c, es, nx=2)
                est = c.sb([128, 512], F32, es, "est")
                rest = Res()
                for ec in range(T // 512):
                    dma(c, "sp", est[64:128, :], eall_in[64:128, ec * 512:(ec + 1) * 512], [], [rest])
                    for g in range(NG):
                        cp(c, "pool", Lslc[64:128, g, ec * 512:(ec + 1) * 512], est[64:128, :], [rest], [rL])
                wtm = c.sb([128, KC, 560], BF16, es, "wtm")
                rwtm = Res()
                dma(c, "sp", wtm[:, :, :], W("wtm").rearrange("p (k j) -> p k j", k=KC), [r_w], [rwtm])
                wfm = [c.sb([128, KC, 128], BF16, es, "wfm") for _ in range(2)]
                rwfm = [Res() for _ in range(2)]
                posi = c.sb([128, TB], I32, es, "nposi")
                ang = c.sb([128, TB], F32, es, "nang")
                rang = Res()
                Ct = c.sb([128, TB], F32, es, "nC")
                St = c.sb([128, TB], F32, es, "nS")
                rC = Res()
                rS = Res()
                tb = TrigBufs(c, es, TB)
                xb = [c.sb([128, TB], BF16, es, "xb") for _ in range(2)]
                rxb = [Res() for _ in range(2)]
                t1 = [c.sb([128, TB], F32, es, "nt1") for _ in range(2)]
                t2 = [c.sb([128, TB], F32, es, "nt2") for _ in range(2)]
                rt1 = [Res() for _ in range(2)]
                rt2 = [Res() for _ in range(2)]
                kr = [c.sb([128, TB], BF16, es, "kr") for _ in range(2)]
                rkr = [Res() for _ in range(2)]
                pX = [c.ps(es) for _ in range(2)]
                rpX = [Res() for _ in range(2)]
                pP = [c.ps(es) for _ in range(2)]
                rpP = [Res() for _ in range(2)]
                ik = 0
                for blk in range(T // TB):
                    tok0 = blk * TB
                    norm_block(c, nb, x_dram, rx_tiles, tok0, gain_ap, consts)
                    dma(c, "sp", posi[:, :], pos_dram[tok0:tok0 + TB].partition_broadcast(128), [], [rang])
                    cp(c, "dve", ang[:, :], posi[:, :], [rang], [rang])
                    ts(c, "dve", ang[:, :], ang[:, :], invf, None, ALU.mult, None, [rang, rconst], [rang])
                    trig(c, tb, ang[:, :], rang, St[:, :], rS, 0.0, scale=sgn)
                    trig(c, tb, ang[:, :], rang, Ct[:, :], rC, np.pi / 2)
                    def post(m, b):
                        rot = not (m in (10, 11))
                        if rot:
                            cp(c, "act", xb[b][:, :], pX[b][:, :], [rpX[b]], [rxb[b]])
                            mm(c, pP[b][:, :], Pm, xb[b][:, :], True, True, [rconst, rxb[b]], [rpP[b]])
                            tt(c, "dve", t1[b][:, :], pP[b][:, :], St[:, :], ALU.mult, [rpP[b], rS], [rt1[b]])
                            tt(c, "pool", t2[b][:, :], xb[b][:, :], Ct[:, :], ALU.mult, [rxb[b], rC], [rt2[b]])
                            tt(c, "pool", kr[b][:, :], t1[b][:, :], t2[b][:, :], ALU.add, [rt1[b], rt2[b]], [rkr[b]])
                        else:
                            cp(c, "act", kr[b][:, :], pX[b][:, :], [rpX[b]], [rkr[b]])
                        ts_ = slice(tok0, tok0 + TB)
                        if m < 8:
                            dma(c, "pool", qS[:, 2 * m, ts_], kr[b][0:64, :], [rkr[b]], [rqS])
                            dma(c, "pool", qS[:, 2 * m + 1, ts_], kr[b][64:128, :], [rkr[b]], [rqS])
                        else:
                            gp = (m - 8) % 2
                            kind = (m - 8) // 2
                            for hh in range(2):
                                g = gp * 2 + hh
                                src = kr[b][hh * 64:(hh + 1) * 64, :]
                                if kind == 0:
                                    dma(c, "pool", Acmp[0:64, g, ts_], src, [rkr[b]], [rA])
                                elif kind == 1:
                                    dma(c, "pool", Acmp[64:128, g, ts_], src, [rkr[b]], [rA])
                                elif kind == 2:
                                    dma(c, "pool", Lslc[0:64, g, ts_], src, [rkr[b]], [rL])
                                else:
                                    dma(c, "pool", Lwin[0:64, g, ts_], src, [rkr[b]], [rL])

                    prev = None
                    for m in range(16):
                        b = ik % 2
                        ik += 1
                        dma(c, "sp", wfm[b][:, :, :],
                            W("wfm")[:, m * 1024:(m + 1) * 1024].rearrange("p (k j) -> p k j", k=KC), [r_w], [rwfm[b]])
                        for kc in range(KC):
                            mm(c, pX[b][:, :], wfm[b][:, kc, :], nb.hT[:, kc, :], kc == 0, kc == KC - 1,
                               [rwfm[b], nb.rhT], [rpX[b]])
                        if prev is not None:
                            post(*prev)
                        prev = (m, b)
                    post(*prev)
                    for t4 in range(4):
                        n = (tok0 // 128) + t4
                        b = t4 % 2
                        for kc in range(KC):
                            mm(c, pX[b][:, :], nb.hT[:, kc, t4 * 128:(t4 + 1) * 128], wtm[:, kc, 0:512], kc == 0,
                               kc == KC - 1, [nb.rhT, rwtm], [rpX[b]])
                        cp(c, "act", Vslc[:, n, :, 0:64], pX[b][:, 0:256].rearrange("p (g d) -> p g d", d=64),
                           [rpX[b]], [rV])
                        cp(c, "act", Vwin[:, n, :, 0:64], pX[b][:, 256:512].rearrange("p (g d) -> p g d", d=64),
                           [rpX[b]], [rV])
                        for kc in range(KC):
                            mm(c, pP[b][:, 0:48], nb.hT[:, kc, t4 * 128:(t4 + 1) * 128], wtm[:, kc, 512:560], kc == 0,
                               kc == KC - 1, [nb.rhT, rwtm], [rpP[b]])
                        act(c, gates[:, n, :], pP[b][:, 0:48], AF.Sigmoid, [rpP[b]], [rG])
            c.P.barrier()
            with ExitStack() as es:
                w1 = c.sb([128, 32, 256], BF16, es, "w1")
                posc = c.sb([128, 32], BF16, es, "posc")
                w2 = c.sb([128, 2, 2, 64], BF16, es, "w2")
                rw = Res()
                dma(c, "sp", w1[:, :, :], W("w1").rearrange("p (l m) -> p l m", m=256), [r_w], [rw])
                dma(c, "sp", posc[:, :], W("posc"), [r_w], [rw])
                dma(c, "sp", w2[:, :, :, :], W("w2").rearrange("p (a b d) -> p a b d", a=2, b=2), [r_w], [rw])
                hid = c.sb([128, 2, 1024], BF16, es, "hid")
                rhid = Res()
                c.P.op("pool", lambda e: e.memset(hid[:, :, :], 0.0), [], [rhid])
                bias = c.sb([128, 4], F32, es, "cbias")
                rb = Res()
                pH = [c.ps(es) for _ in range(2)]
                rpH = [Res() for _ in range(2)]
                pb = c.ps(es)
                rpb = Res()
                i2 = 0
                for kv in range(2):
                    rows = slice(kv * 64, kv * 64 + 64)
                    for mc in range(2):
                        for l in range(32):
                            mm(c, pb[:, 0:1], w1[rows, l, mc * 128:(mc + 1) * 128], posc[rows, l:l + 1], l == 0, l == 31,
                               [rw], [rpb])
                        cp(c, "dve", bias[:, kv * 2 + mc:kv * 2 + mc + 1], pb[:, 0:1], [rpb], [rb])
                    for mc in range(2):
                        for gp in range(2):
                            b = i2 % 2
                            i2 += 1
                            for l in range(32):
                                rhs = Acmp[rows, 2 * gp:2 * gp + 2, l:l + 16 * (n_cmp - 1) + 1:16]
                                mm(c, pH[b][:, 0:2 * n_cmp].rearrange("p (g i) -> p g i", g=2),
                                   w1[rows, l, mc * 128:(mc + 1) * 128], rhs, l == 0, l == 31, [rw, rA], [rpH[b]])
                            act(c, hid[:, mc, gp * 512:gp * 512 + 512].rearrange("p (g i) -> p g i", g=2)[:, :, 0:n_cmp],
                                pH[b][:, 0:2 * n_cmp].rearrange("p (g i) -> p g i", g=2), AF.Silu, [rpH[b], rb],
                                [rhid], bias=bias[:, kv * 2 + mc:kv * 2 + mc + 1])
                    if kv == 0:
                        for gp in range(2):
                            b = i2 % 2
                            i2 += 1
                            for mc in range(2):
                                mm(c, pH[b][0:64, :], w2[:, 0, mc, :], hid[:, mc, gp * 512:(gp + 1) * 512], mc == 0,
                                   mc == 1, [rw, rhid], [rpH[b]])
                            cp(c, "dve", KcT[:, 2 * gp:2 * gp + 2, :],
                               pH[b][0:64, :].rearrange("p (g i) -> p g i", g=2), [rpH[b]], [rKc])
                        c.P.op("dve", lambda e: e.memset(KcT[:, :, n_cmp:256], 0.0), [rKc], [rKc])
                    else:
                        for g in range(NG):
                            for it in range(2):
                                b = i2 % 2
                                i2 += 1
                                for mc in range(2):
                                    mm(c, pH[b][:, 0:64], hid[:, mc, g * 256 + it * 128:g * 256 + (it + 1) * 128],
                                       w2[:, 1, mc, :], mc == 0, mc == 1, [rhid, rw], [rpH[b]])
                                cp(c, "dve", Vc[:, it, g, 0:64], pH[b][:, 0:64], [rpH[b]], [rKc])
            c.P.barrier()
        with ExitStack() as es:
            wo = c.sb([128, 8, D], BF16, es, "nwo")
            rwo = Res()
            dma(c, "sp", wo[:, :, :], W("wo").rearrange("p (f n) -> p f n", n=D), [r_w], [rwo])
            R_ = [c.sb([128, 4, 128], BF16, es, "R") for _ in range(2)]
            rR = [Res() for _ in range(2)]
            Ec = [c.sb([128, 512], BF16, es, "Ec") for _ in range(2)]
            rEc = [Res() for _ in range(2)]
            Es = [c.sb([128, 512], BF16, es, "Es") for _ in range(2)]
            rEs = [Res() for _ in range(2)]
            imp = c.sb([128, 64], F32, es, "imp")
            imp2 = c.sb([128, 64], F32, es, "imp2")
            m8 = c.sb([128, 16], F32, es, "m8")
            rimp = Res()
            bsel = c.sb([128, 128], BF16, es, "bsel")
            rbsel = Res()
            c.P.op("pool", lambda e: e.memset(bsel[:, :], 0.0), [], [rbsel])
            den = c.sb([128, 16], F32, es, "den")
            rden = Res()
            coef = c.sb([128, 12], F32, es, "coef")
            rcoef = Res()
            oacc = c.sb([128, 4, 64], F32, es, "oacc")
            roacc = Res()
            Otok = c.sb([128, D], BF16, es, "Otok")
            rOtok = Res()
            OT = c.sb([128, 8, 128], BF16, es, "OT")
            rOT = Res()
            xt = c.sb([128, D], F32, es, "nxt")
            rxt = Res()
            xo = [c.sb([128, 512], F32, es, "nxo") for _ in range(2)]
            rxo = [Res() for _ in range(2)]
            pS = [c.ps(es) for _ in range(2)]
            rpS = [Res() for _ in range(2)]
            pOc = c.ps(es)
            pI = c.ps(es)
            pOs = c.ps(es)
            pOw = c.ps(es)
            pT = c.ps(es)
            pY = c.ps(es)
            rpOc, rpI, rpOs, rpOw, rpT, rpY = Res(), Res(), Res(), Res(), Res(), Res()
            iS = 0
            iR = 0

            def heads3(ps):
                return ps[:, 0:260].rearrange("p (h e) -> p h e", e=65)

            addm_t = [c.sb([128, 64], F32, es, "addm") for _ in range(2)]
            raddm = [Res() for _ in range(2)]
            for n in range(NQ):
                qs = slice(n * 128, (n + 1) * 128)
                dma(c, "sp", xt[:, :], x_dram[n * 128:(n + 1) * 128, :], [rx_tiles[n]], [rxt])
                dma(c, "sp", addm_t[n % 2][:, :], addm_in[:, n * 64:(n + 1) * 64], [], [raddm[n % 2]])
                for g in range(NG):
                    rb_ = iR % 2
                    iR += 1
                    R = R_[rb_]
                    rRr = rR[rb_]
                    dma(c, "sp", R[0:64, :, :], qS[:, 4 * g:4 * g + 4, qs], [rqS], [rRr])
                    Rq = R[0:64, :, :]
                    ntile = 1 if (8 * n + 6) < 128 else 2
                    items = []

                    def mk_iter(kind, kt, last_kt, first_kt):
                        def S_(sb_):
                            if kind == "c":
                                mm(c, pS[sb_][:, :].rearrange("p (h q) -> p h q", h=4),
                                   KcT[0:64, g, kt * 128:(kt + 1) * 128], Rq, True, True, [rKc, rRr], [rpS[sb_]])
                            elif kind == "s":
                                mm(c, pS[sb_][:, :].rearrange("p (h q) -> p h q", h=4),
                                   Lslc[:, g, kt * 128:(kt + 1) * 128], Rf, True, True, [rL, rRr], [rpS[sb_]])
                            else:
                                mm(c, pS[sb_][:, :].rearrange("p (h q) -> p h q", h=4),
                                   Lwin[0:64, g, kt * 128:(kt + 1) * 128], Rq, True, True, [rL, rRr], [rpS[sb_]])

                        def E_(sb_):
                            act(c, Es[sb_][:, :], pS[sb_][:, :], AF.Exp, [rpS[sb_]], [rEs[sb_]], scale=0.125)
                            e3 = Es[sb_][:, :].rearrange("p (h q) -> p h q", h=4)
                            if kind == "c":
                                mi = min(n, 16) if kt == 0 else n - 16
                                if not (kt == 0 and n >= 17):
                                    tt(c, "dve", e3, e3, bc_mid(maskc[:, mi * 128:(mi + 1) * 128], 4), ALU.mult,
                                       [rEs[sb_], rconst], [rEs[sb_]])
                            else:
                                if kt == n:
                                    tt(c, "dve", e3, e3, bc_mid(tri, 4), ALU.mult, [rEs[sb_], rconst], [rEs[sb_]])
                                if kind == "w" and kt == n - 4:
                                    tt(c, "dve", e3, e3, bc_mid(upper, 4), ALU.mult, [rEs[sb_], rconst], [rEs[sb_]])

                        def PV_(sb_):
                            for h in range(4):
                                st_ = (kt == first_kt and h == 0)
                                sp_ = (kt == last_kt and h == 3)
                                lw = Es[sb_][:, h * 128:(h + 1) * 128]
                                if kind == "c":
                                    mm(c, pOc[:, h * 65:(h + 1) * 65], lw, Vc[:, kt, g, :], st_, sp_,
                                       [rEs[sb_], rKc], [rpOc], skip_group_check=True)
                                    mm(c, pI[:, h * 64:(h + 1) * 64], lw, ovl[:, kt * 64:(kt + 1) * 64], st_, sp_,
                                       [rEs[sb_], rconst], [rpI], skip_group_check=True)
                                elif kind == "s":
                                    mm(c, pOs[:, h * 65:(h + 1) * 65], lw, Vslc[:, kt, g, :], st_, sp_,
                                       [rEs[sb_], rV], [rpOs], skip_group_check=True)
                                else:
                                    mm(c, pOw[:, h * 65:(h + 1) * 65], lw, Vwin[:, kt, g, :], st_, sp_,
                                       [rEs[sb_], rV], [rpOw], skip_group_check=True)
                        return ("it", S_, E_, PV_)

                    def sel_dve():
                        ts(c, "dve", den[:, 0:4], heads3(pOc)[:, :, 64], 1e-30, None, ALU.max, None, [rpOc], [rden])
                        c.P.op("dve", lambda e: e.reciprocal(den[:, 4:8], den[:, 0:4]), [rden], [rden])
                        stt(c, imp[:, :], pI[:, 0:64], den[:, 4:5], addm_t[n % 2][:, :], ALU.mult, ALU.add,
                            [rpI, rden, raddm[n % 2]], [rimp])
                        for h in range(1, 4):
                            stt(c, imp[:, :], pI[:, h * 64:(h + 1) * 64], den[:, 4 + h:5 + h], imp[:, :], ALU.mult,
                                ALU.add, [rpI, rden, rimp], [rimp])
                        c.P.op("dve", lambda e: e.max(m8[:, 0:8], imp[:, :]), [rimp], [rimp])
                        c.P.op("dve", lambda e: e.match_replace(imp2[:, :], m8[:, 0:8], imp[:, :], -3.0e38), [rimp],
                               [rimp])
                        c.P.op("dve", lambda e: e.max(m8[:, 8:16], imp2[:, :]), [rimp], [rimp])
                        ts(c, "dve", bsel[:, 64:128], imp[:, :], m8[:, 15:16], NEGB, ALU.is_lt, ALU.mult, [rimp],
                           [rbsel])
                        tt(c, "dve", coef[:, 0:4], den[:, 4:8], gates[:, n, (4 * g) * 3:(4 * g + 4) * 3:3], ALU.mult,
                           [rden, rG], [rcoef])
                        for h in range(4):
                            ts(c, "dve", oacc[:, h, :], heads3(pOc)[:, h, 0:64], coef[:, h:h + 1], None, ALU.mult,
                               None, [rpOc, rcoef], [roacc])

                    def sel_pe():
                        tr(c, pT[:, :].bitcast(BF16)[:, 0:128], bsel[:, :], ident[:, :], [rbsel, rconst], [rpT])
                        cp(c, "act", R[64:128, :, :], bc_mid(pT[:, :].bitcast(BF16)[64:128, 0:128], 4), [rpT], [rRr])

                    def combine():
                        for br, (po, rpo) in ((1, (pOs, rpOs)), (2, (pOw, rpOw))):
                            ts(c, "dve", den[:, 8:12], heads3(po)[:, :, 64], 1e-30, None, ALU.max, None, [rpo], [rden])
                            c.P.op("dve", lambda e: e.reciprocal(den[:, 12:16], den[:, 8:12]), [rden], [rden])
                            gsl = gates[:, n, (4 * g) * 3 + br:(4 * g + 4) * 3:3]
                            tt(c, "dve", coef[:, br * 4:(br + 1) * 4], den[:, 12:16], gsl, ALU.mult, [rden, rG],
                               [rcoef])
                        for h in range(4):
                            stt(c, oacc[:, h, :], heads3(pOs)[:, h, 0:64], coef[:, 4 + h:5 + h], oacc[:, h, :],
                                ALU.mult, ALU.add, [rpOs, rcoef, roacc], [roacc])
                            stt(c, Otok[:, (4 * g + h) * 64:(4 * g + h + 1) * 64], heads3(pOw)[:, h, 0:64],
                                coef[:, 8 + h:9 + h], oacc[:, h, :], ALU.mult, ALU.add, [rpOw, rcoef, roacc], [rOtok])

                    Rf = R[:, :, :]
                    for it in range(ntile):
                        items.append(mk_iter("c", it, ntile - 1, 0))
                    items.append(("flush", sel_dve))
                    k0 = max(0, n - 4)
                    for kt in range(k0, n + 1):
                        items.append(mk_iter("w", kt, n, k0))
                    items.append(("noflush", sel_pe))
                    for kt in range(n + 1):
                        items.append(mk_iter("s", kt, n, 0))
                    items.append(("flush", combine))
                    pend = None
                    for item in items:
                        if item[0] == "it":
                            sb_ = iS % 2
                            iS += 1
                            item[1](sb_)
                            if pend is not None:
                                pend[0](pend[1])
                            item[2](sb_)
                            pend = (item[3], sb_)
                        elif item[0] == "flush":
                            if pend is not None:
                                pend[0](pend[1])
                                pend = None
                            item[1]()
                        else:
                            item[1]()
                tpb = pT[:, :].bitcast(BF16)
                for f in range(8):
                    tr(c, tpb[:, f * 128:(f + 1) * 128], Otok[:, f * 128:(f + 1) * 128], ident[:, :], [rOtok, rconst],
                       [rpT])
                cp(c, "act", OT[:, :, :], tpb[:, :].rearrange("p (f t) -> p f t", f=8), [rpT], [rOT])
                for nh in range(2):
                    for f in range(8):
                        mm(c, pY[:, :], OT[:, f, :], wo[:, f, nh * 512:(nh + 1) * 512], f == 0, f == 7, [rOT, rwo],
                           [rpY])
                    tt(c, "dve", xo[nh][:, :], pY[:, :], xt[:, nh * 512:(nh + 1) * 512], ALU.add, [rpY, rxt],
                       [rxo[nh]])
                    dma(c, "pool", x_dram[n * 128:(n + 1) * 128, nh * 512:(nh + 1) * 512], xo[nh][:, :], [rxo[nh]],
                        [rx_tiles[n]])


def lay_gain(g):
    return np.ascontiguousarray(g.reshape(KC, 128).T)


def lay_wgu(w):
    a = w.reshape(KC, 128, 2, FC, 128)
    return np.ascontiguousarray(a.transpose(1, 3, 0, 2, 4).reshape(128, -1))


def lay_wd(w):
    return np.ascontiguousarray(w.reshape(FC, 128, D).transpose(1, 0, 2).reshape(128, -1))


def build(T, plan):
    nc = bass.Bass("TRN2", target_bir_lowering=False)
    nlay = 4
    x_in = nc.dram_tensor("x", [T, D], F32, kind="ExternalInput")
    gains_in = nc.dram_tensor("gains", [128, 8 * KC], F32, kind="ExternalInput")
    gfin_in = nc.dram_tensor("gfin", [D], F32, kind="ExternalInput")
    wgu_in = nc.dram_tensor("wgu", [nlay, 128, FC * 2048], F32, kind="ExternalInput")
    wd_in = nc.dram_tensor("wd", [nlay, 128, FC * D], F32, kind="ExternalInput")
    ident_in = nc.dram_tensor("ident", [128, 128], F32, kind="ExternalInput")
    pos_in = nc.dram_tensor("pos", [T], I32, kind="ExternalInput")
    rwqk_in = nc.dram_tensor("rwqk", [2, 128, 16 * 1024], F32, kind="ExternalInput")
    rwvg_in = nc.dram_tensor("rwvg", [2, 128, 8 * 4096], F32, kind="ExternalInput")
    rwo_in = nc.dram_tensor("rwo", [2, 128, 16 * D], F32, kind="ExternalInput")
    rc_in = nc.dram_tensor("rc", [128, 521], F32, kind="ExternalInput")
    nsaw_in = nc.dram_tensor("nsaw", [2, 128, NSA_WTOT], F32, kind="ExternalInput")
    ncs_in = nc.dram_tensor("ncs", [128, NSMALL], F32, kind="ExternalInput")
    eall_in = nc.dram_tensor("eall", [128, T], F32, kind="ExternalInput")
    addm_in = nc.dram_tensor("addm", [128, (T // 128) * 64], F32, kind="ExternalInput")
    out = nc.dram_tensor("out", [T, D], F32, kind="ExternalOutput")

    with ExitStack() as es:
        c = Ctx(nc, es)
        xs = c.dram([T, D], F32, "xres")
        rx_tiles = [Res() for _ in range(T // 128)]
        consts = {}
        rconst = Res()
        consts["r"] = rconst
        identf = c.sb([128, 128], F32, None, "identf")
        ident = c.sb([128, 128], BF16, None, "ident")
        gains = c.sb([128, 8 * KC], F32, None, "gains")
        eps = c.sb([128, 1], F32, None, "eps")
        consts["ident"] = ident
        consts["eps"] = eps
        rtmp = Res()
        dma(c, "sp", identf[:, :], ident_in[:, :], [], [rtmp])
        cp(c, "dve", ident[:, :], identf[:, :], [rtmp], [rconst])
        dma(c, "sp", gains[:, :], gains_in[:, :], [], [rconst])
        c.P.op("dve", lambda e: e.memset(eps[:, :], EPS), [], [rconst])
        rc_sb = c.sb([128, 521], F32, None, "rc_sb")
        dma(c, "sp", rc_sb[:, :], rc_in[:, :], [], [rconst])
        _, cdec = ret_consts_host()
        qS = c.dram([64, 16, T], BF16, "qS")
        for i in range(T // 128):
            dma(c, "sp", xs[i * 128:(i + 1) * 128, :], x_in[i * 128:(i + 1) * 128, :], [], [rx_tiles[i]])
        used_ffn = sorted(set(p[1] for p in plan if p[0] == "ffn"))
        wgu_s = {}
        wd_s = {}
        r_w = {}
        items = []
        for l in used_ffn:
            wgu_s[l] = c.dram([128, FC * 2048], BF16, "wgus")
            wd_s[l] = c.dram([128, FC * D], BF16, "wds")
            r_w[("gu", l)] = Res()
            r_w[("d", l)] = Res()
            items.append((lambda lo, hi, l=l: wgu_in[l, :, lo:hi], lambda lo, hi, l=l: wgu_s[l][:, lo:hi],
                          FC * 2048, r_w[("gu", l)]))
            items.append((lambda lo, hi, l=l: wd_in[l, :, lo:hi], lambda lo, hi, l=l: wd_s[l][:, lo:hi],
                          FC * D, r_w[("d", l)]))
        used_ret = sorted(set(p[1] for p in plan if p[0] == "ret"))
        rw_s = {}
        for j in used_ret:
            rw_s[j] = (c.dram([128, 16 * 1024], BF16, "rwqks"), c.dram([128, 8 * 4096], BF16, "rwvgs"),
                       c.dram([128, 16 * D], BF16, "rwos"))
            r_w[("ret", j)] = Res()
            for (src_t, dst_t, N) in ((rwqk_in, rw_s[j][0], 16 * 1024), (rwvg_in, rw_s[j][1], 8 * 4096),
                                      (rwo_in, rw_s[j][2], 16 * D)):
                items.append((lambda lo, hi, j=j, src_t=src_t: src_t[j, :, lo:hi],
                              lambda lo, hi, dst_t=dst_t: dst_t[:, lo:hi], N, r_w[("ret", j)]))
        used_nsa = sorted(set(p[1] for p in plan if p[0] == "nsa"))
        nw_s = {}
        for j in used_nsa:
            nw_s[j] = c.dram([128, NSA_WTOT], BF16, "nsaws")
            r_w[("nsa", j)] = Res()
            items.append((lambda lo, hi, j=j: nsaw_in[j, :, lo:hi], lambda lo, hi, j=j: nw_s[j][:, lo:hi], NSA_WTOT,
                          r_w[("nsa", j)]))
        convert_weights(c, items)
        for p in plan:
            c.P.barrier()
            if p[0] == "nsa":
                j, li = p[1], p[2]
                nsa_layer(c, T, xs, rx_tiles, pos_in.ap(), nw_s[j], r_w[("nsa", j)], gains[:, li * KC:(li + 1) * KC],
                          consts, ncs_in, eall_in, addm_in, qS)
                continue
            if p[0] == "ret":
                j, li = p[1], p[2]
                ret_pass(c, T, xs, rx_tiles, pos_in.ap(), rw_s[j][0], rw_s[j][1], rw_s[j][2], r_w[("ret", j)],
                         gains[:, li * KC:(li + 1) * KC], consts, rc_sb, cdec)
                continue
            if p[0] == "ffn":
                l = p[1]
                ffn_pass(c, T, xs, rx_tiles, wgu_s[l], r_w[("gu", l)], wd_s[l], r_w[("d", l)],
                         gains[:, (4 + l) * KC:(5 + l) * KC], consts)
            elif p[0] == "final":
                final_pass(c, T, xs, rx_tiles, out, gfin_in, rconst, consts)
        c.P.emit()
    return nc


def prep_common(inputs):
    g = np.concatenate([lay_gain(inputs["norm_mix"][i]) for i in range(4)] +
                       [lay_gain(inputs["norm_ffn"][i]) for i in range(4)], axis=1)
    m = {
        "gains": np.ascontiguousarray(g, dtype=np.float32),
        "gfin": np.ascontiguousarray(inputs["norm_final"], dtype=np.float32),
        "wgu": np.stack([lay_wgu(inputs["ffn_w_gu"][i]) for i in range(4)]),
        "wd": np.stack([lay_wd(inputs["ffn_w_down"][i]) for i in range(4)]),
        "ident": np.eye(128, dtype=np.float32),
    }
    lr = [lay_ret(inputs["ret_w_in"][j], inputs["ret_w_out"][j]) for j in range(2)]
    m["rwqk"] = np.stack([l[0] for l in lr])
    m["rwvg"] = np.stack([l[1] for l in lr])
    m["rwo"] = np.stack([l[2] for l in lr])
    m["rc"] = ret_consts_host()[0]
    m["nsaw"] = np.stack([lay_nsa(inputs["nsa_w_in"][j], inputs["nsa_cmp_pos"][j], inputs["nsa_cmp_w1"][j],
                                  inputs["nsa_cmp_w2"][j], inputs["nsa_w_out"][j]) for j in range(2)])
    return m


def run(inputs, T, plan, ncores, trace=False):
    nc = build(T, plan)
    common = prep_common(inputs)
    common["ncs"], common["eall"], common["addm"] = nsa_consts_host(T)
    in_maps = []
    for b in range(ncores):
        m = dict(common)
        m["x"] = np.ascontiguousarray(inputs["x"][b, :T], dtype=np.float32)
        m["pos"] = np.ascontiguousarray(inputs["positions"][b, :T], dtype=np.int32)
        in_maps.append(m)
    res = run_bass_kernel_spmd(nc, in_maps, core_ids=list(range(ncores)), trace=trace)
    return np.stack([r["out"] for r in res.results], axis=0), res


def kernel(**inputs):
    inputs = {k: np.asarray(v) for k, v in inputs.items()}
    plan = [("ret", 0, 0), ("ffn", 0), ("nsa", 0, 1), ("ffn", 1), ("ret", 1, 2), ("ffn", 2), ("nsa", 1, 3),
            ("ffn", 3), ("final",)]
    out, _ = run(inputs, 4096, plan, 8)
    return out.astype(np.float32)
```

```python
import numpy as np
import concourse.bass as bass
import concourse.mybir as mybir
from concourse.bass_utils import run_bass_kernel_spmd
from contextlib import ExitStack

F32 = mybir.dt.float32
BF16 = mybir.dt.bfloat16
I32 = mybir.dt.int32
AF = mybir.ActivationFunctionType
ALU = mybir.AluOpType

D = 1024
KC = 8
FH = 2816
FC = 22
EPS = 1e-6
TB = 512
NDMA_SEM = 8

ENGS = ("pe", "act", "dve", "pool", "sp")


class Res:
    __slots__ = ("w", "r")

    def __init__(self):
        self.w = None
        self.r = {}


class Prog:
    def __init__(self, nc):
        self.nc = nc
        self.eng = {"pe": nc.tensor, "act": nc.scalar, "dve": nc.vector, "pool": nc.gpsimd, "sp": nc.sync}
        self.q = {e: [] for e in ENGS}
        self.cnt = {e: 0 for e in ENGS}
        self.dcnt = {e: 0 for e in ENGS}
        self.waited = {e: {} for e in ENGS}
        self.final = []
        self.pend = {e: {} for e in ENGS}
        self.ep = 0
        self.maxval = {}

    def barrier(self):
        cur = {}
        for e in ENGS:
            if self.cnt[e]:
                cur[(e, "c", self.ep)] = self.cnt[e]
            for j in range(min(self.dcnt[e], NDMA_SEM)):
                cur[(e, "d", j)] = 16 * ((self.dcnt[e] - 1 - j) // NDMA_SEM + 1)
        for e in ENGS:
            for k2, v in cur.items():
                if self.pend[e].get(k2, 0) < v:
                    self.pend[e][k2] = v
        self.ep += 1
        self.cnt = {e: 0 for e in ENGS}

    def op(self, eng, fn, reads=(), writes=(), dma=False):
        deps = []
        for r in reads:
            if r.w is not None:
                deps.append(r.w)
        for w in writes:
            if w.w is not None:
                deps.append(w.w)
            deps.extend(w.r.items())
        if dma:
            k = self.dcnt[eng]
            self.dcnt[eng] += 1
            sk = (eng, "d", k % NDMA_SEM)
            val = 16 * (k // NDMA_SEM + 1)
            if k >= NDMA_SEM:
                deps.append((sk, val - 16))
        else:
            self.cnt[eng] += 1
            sk = (eng, "c", self.ep)
            val = self.cnt[eng]
        waits = {}
        wd = self.waited[eng]
        if self.pend[eng]:
            deps.extend(self.pend[eng].items())
            self.pend[eng] = {}
        for (k2, v) in deps:
            if eng == "pe" and k2[0] == "pe" and k2[1] == "c":
                continue
            if wd.get(k2, 0) >= v:
                continue
            if waits.get(k2, 0) < v:
                waits[k2] = v
        for k2, v in waits.items():
            wd[k2] = v
        self.q[eng].append((tuple(waits.items()), fn, sk, 16 if dma else 1))
        ev = (sk, val)
        if self.maxval.get(sk, 0) < val:
            self.maxval[sk] = val
        for r in reads:
            if r.r.get(sk, 0) < val:
                r.r[sk] = val
        for w in writes:
            w.w = ev
            w.r = {}
        return ev

    def emit(self):
        nc = self.nc
        keys = set()
        for e in ENGS:
            for (waits, fn, sk, inc) in self.q[e]:
                keys.add(sk)
        with ExitStack() as es:
            sems = {}
            for sk in sorted(keys):
                sems[sk] = es.enter_context(nc.semaphore("s_%s_%s_%d" % sk))
            block = es.enter_context(nc.Block())
            fin = [(sems[k2], v) for k2, v in sorted(self.maxval.items())]

            def mk(e):
                def body(engine):
                    for (waits, fn, sk, inc) in self.q[e]:
                        for (k2, v) in waits:
                            engine.wait_ge(sems[k2], v)
                        fn(engine).then_inc(sems[sk], inc)
                    if e == "sp":
                        for (s, v) in fin:
                            engine.wait_ge(s, v)
                return body

            block.tensor(mk("pe"))
            block.scalar(mk("act"))
            block.vector(mk("dve"))
            block.gpsimd(mk("pool"))
            block.sync(mk("sp"))


def ap_of(t, offset, pat):
    return bass.AP(tensor=t, offset=offset, ap=[list(p) for p in pat])


class Ctx:
    def __init__(self, nc, es):
        self.nc = nc
        self.es = es
        self.P = Prog(nc)
        self.n = 0

    def sb(self, shape, dt, es=None, name=None):
        self.n += 1
        t = (es or self.es).enter_context(self.nc.sbuf_tensor("%s%d" % (name or "sb", self.n), list(shape), dt))
        return t

    def ps(self, es=None):
        self.n += 1
        t = (es or self.es).enter_context(self.nc.psum_tensor("ps%d" % self.n, [128, 512], F32))
        return t

    def dram(self, shape, dt, name=None):
        self.n += 1
        return self.nc.dram_tensor("%s%d" % (name or "scr", self.n), list(shape), dt, kind="Internal")


def dma(c, eng, out, in_, reads, writes):
    return c.P.op(eng, lambda e: e.dma_start(out=out, in_=in_), reads, writes, dma=True)


def mm(c, out, lhsT, rhs, start, stop, reads, writes, **kw):
    return c.P.op("pe", lambda e: e.matmul(out, lhsT, rhs, start=start, stop=stop, **kw), reads, writes)


def tr(c, out, in_, ident, reads, writes):
    return c.P.op("pe", lambda e: e.transpose(out, in_, ident), reads, writes)


def act(c, out, in_, func, reads, writes, **kw):
    return c.P.op("act", lambda e: e.activation(out, in_, func, **kw), reads, writes)


def tt(c, eng, out, in0, in1, op, reads, writes):
    return c.P.op(eng, lambda e: e.tensor_tensor(out, in0, in1, op), reads, writes)


def ts(c, eng, out, in0, s1, s2, op0, op1, reads, writes):
    if op1 is None:
        return c.P.op(eng, lambda e: e.tensor_scalar(out, in0, s1, None, op0), reads, writes)
    return c.P.op(eng, lambda e: e.tensor_scalar(out, in0, s1, s2, op0, op1), reads, writes)


def stt(c, out, in0, scalar, in1, op0, op1, reads, writes):
    return c.P.op("dve", lambda e: e.scalar_tensor_tensor(out, in0, scalar, in1, op0, op1), reads, writes)


def cp(c, eng, out, in_, reads, writes):
    if eng == "act":
        return c.P.op("act", lambda e: e.copy(out, in_), reads, writes)
    return c.P.op(eng, lambda e: e.tensor_copy(out, in_), reads, writes)


def conv_chunks(items, CH):
    out = []
    for (src, dst, N) in items:
        lo = 0
        while lo < N:
            n = min(CH, N - lo)
            out.append((src(lo, lo + n), dst(lo, lo + n), n))
            lo += n
    return out


def convert_weights(c, items):
    CH = 4096
    chunks = conv_chunks(items, CH)
    with ExitStack() as es:
        NB = 3
        stg = [c.sb([128, CH], F32, es, "cvf") for _ in range(NB)]
        out = [c.sb([128, CH], BF16, es, "cvb") for _ in range(NB)]
        rs = [Res() for _ in range(NB)]
        ro = [Res() for _ in range(NB)]
        engs = ["dve", "act"]

        def fin(i):
            src, dst, n = chunks[i]
            b = i % NB
            cp(c, engs[i % 2], out[b][:, :n], stg[b][:, :n], [rs[b]], [ro[b]])
            dma(c, "pool", dst, out[b][:, :n], [ro[b]], [Res()])

        for i, (src, dst, n) in enumerate(chunks):
            b = i % NB
            dma(c, "sp", stg[b][:, :n], src, [], [rs[b]])
            if i >= 1:
                fin(i - 1)
        if chunks:
            fin(len(chunks) - 1)


class BgConv:
    CH = 2048

    def __init__(self, items):
        self.chunks = conv_chunks(items, self.CH)
        self.i = 0
        self.pending = None
        self.on = False

    def attach(self, c, es):
        self.on = len(self.chunks) > 0
        if not self.on:
            return
        self.stg = [c.sb([128, self.CH], F32, es, "bgf") for _ in range(2)]
        self.out = [c.sb([128, self.CH], BF16, es, "bgb") for _ in range(2)]
        self.rs = [Res() for _ in range(2)]
        self.ro = [Res() for _ in range(2)]

    def step(self, c):
        if not self.on:
            return
        nxt = None
        if self.i < len(self.chunks):
            src, dst, n = self.chunks[self.i]
            b = self.i % 2
            dma(c, "pool", self.stg[b][:, :n], src, [], [self.rs[b]])
            nxt = (b, dst, n)
            self.i += 1
        if self.pending is not None:
            b, dst, n = self.pending
            cp(c, "pool", self.out[b][:, :n], self.stg[b][:, :n], [self.rs[b]], [self.ro[b]])
            dma(c, "pool", dst, self.out[b][:, :n], [self.ro[b]], [Res()])
        self.pending = nxt

    def flush(self, c):
        while self.on and (self.pending is not None or self.i < len(self.chunks)):
            self.step(c)


NOBG = BgConv([])


class NormBufs:
    def __init__(self, c, es, nx=4, nh=1):
        self.nx = nx
        self.xt = [c.sb([128, D], F32, es, "xt") for _ in range(nx)]
        self.rxt = [Res() for _ in range(nx)]
        self.xs = c.sb([128, D], BF16, es, "xs")
        self.rxs = Res()
        self.junk = c.sb([128, D], BF16, es, "junk")
        self.rjunk = Res()
        self.ss = c.sb([128, 4], F32, es, "ss")
        self.rs = c.sb([128, 4], F32, es, "rs")
        self.rstd = c.sb([128, 4], F32, es, "rstd")
        self.rss = [Res() for _ in range(4)]
        self.hTs = [c.sb([128, KC, TB], BF16, es, "hT") for _ in range(nh)]
        self.rhTs = [Res() for _ in range(nh)]
        self.hT = self.hTs[0]
        self.rhT = self.rhTs[0]
        self.tp = c.ps(es)
        self.rtp = Res()


def norm_block(c, nb, x_dram, rx_tiles, tok0, gain_ap, consts, hsel=0, xoff=0):
    ident, rconst = consts["ident"], consts["r"]
    for t4 in range(4):
        r0 = tok0 + t4 * 128
        dma(c, "sp", nb.xt[(xoff + t4) % nb.nx][:, :], x_dram[r0:r0 + 128, :], [rx_tiles[r0 // 128]], [nb.rxt[(xoff + t4) % nb.nx]])
        act(c, nb.junk[:, :], nb.xt[(xoff + t4) % nb.nx][:, :], AF.Square, [nb.rxt[(xoff + t4) % nb.nx]], [nb.rjunk, nb.rss[t4]],
            accum_out=nb.ss[:, t4:t4 + 1])
        act(c, nb.rs[:, t4:t4 + 1], nb.ss[:, t4:t4 + 1], AF.Sqrt, [nb.rss[t4]], [nb.rss[t4]],
            scale=1.0 / D, bias=consts["eps"][:, 0:1])
        c.P.op("dve", lambda e, t4=t4: e.reciprocal(nb.rstd[:, t4:t4 + 1], nb.rs[:, t4:t4 + 1]),
               [nb.rss[t4]], [nb.rss[t4]])
        ts(c, "dve", nb.xs[:, :], nb.xt[(xoff + t4) % nb.nx][:, :], nb.rstd[:, t4:t4 + 1], None, ALU.mult, None,
           [nb.rxt[(xoff + t4) % nb.nx], nb.rss[t4]], [nb.rxs])
        tpb = nb.tp[:, :].bitcast(BF16)
        for kc in range(KC):
            tr(c, tpb[:, kc * 128:(kc + 1) * 128], nb.xs[:, kc * 128:(kc + 1) * 128], ident[:, :],
               [nb.rxs, rconst], [nb.rtp])
        for kc in range(KC):
            ts(c, "dve", nb.hTs[hsel][:, kc, t4 * 128:(t4 + 1) * 128],
               tpb[:, kc * 128:(kc + 1) * 128], gain_ap[:, kc:kc + 1], None, ALU.mult, None,
               [nb.rtp, rconst], [nb.rhTs[hsel]])


def ffn_pass(c, T, x_dram, rx_tiles, wgu_s, r_wgu, wd_s, r_wd, gain_ap, consts, bg=NOBG):
    with ExitStack() as es:
        nb = NormBufs(c, es, nx=8, nh=2)
        bg.attach(c, es)
        wd = c.sb([128, FC, D], BF16, es, "wd")
        rwd = Res()
        NWB = 3
        wg = [c.sb([128, KC, 2, 128], BF16, es, "wg") for _ in range(NWB)]
        rwg = [Res() for _ in range(NWB)]
        actT = c.sb([128, FC, TB], BF16, es, "actT")
        ractT = Res()
        sa = [c.sb([128, TB], F32, es, "sa") for _ in range(2)]
        rsa = [Res() for _ in range(2)]
        pa = [c.ps(es) for _ in range(2)]
        pb = [c.ps(es) for _ in range(2)]
        rpa = [Res() for _ in range(2)]
        rpb = [Res() for _ in range(2)]
        py = [c.ps(es) for _ in range(2)]
        rpy = [Res() for _ in range(2)]
        xo = [c.sb([128, TB], F32, es, "xo") for _ in range(2)]
        rxo = [Res() for _ in range(2)]
        half = FC // 2
        dma(c, "sp", wd[:, 0:half, :], wd_s[:, 0:half * D].rearrange("p (c n) -> p c n", n=D), [r_wd], [rwd])
        dma(c, "sp", wd[:, half:FC, :], wd_s[:, half * D:FC * D].rearrange("p (c n) -> p c n", n=D), [r_wd], [rwd])
        it = 0
        nblk = T // TB
        norm_block(c, nb, x_dram, rx_tiles, 0, gain_ap, consts, hsel=0, xoff=0)
        for blk in range(nblk):
            tok0 = blk * TB
            par = blk % 2
            hT = nb.hTs[par]
            rhT = nb.rhTs[par]
            for cc in range(FC):
                b = it % NWB
                pp = it % 2
                it += 1
                dma(c, "sp", wg[b][:, :, :, :],
                    wgu_s[:, cc * 2048:(cc + 1) * 2048].rearrange("p (k a j) -> p k a j", k=KC, a=2),
                    [r_wgu], [rwg[b]])
                for kc in range(KC):
                    mm(c, pa[pp][:, :], wg[b][:, kc, 0, :], hT[:, kc, :], kc == 0, kc == KC - 1,
                       [rwg[b], rhT], [rpa[pp]])
                for kc in range(KC):
                    mm(c, pb[pp][:, :], wg[b][:, kc, 1, :], hT[:, kc, :], kc == 0, kc == KC - 1,
                       [rwg[b], rhT], [rpb[pp]])
                act(c, sa[pp][:, :], pa[pp][:, :], AF.Silu, [rpa[pp]], [rsa[pp]])
                tt(c, "dve", actT[:, cc, :], sa[pp][:, :], pb[pp][:, :], ALU.mult, [rsa[pp], rpb[pp]], [ractT])
                if cc % 3 == 0:
                    bg.step(c)
                if cc == 10 and blk + 1 < nblk:
                    norm_block(c, nb, x_dram, rx_tiles, tok0 + TB, gain_ap, consts, hsel=1 - par, xoff=(1 - par) * 4)
            j = 0
            for t4 in range(4):
                xt = nb.xt[par * 4 + t4]
                rxt = nb.rxt[par * 4 + t4]
                for nh in range(2):
                    pp = j % 2
                    j += 1
                    for cc in range(FC):
                        mm(c, py[pp][:, :], actT[:, cc, t4 * 128:(t4 + 1) * 128], wd[:, cc, nh * 512:(nh + 1) * 512],
                           cc == 0, cc == FC - 1, [ractT, rwd], [rpy[pp]])
                    tt(c, "dve", xo[pp][:, :], py[pp][:, :], xt[:, nh * 512:(nh + 1) * 512], ALU.add,
                       [rpy[pp], rxt], [rxo[pp]])
                    r0 = tok0 + t4 * 128
                    dma(c, "sp", x_dram[r0:r0 + 128, nh * 512:(nh + 1) * 512], xo[pp][:, :], [rxo[pp]],
                        [rx_tiles[r0 // 128]])
        bg.flush(c)


def final_pass(c, T, x_dram, rx_tiles, out_dram, gfin_in, rconst, consts):
    with ExitStack() as es:
        gain_rep = c.sb([128, D], F32, es, "gfin")
        rconst = Res()
        dma(c, "sp", gain_rep[:, :], gfin_in.ap().partition_broadcast(128), [], [rconst])
        NBF = 2
        xt = [c.sb([128, D], F32, es, "fx") for _ in range(NBF)]
        rxt = [Res() for _ in range(NBF)]
        yo = [c.sb([128, D], F32, es, "fy") for _ in range(NBF)]
        ryo = [Res() for _ in range(NBF)]
        junk = c.sb([128, D], BF16, es, "fj")
        rj = Res()
        st = [c.sb([128, 4], F32, es, "fs") for _ in range(NBF)]
        rst = [Res() for _ in range(NBF)]
        rout = Res()
        for i in range(T // 128):
            b = i % NBF
            dma(c, "sp", xt[b][:, :], x_dram[i * 128:(i + 1) * 128, :], [rx_tiles[i]], [rxt[b]])
            act(c, junk[:, :], xt[b][:, :], AF.Square, [rxt[b]], [rj, rst[b]], accum_out=st[b][:, 0:1])
            act(c, st[b][:, 1:2], st[b][:, 0:1], AF.Sqrt, [rst[b]], [rst[b]], scale=1.0 / D,
                bias=consts["eps"][:, 0:1])
            c.P.op("dve", lambda e, b=b: e.reciprocal(st[b][:, 2:3], st[b][:, 1:2]), [rst[b]], [rst[b]])
            stt(c, yo[b][:, :], xt[b][:, :], st[b][:, 2:3], gain_rep[:, :], ALU.mult, ALU.mult,
                [rxt[b], rst[b], rconst], [ryo[b]])
            dma(c, "pool", out_dram[i * 128:(i + 1) * 128, :], yo[b][:, :], [ryo[b]], [rout])


TWO_PI = 6.283185307179586
C1 = 6.28125
C2 = TWO_PI - C1
PI_SAFE = 3.1415925


class TrigBufs:
    def __init__(self, c, es, n):
        self.u = c.sb([128, n], F32, es, "tg_u")
        self.ki = c.sb([128, n], I32, es, "tg_k")
        self.kf = c.sb([128, n], F32, es, "tg_kf")
        self.r = Res()


def trig(c, tb, ang, rang, out, rout, phase, scale=None):
    n = ang.shape[-1]
    ts(c, "dve", tb.u[:, :n], ang, 1.0 / TWO_PI, 0.5 + phase / TWO_PI, ALU.mult, ALU.add, [rang], [tb.r])
    cp(c, "dve", tb.ki[:, :n], tb.u[:, :n], [tb.r], [tb.r])
    cp(c, "dve", tb.kf[:, :n], tb.ki[:, :n], [tb.r], [tb.r])
    stt(c, tb.u[:, :n], tb.kf[:, :n], -C1, ang, ALU.mult, ALU.add, [tb.r, rang], [tb.r])
    stt(c, tb.u[:, :n], tb.kf[:, :n], -C2, tb.u[:, :n], ALU.mult, ALU.add, [tb.r], [tb.r])
    if phase != 0.0:
        ts(c, "dve", tb.u[:, :n], tb.u[:, :n], float(phase), None, ALU.add, None, [tb.r], [tb.r])
    ts(c, "dve", tb.kf[:, :n], tb.u[:, :n], -PI_SAFE, TWO_PI, ALU.is_lt, ALU.mult, [tb.r], [tb.r])
    tt(c, "dve", tb.u[:, :n], tb.u[:, :n], tb.kf[:, :n], ALU.add, [tb.r], [tb.r])
    ts(c, "dve", tb.u[:, :n], tb.u[:, :n], PI_SAFE, -PI_SAFE, ALU.min, ALU.max, [tb.r], [tb.r])
    if scale is None:
        act(c, out, tb.u[:, :n], AF.Sin, [tb.r], [rout])
    else:
        act(c, out, tb.u[:, :n], AF.Sin, [tb.r], [rout], scale=scale)


RH = 4
RC = 128


def ret_consts_host():
    h = np.arange(RH, dtype=np.float64)
    log_g = np.log(1.0 - 2.0 ** (-5.0 - h))
    idx = np.arange(RC, dtype=np.float64)
    diff = idx[None, :] - idx[:, None]
    m = np.where(diff[:, None, :] >= 0, np.exp(log_g[None, :, None] * np.maximum(diff[:, None, :], 0.0)), 0.0)
    qdec = np.exp(log_g[None, :] * (idx[:, None] + 1.0))
    kdec = np.exp(log_g[None, :] * (RC - 1.0 - idx[:, None]))
    cdec = np.exp(log_g * RC)
    half = 128
    invf = (np.float32(10000.0) ** (np.float32(-2.0) * np.arange(half, dtype=np.float32) / np.float32(256.0))).astype(np.float32)
    arr = np.concatenate([m.reshape(128, RH * 128), qdec, kdec, invf[:, None]], axis=1).astype(np.float32)
    return arr, [float(x) for x in cdec]


def ret_pass(c, T, x_dram, rx_tiles, pos_dram, wqk_s, wvg_s, wo_s, r_w, gain_ap, consts, rc_sb, cdec):
    rconst = consts["r"]
    ident = consts["ident"]
    maskT = rc_sb[:, 0:512]
    qdec = rc_sb[:, 512:516]
    kdec = rc_sb[:, 516:520]
    invf = rc_sb[:, 520:521]
    with ExitStack() as es:
        nb = NormBufs(c, es)
        wo = c.sb([128, 16, D], BF16, es, "wo")
        rwo = Res()
        st32 = c.sb([128, 2, RH, 512], F32, es, "st32")
        stbf = c.sb([128, 2, RH, 512], BF16, es, "stbf")
        rst32 = [[Res() for _ in range(RH)] for _ in range(2)]
        rstbf = [[Res() for _ in range(RH)] for _ in range(2)]
        qT = c.sb([128, KC, TB], BF16, es, "qT")
        kT = c.sb([128, KC, TB], BF16, es, "kT")
        rqT = [Res() for _ in range(KC)]
        rkT = [Res() for _ in range(KC)]
        ktok = [c.sb([128, D], BF16, es, "ktok") for _ in range(4)]
        rktok = [Res() for _ in range(4)]
        v = [c.sb([128, 2048], BF16, es, "v") for _ in range(4)]
        rv = [Res() for _ in range(4)]
        sg = [c.sb([128, 2048], BF16, es, "sg") for _ in range(4)]
        rsg = [Res() for _ in range(4)]
        og = c.sb([128, 2048], BF16, es, "og")
        rog = Res()
        ogT = c.sb([128, 16, 128], BF16, es, "ogT")
        rogT = Res()
        wqk = [c.sb([128, KC, 128], BF16, es, "wqk") for _ in range(2)]
        rwqk = [Res() for _ in range(2)]
        wvg = [c.sb([128, KC, 512], BF16, es, "wvg") for _ in range(2)]
        rwvg = [Res() for _ in range(2)]
        posi = c.sb([128, TB], I32, es, "posi")
        ang = c.sb([128, TB], F32, es, "ang")
        rang = Res()
        cosT = c.sb([128, TB], F32, es, "cosT")
        sinT = c.sb([128, TB], F32, es, "sinT")
        rcos = Res()
        rsin = Res()
        tb = TrigBufs(c, es, TB)
        t1 = [c.sb([128, TB], F32, es, "t1") for _ in range(2)]
        t2 = [c.sb([128, TB], F32, es, "t2") for _ in range(2)]
        rt1 = [Res() for _ in range(2)]
        rt2 = [Res() for _ in range(2)]
        sm2 = [c.sb([128, 128], BF16, es, "sm") for _ in range(2)]
        rsm2 = [Res() for _ in range(2)]
        rpS2 = [Res() for _ in range(2)]
        ross2 = [Res() for _ in range(2)]
        icore = 0
        ocs = c.sb([128, 512], F32, es, "ocs")
        rocs = Res()
        ofl = c.sb([128, 512], F32, es, "ofl")
        rofl = Res()
        oss = c.sb([128, 8], F32, es, "oss")
        ross = Res()
        xo = [c.sb([128, 512], F32, es, "rxo") for _ in range(2)]
        rxo = [Res() for _ in range(2)]
        ojunk = nb.junk
        rojunk = nb.rjunk
        pA = c.ps(es)
        pB = c.ps(es)
        rpA = Res()
        rpB = Res()
        pS = c.ps(es)
        rpS = Res()
        pO = c.ps(es)
        rpO = Res()
        pC = c.ps(es)
        rpC = Res()
        pU = [c.ps(es) for _ in range(2)]
        rpU = [Res() for _ in range(2)]
        tp = nb.tp
        rtp = nb.rtp

        c.P.op("pool", lambda e: e.memset(st32[:, :, :, :], 0.0), [], [r for rr in rst32 for r in rr])
        c.P.op("pool", lambda e: e.memset(stbf[:, :, :, :], 0.0), [], [r for rr in rstbf for r in rr])
        dma(c, "sp", wo[:, 0:8, :], wo_s[:, 0:8 * D].rearrange("p (c n) -> p c n", n=D), [r_w], [rwo])
        dma(c, "sp", wo[:, 8:16, :], wo_s[:, 8 * D:16 * D].rearrange("p (c n) -> p c n", n=D), [r_w], [rwo])
        iq = 0
        ivg = 0
        for blk in range(T // TB):
            tok0 = blk * TB
            norm_block(c, nb, x_dram, rx_tiles, tok0, gain_ap, consts)
            dma(c, "sp", posi[:, :], pos_dram[tok0:tok0 + TB].partition_broadcast(128), [], [rang])
            cp(c, "dve", ang[:, :], posi[:, :], [rang], [rang])
            ts(c, "dve", ang[:, :], ang[:, :], invf, None, ALU.mult, None, [rang, rconst], [rang])
            trig(c, tb, ang[:, :], rang, sinT[:, :], rsin, 0.0)
            trig(c, tb, ang[:, :], rang, cosT[:, :], rcos, np.pi / 2)
            for qk in range(2):
                dstT = qT if qk == 0 else kT
                rdst = rqT if qk == 0 else rkT
                for h in range(RH):
                    PA, rPA, PB, rPB = (pA, [rpA], pB, [rpB]) if h % 2 == 0 else (pS, rpS2, pO, [rpO])
                    for a, (pp, rpp) in enumerate(((PA, rPA), (PB, rPB))):
                        m = qk * 8 + 2 * h + a
                        b = iq % 2
                        iq += 1
                        dma(c, "sp", wqk[b][:, :, :],
                            wqk_s[:, m * 1024:(m + 1) * 1024].rearrange("p (k j) -> p k j", k=KC), [r_w], [rwqk[b]])
                        for kc in range(KC):
                            mm(c, pp[:, :], wqk[b][:, kc, :], nb.hT[:, kc, :], kc == 0, kc == KC - 1,
                               [rwqk[b], nb.rhT], rpp)
                    sc = 1.0 if qk == 0 else 1.0 / 16.0
                    stt(c, t1[0][:, :], PA[:, :], sc, cosT[:, :], ALU.mult, ALU.mult, rPA + [rcos], [rt1[0]])
                    stt(c, t2[0][:, :], PB[:, :], sc, sinT[:, :], ALU.mult, ALU.mult, rPB + [rsin], [rt2[0]])
                    tt(c, "pool", dstT[:, 2 * h, :], t1[0][:, :], t2[0][:, :], ALU.subtract, [rt1[0], rt2[0]],
                       [rdst[2 * h]])
                    stt(c, t1[1][:, :], PB[:, :], sc, cosT[:, :], ALU.mult, ALU.mult, rPB + [rcos], [rt1[1]])
                    stt(c, t2[1][:, :], PA[:, :], sc, sinT[:, :], ALU.mult, ALU.mult, rPA + [rsin], [rt2[1]])
                    tt(c, "pool", dstT[:, 2 * h + 1, :], t1[1][:, :], t2[1][:, :], ALU.add, [rt1[1], rt2[1]],
                       [rdst[2 * h + 1]])
            tpb = tp[:, :].bitcast(BF16)
            for t4 in range(4):
                for m in range(KC):
                    tr(c, tpb[:, m * 128:(m + 1) * 128], kT[:, m, t4 * 128:(t4 + 1) * 128], ident[:, :],
                       [rkT[m], rconst], [rtp])
                for h in range(RH):
                    ts(c, "dve", ktok[t4][:, h * 256:(h + 1) * 256], tpb[:, h * 256:(h + 1) * 256],
                       kdec[:, h:h + 1], None, ALU.mult, None, [rtp, rconst], [rktok[t4]])
            for grp in range(8):
                b = ivg % 2
                ivg += 1
                dma(c, "sp", wvg[b][:, :, :],
                    wvg_s[:, grp * 4096:(grp + 1) * 4096].rearrange("p (k j) -> p k j", k=KC), [r_w], [rwvg[b]])
                for t4 in range(4):
                    pp, rpp = ((pA, rpA), (pB, rpB), (pC, rpC), (pU[0], rpU[0]))[t4]
                    for kc in range(KC):
                        mm(c, pp[:, :], nb.hT[:, kc, t4 * 128:(t4 + 1) * 128], wvg[b][:, kc, :], kc == 0,
                           kc == KC - 1, [nb.rhT, rwvg[b]], [rpp])
                    if grp < 4:
                        cp(c, "act", v[t4][:, grp * 512:(grp + 1) * 512], pp[:, :], [rpp], [rv[t4]])
                    else:
                        act(c, sg[t4][:, (grp - 4) * 512:(grp - 3) * 512], pp[:, :], AF.Silu, [rpp], [rsg[t4]])
            def stage1(t4, h, par):
                cs = slice(t4 * 128, (t4 + 1) * 128)
                for a in range(2):
                    mm(c, pS[:, par * 128:(par + 1) * 128], kT[:, 2 * h + a, cs], qT[:, 2 * h + a, cs], a == 0, a == 1,
                       [rkT[2 * h + a], rqT[2 * h + a]], [rpS2[par]], skip_group_check=True)
                tt(c, "dve", sm2[par][:, :], pS[:, par * 128:(par + 1) * 128], maskT[:, h * 128:(h + 1) * 128],
                   ALU.mult, [rpS2[par], rconst], [rsm2[par]])

            def stage2(t4, h, par):
                cs = slice(t4 * 128, (t4 + 1) * 128)
                hs = slice(h * 512, (h + 1) * 512)
                PO, rPO, PC, rPC = (pO, rpO, pC, rpC) if par == 0 else (pA, rpA, pB, rpB)
                ocs_, rocs_ = (ocs, rocs) if par == 0 else (t1[0], rt1[0])
                ofl_, rofl_ = (ofl, rofl) if par == 0 else (t2[0], rt2[0])
                o0 = par * 4
                mm(c, PO[:, :], sm2[par][:, :], v[t4][:, hs], True, True, [rsm2[par], rv[t4]], [rPO])
                for a in range(2):
                    mm(c, PC[:, :], qT[:, 2 * h + a, cs], stbf[:, a, h, :], a == 0, a == 1,
                       [rqT[2 * h + a], rstbf[a][h]], [rPC])
                for a in range(2):
                    mm(c, pU[a][:, :], ktok[t4][:, (2 * h + a) * 128:(2 * h + a + 1) * 128], v[t4][:, hs],
                       True, True, [rktok[t4], rv[t4]], [rpU[a]])
                act(c, ocs_[:, :], PC[:, :], AF.Identity, [rPC, rconst], [rocs_], scale=qdec[:, h:h + 1])
                tt(c, "dve", ofl_[:, :], PO[:, :], ocs_[:, :], ALU.add, [rPO, rocs_], [rofl_])
                for a in range(2):
                    stt(c, st32[:, a, h, :], st32[:, a, h, :], cdec[h], pU[a][:, :], ALU.mult, ALU.add,
                        [rst32[a][h], rpU[a]], [rst32[a][h]])
                    cp(c, "pool", stbf[:, a, h, :], st32[:, a, h, :], [rst32[a][h]], [rstbf[a][h]])
                act(c, ojunk[:, 0:512], ofl_[:, :], AF.Square, [rofl_], [rojunk, ross2[par]],
                    accum_out=oss[:, o0:o0 + 1])
                act(c, oss[:, o0 + 1:o0 + 2], oss[:, o0:o0 + 1], AF.Sqrt, [ross2[par]], [ross2[par]],
                    scale=1.0 / 512.0, bias=consts["eps"][:, 0:1])
                c.P.op("dve", lambda e: e.reciprocal(oss[:, o0 + 2:o0 + 3], oss[:, o0 + 1:o0 + 2]), [ross2[par]],
                       [ross2[par]])
                stt(c, og[:, hs], ofl_[:, :], oss[:, o0 + 2:o0 + 3], sg[t4][:, hs], ALU.mult, ALU.mult,
                    [rofl_, ross2[par], rsg[t4]], [rog])

            for t4 in range(4):
                pend = None
                for h in range(RH):
                    par = icore % 2
                    icore += 1
                    stage1(t4, h, par)
                    if pend is not None:
                        stage2(*pend)
                    pend = (t4, h, par)
                stage2(*pend)
                for half in range(2):
                    for f in range(8):
                        fc = half * 8 + f
                        tr(c, tpb[:, f * 128:(f + 1) * 128], og[:, fc * 128:(fc + 1) * 128], ident[:, :],
                           [rog, rconst], [rtp])
                    cp(c, "act", ogT[:, half * 8:(half + 1) * 8, :],
                       tpb[:, :].rearrange("p (f t) -> p f t", f=8), [rtp], [rogT])
                for nh in range(2):
                    pp, rpp = (pA, rpA) if nh == 0 else (pB, rpB)
                    for fc in range(16):
                        mm(c, pp[:, :], ogT[:, fc, :], wo[:, fc, nh * 512:(nh + 1) * 512], fc == 0, fc == 15,
                           [rogT, rwo], [rpp])
                    tt(c, "dve", xo[nh][:, :], pp[:, :], nb.xt[t4][:, nh * 512:(nh + 1) * 512], ALU.add,
                       [rpp, nb.rxt[t4]], [rxo[nh]])
                    r0 = tok0 + t4 * 128
                    dma(c, "pool", x_dram[r0:r0 + 128, nh * 512:(nh + 1) * 512], xo[nh][:, :], [rxo[nh]],
                        [rx_tiles[r0 // 128]])


def lay_ret(w_in, w_out):
    a = w_in[:, :2048].reshape(KC, 128, 16, 128)
    wqk = np.ascontiguousarray(a.transpose(1, 2, 0, 3).reshape(128, -1))
    b = w_in[:, 2048:].reshape(KC, 128, 8, 512)
    wvg = np.ascontiguousarray(b.transpose(1, 2, 0, 3).reshape(128, -1))
    wo = np.ascontiguousarray(w_out.reshape(16, 128, D).transpose(1, 0, 2).reshape(128, -1))
    return wqk, wvg, wo


NG = 4
HD = 64
NEGB = -30000.0


def nsa_consts_host(T):
    NQ = T // 128
    NB = T // 64
    n_cmp = (T - 32) // 16 + 1
    p = np.arange(128)
    pm = p % 64
    f = (pm % 8).astype(np.float32)
    invf = np.where(pm < 16, np.float32(500000.0) ** (np.float32(-2.0) * f / np.float32(16.0)), 0.0).astype(np.float32)
    sgn = np.where(pm < 8, -1.0, 1.0).astype(np.float32)
    partner = np.where(pm < 8, p + 8, np.where(pm < 16, p - 8, p))
    Pm = np.zeros((128, 128), np.float32)
    Pm[partner, p] = 1.0
    j = np.arange(128)[:, None]
    q = np.arange(128)[None, :]
    tri = (j <= q).astype(np.float32)
    upper = (j > q).astype(np.float32)
    ci = np.arange(256)
    nb = np.arange(64)
    ovl = ((ci[:, None] * 16 < nb[None, :] * 64 + 64) & (ci[:, None] * 16 + 32 > nb[None, :] * 64)).astype(np.float32)
    ovl[n_cmp:] = 0.0
    ovl2 = ovl.reshape(2, 128, 64).transpose(1, 0, 2).reshape(128, 128)
    maskc = np.zeros((128, 17, 128), np.float32)
    for n in range(17):
        maskc[:, n, :] = (16 * j + 31 - q <= 128 * n)
    small = np.concatenate([invf[:, None], sgn[:, None], Pm, tri, upper, ovl2, maskc.reshape(128, -1)], axis=1)
    eall = np.zeros((128, T), np.float32)
    key = np.arange(T)
    for b in range(min(64, NB)):
        eall[64 + b] = (key // 64 == b)
    addm = np.zeros((128, NQ, 64), np.float32)
    for n in range(NQ):
        t = 128 * n + np.arange(128)
        cur = t // 64
        blk = np.arange(64)[None, :]
        forced = (blk == 0) | (blk == cur[:, None]) | (blk == cur[:, None] - 1)
        future = blk > cur[:, None]
        addm[:, n, :] = np.where(future, -1e30, np.where(forced, 1e4, 0.0))
    return (np.ascontiguousarray(small, np.float32), eall, np.ascontiguousarray(addm.reshape(128, -1), np.float32))


NSMALL = 2 + 128 * 4 + 17 * 128


def lay_nsa(w_in, cmp_pos, cmp_w1, cmp_w2, w_out):
    cols = []
    for m in range(8):
        cols.append(np.arange(m * 128, (m + 1) * 128))
    base = 1024
    for (br, kv) in ((0, 0), (0, 1), (1, 0), (2, 0)):
        for gp in range(2):
            cols.append(base + br * 512 + kv * 256 + gp * 128 + np.arange(128))
    cols = np.concatenate(cols)
    a = w_in[:, cols].reshape(KC, 128, 16, 128)
    wfm = np.ascontiguousarray(a.transpose(1, 2, 0, 3).reshape(128, -1))
    tcols = np.concatenate([base + 1 * 512 + 256 + np.arange(256), base + 2 * 512 + 256 + np.arange(256),
                            2560 + np.arange(48)])
    b = w_in[:, tcols].reshape(KC, 128, 560)
    wtm = np.ascontiguousarray(b.transpose(1, 0, 2).reshape(128, -1))
    w1 = cmp_w1.reshape(2, 32, 64, 256).transpose(0, 2, 1, 3).reshape(128, 32 * 256)
    posc = cmp_pos.transpose(0, 2, 1).reshape(128, 32)
    w2 = cmp_w2.reshape(2, 2, 128, 64).transpose(2, 0, 1, 3).reshape(128, 256)
    wo = w_out.reshape(8, 128, D).transpose(1, 0, 2).reshape(128, -1)
    pack = np.concatenate([wfm, wtm, w1, posc, w2, wo], axis=1)
    return np.ascontiguousarray(pack, np.float32)


NSA_OFF = {}
_o = 0
for _k, _n in (("wfm", 16 * 1024), ("wtm", KC * 560), ("w1", 32 * 256), ("posc", 32), ("w2", 256), ("wo", 8 * D)):
    NSA_OFF[_k] = (_o, _o + _n)
    _o += _n
NSA_WTOT = _o


def bc_mid(ap2d, n):
    return ap2d.unsqueeze(1).broadcast_to([ap2d.shape[0], n, ap2d.shape[1]])


def nsa_layer(c, T, x_dram, rx_tiles, pos_dram, w_s, r_w, gain_ap, consts, ncs_in, eall_in, addm_in, qS,
              bg=NOBG):
    rconst = consts["r"]
    ident = consts["ident"]
    NQ = T // 128
    n_cmp = (T - 32) // 16 + 1

    def W(k):
        return w_s[:, NSA_OFF[k][0]:NSA_OFF[k][1]]

    with ExitStack() as eo:
        Lslc = c.sb([128, NG, T], BF16, eo, "Lslc")
        Lwin = c.sb([64, NG, T], BF16, eo, "Lwin")
        Vslc = c.sb([128, NQ, NG, 65], BF16, eo, "Vslc")
        Vwin = c.sb([128, NQ, NG, 65], BF16, eo, "Vwin")
        KcT = c.sb([64, NG, 256], BF16, eo, "KcT")
        Vc = c.sb([128, 2, NG, 65], BF16, eo, "Vc")
        gates = c.sb([128, NQ, 48], F32, eo, "gates")
        cb = c.sb([128, 128 * 4 + 17 * 128], BF16, eo, "cb")
        rL = Res()
        rV = Res()
        rKc = Res()
        rG = Res()
        rqS = Res()
        Pm = cb[:, 0:128]
        tri = cb[:, 128:256]
        upper = cb[:, 256:384]
        ovl = cb[:, 384:512]
        maskc = cb[:, 512:512 + 17 * 128]
        isg = c.sb([128, 2], F32, eo, "isg")
        invf = isg[:, 0:1]
        sgn = isg[:, 1:2]
        with ExitStack() as et:
            ncs = c.sb([128, NSMALL], F32, et, "ncs")
            rn = Res()
            dma(c, "sp", ncs[:, :], ncs_in[:, :], [], [rn])
            cp(c, "dve", cb[:, :], ncs[:, 2:2 + 128 * 4 + 17 * 128], [rn], [rconst])
            cp(c, "dve", isg[:, :], ncs[:, 0:2], [rn], [rconst])
        c.P.barrier()
        c.P.op("pool", lambda e: e.memset(Vslc[:, :, :, :], 1.0), [], [rV])
        c.P.op("pool", lambda e: e.memset(Vwin[:, :, :, :], 1.0), [], [rV])
        c.P.op("pool", lambda e: e.memset(Vc[:, :, :, :], 0.0), [], [rKc])
        c.P.op("pool", lambda e: e.memset(Vc[:, :, :, 64:65], 1.0), [], [rKc])
        c.P.op("pool", lambda e: e.memset(KcT[:, :, :], 0.0), [], [rKc])

        with ExitStack() as ex:
            Acmp = c.sb([128, NG, T], BF16, ex, "Acmp")
            rA = Res()
            with ExitStack() as es:
                nb = NormBufs(c, es, nx=2)
                est = c.sb([128, 512], F32, es, "est")
                rest = Res()
                for ec in range(T // 512):
                    dma(c, "sp", est[64:128, :], eall_in[64:128, ec * 512:(ec + 1) * 512], [], [rest])
                    for g in range(NG):
                        cp(c, "pool", Lslc[64:128, g, ec * 512:(ec + 1) * 512], est[64:128, :], [rest], [rL])
                wtm = c.sb([128, KC, 560], BF16, es, "wtm")
                rwtm = Res()
                dma(c, "sp", wtm[:, :, :], W("wtm").rearrange("p (k j) -> p k j", k=KC), [r_w], [rwtm])
                wfm = [c.sb([128, KC, 128], BF16, es, "wfm") for _ in range(2)]
                rwfm = [Res() for _ in range(2)]
                posi = c.sb([128, TB], I32, es, "nposi")
                ang = c.sb([128, TB], F32, es, "nang")
                rang = Res()
                Ct = c.sb([128, TB], F32, es, "nC")
                St = c.sb([128, TB], F32, es, "nS")
                rC = Res()
                rS = Res()
                tb = TrigBufs(c, es, TB)
                xb = [c.sb([128, TB], BF16, es, "xb") for _ in range(2)]
                rxb = [Res() for _ in range(2)]
                t1 = [c.sb([128, TB], F32, es, "nt1") for _ in range(2)]
                t2 = [c.sb([128, TB], F32, es, "nt2") for _ in range(2)]
                rt1 = [Res() for _ in range(2)]
                rt2 = [Res() for _ in range(2)]
                kr = [c.sb([128, TB], BF16, es, "kr") for _ in range(2)]
                rkr = [Res() for _ in range(2)]
                pX = [c.ps(es) for _ in range(2)]
                rpX = [Res() for _ in range(2)]
                pP = [c.ps(es) for _ in range(2)]
                rpP = [Res() for _ in range(2)]
                ik = 0
                for blk in range(T // TB):
                    tok0 = blk * TB
                    norm_block(c, nb, x_dram, rx_tiles, tok0, gain_ap, consts)
                    dma(c, "sp", posi[:, :], pos_dram[tok0:tok0 + TB].partition_broadcast(128), [], [rang])
                    cp(c, "dve", ang[:, :], posi[:, :], [rang], [rang])
                    ts(c, "dve", ang[:, :], ang[:, :], invf, None, ALU.mult, None, [rang, rconst], [rang])
                    trig(c, tb, ang[:, :], rang, St[:, :], rS, 0.0, scale=sgn)
                    trig(c, tb, ang[:, :], rang, Ct[:, :], rC, np.pi / 2)
                    def post(m, b):
                        rot = not (m in (10, 11))
                        if rot:
                            cp(c, "act", xb[b][:, :], pX[b][:, :], [rpX[b]], [rxb[b]])
                            mm(c, pP[b][:, :], Pm, xb[b][:, :], True, True, [rconst, rxb[b]], [rpP[b]])
                            tt(c, "dve", t1[b][:, :], pP[b][:, :], St[:, :], ALU.mult, [rpP[b], rS], [rt1[b]])
                            tt(c, "pool", t2[b][:, :], xb[b][:, :], Ct[:, :], ALU.mult, [rxb[b], rC], [rt2[b]])
                            tt(c, "pool", kr[b][:, :], t1[b][:, :], t2[b][:, :], ALU.add, [rt1[b], rt2[b]], [rkr[b]])
                        else:
                            cp(c, "act", kr[b][:, :], pX[b][:, :], [rpX[b]], [rkr[b]])
                        ts_ = slice(tok0, tok0 + TB)
                        if m < 8:
                            dma(c, "pool", qS[:, 2 * m, ts_], kr[b][0:64, :], [rkr[b]], [rqS])
                            dma(c, "pool", qS[:, 2 * m + 1, ts_], kr[b][64:128, :], [rkr[b]], [rqS])
                        else:
                            gp = (m - 8) % 2
                            kind = (m - 8) // 2
                            for hh in range(2):
                                g = gp * 2 + hh
                                src = kr[b][hh * 64:(hh + 1) * 64, :]
                                if kind == 0:
                                    dma(c, "pool", Acmp[0:64, g, ts_], src, [rkr[b]], [rA])
                                elif kind == 1:
                                    dma(c, "pool", Acmp[64:128, g, ts_], src, [rkr[b]], [rA])
                                elif kind == 2:
                                    dma(c, "pool", Lslc[0:64, g, ts_], src, [rkr[b]], [rL])
                                else:
                                    dma(c, "pool", Lwin[0:64, g, ts_], src, [rkr[b]], [rL])

                    prev = None
                    for m in range(16):
                        b = ik % 2
                        ik += 1
                        dma(c, "sp", wfm[b][:, :, :],
                            W("wfm")[:, m * 1024:(m + 1) * 1024].rearrange("p (k j) -> p k j", k=KC), [r_w], [rwfm[b]])
                        for kc in range(KC):
                            mm(c, pX[b][:, :], wfm[b][:, kc, :], nb.hT[:, kc, :], kc == 0, kc == KC - 1,
                               [rwfm[b], nb.rhT], [rpX[b]])
                        if prev is not None:
                            post(*prev)
                        prev = (m, b)
                    post(*prev)
                    for t4 in range(4):
                        n = (tok0 // 128) + t4
                        b = t4 % 2
                        for kc in range(KC):
                            mm(c, pX[b][:, :], nb.hT[:, kc, t4 * 128:(t4 + 1) * 128], wtm[:, kc, 0:512], kc == 0,
                               kc == KC - 1, [nb.rhT, rwtm], [rpX[b]])
                        cp(c, "act", Vslc[:, n, :, 0:64], pX[b][:, 0:256].rearrange("p (g d) -> p g d", d=64),
                           [rpX[b]], [rV])
                        cp(c, "act", Vwin[:, n, :, 0:64], pX[b][:, 256:512].rearrange("p (g d) -> p g d", d=64),
                           [rpX[b]], [rV])
                        for kc in range(KC):
                            mm(c, pP[b][:, 0:48], nb.hT[:, kc, t4 * 128:(t4 + 1) * 128], wtm[:, kc, 512:560], kc == 0,
                               kc == KC - 1, [nb.rhT, rwtm], [rpP[b]])
                        act(c, gates[:, n, :], pP[b][:, 0:48], AF.Sigmoid, [rpP[b]], [rG])
            c.P.barrier()
            with ExitStack() as es:
                w1 = c.sb([128, 32, 256], BF16, es, "w1")
                posc = c.sb([128, 32], BF16, es, "posc")
                w2 = c.sb([128, 2, 2, 64], BF16, es, "w2")
                rw = Res()
                dma(c, "sp", w1[:, :, :], W("w1").rearrange("p (l m) -> p l m", m=256), [r_w], [rw])
                dma(c, "sp", posc[:, :], W("posc"), [r_w], [rw])
                dma(c, "sp", w2[:, :, :, :], W("w2").rearrange("p (a b d) -> p a b d", a=2, b=2), [r_w], [rw])
                hid = c.sb([128, 2, 1024], BF16, es, "hid")
                rhid = Res()
                c.P.op("pool", lambda e: e.memset(hid[:, :, :], 0.0), [], [rhid])
                bias = c.sb([128, 4], F32, es, "cbias")
                rb = Res()
                pH = [c.ps(es) for _ in range(2)]
                rpH = [Res() for _ in range(2)]
                pb = c.ps(es)
                rpb = Res()
                i2 = 0
                for kv in range(2):
                    rows = slice(kv * 64, kv * 64 + 64)
                    for mc in range(2):
                        for l in range(32):
                            mm(c, pb[:, 0:1], w1[rows, l, mc * 128:(mc + 1) * 128], posc[rows, l:l + 1], l == 0, l == 31,
                               [rw], [rpb])
                        cp(c, "dve", bias[:, kv * 2 + mc:kv * 2 + mc + 1], pb[:, 0:1], [rpb], [rb])
                    for mc in range(2):
                        for gp in range(2):
                            b = i2 % 2
                            i2 += 1
                            for l in range(32):
                                rhs = Acmp[rows, 2 * gp:2 * gp + 2, l:l + 16 * (n_cmp - 1) + 1:16]
                                mm(c, pH[b][:, 0:2 * n_cmp].rearrange("p (g i) -> p g i", g=2),
                                   w1[rows, l, mc * 128:(mc + 1) * 128], rhs, l == 0, l == 31, [rw, rA], [rpH[b]])
                            act(c, hid[:, mc, gp * 512:gp * 512 + 512].rearrange("p (g i) -> p g i", g=2)[:, :, 0:n_cmp],
                                pH[b][:, 0:2 * n_cmp].rearrange("p (g i) -> p g i", g=2), AF.Silu, [rpH[b], rb],
                                [rhid], bias=bias[:, kv * 2 + mc:kv * 2 + mc + 1])
                    if kv == 0:
                        for gp in range(2):
                            b = i2 % 2
                            i2 += 1
                            for mc in range(2):
                                mm(c, pH[b][0:64, :], w2[:, 0, mc, :], hid[:, mc, gp * 512:(gp + 1) * 512], mc == 0,
                                   mc == 1, [rw, rhid], [rpH[b]])
                            cp(c, "dve", KcT[:, 2 * gp:2 * gp + 2, :],
                               pH[b][0:64, :].rearrange("p (g i) -> p g i", g=2), [rpH[b]], [rKc])
                        c.P.op("dve", lambda e: e.memset(KcT[:, :, n_cmp:256], 0.0), [rKc], [rKc])
                    else:
                        for g in range(NG):
                            for it in range(2):
                                b = i2 % 2
                                i2 += 1
                                for mc in range(2):
                                    mm(c, pH[b][:, 0:64], hid[:, mc, g * 256 + it * 128:g * 256 + (it + 1) * 128],
                                       w2[:, 1, mc, :], mc == 0, mc == 1, [rhid, rw], [rpH[b]])
                                cp(c, "dve", Vc[:, it, g, 0:64], pH[b][:, 0:64], [rpH[b]], [rKc])
            c.P.barrier()
        with ExitStack() as es:
            bg.attach(c, es)
            wo = c.sb([128, 8, D], BF16, es, "nwo")
            rwo = Res()
            dma(c, "sp", wo[:, :, :], W("wo").rearrange("p (f n) -> p f n", n=D), [r_w], [rwo])
            R_ = [c.sb([128, 4, 128], BF16, es, "R") for _ in range(2)]
            rR = [Res() for _ in range(2)]
            Ec = [c.sb([128, 512], BF16, es, "Ec") for _ in range(2)]
            rEc = [Res() for _ in range(2)]
            Es = [c.sb([128, 512], BF16, es, "Es") for _ in range(2)]
            rEs = [Res() for _ in range(2)]
            imp = c.sb([128, 64], F32, es, "imp")
            imp2 = c.sb([128, 64], F32, es, "imp2")
            m8 = c.sb([128, 16], F32, es, "m8")
            rimp = Res()
            bsel = c.sb([128, 128], BF16, es, "bsel")
            rbsel = Res()
            c.P.op("pool", lambda e: e.memset(bsel[:, :], 0.0), [], [rbsel])
            den = c.sb([128, 16], F32, es, "den")
            rden = Res()
            coef = c.sb([128, 12], F32, es, "coef")
            rcoef = Res()
            oacc = c.sb([128, 4, 64], F32, es, "oacc")
            roacc = Res()
            Otok = c.sb([128, D], BF16, es, "Otok")
            rOtok = Res()
            OT = c.sb([128, 8, 128], BF16, es, "OT")
            rOT = Res()
            xt = c.sb([128, D], F32, es, "nxt")
            rxt = Res()
            xo = [c.sb([128, 512], F32, es, "nxo") for _ in range(2)]
            rxo = [Res() for _ in range(2)]
            pS = [c.ps(es) for _ in range(2)]
            rpS = [Res() for _ in range(2)]
            pOc = c.ps(es)
            pI = c.ps(es)
            pOs = c.ps(es)
            pOw = c.ps(es)
            pT = c.ps(es)
            pY = c.ps(es)
            rpOc, rpI, rpOs, rpOw, rpT, rpY = Res(), Res(), Res(), Res(), Res(), Res()
            iS = 0
            iR = 0

            def heads3(ps):
                return ps[:, 0:260].rearrange("p (h e) -> p h e", e=65)

            addm_t = [c.sb([128, 64], F32, es, "addm") for _ in range(2)]
            raddm = [Res() for _ in range(2)]
            xt2 = [xt, c.sb([128, D], F32, es, "nxt2")]
            rxt2 = [rxt, Res()]
            oacc2 = [oacc, c.sb([128, 4, 64], F32, es, "oacc2")]
            roacc2 = [roacc, Res()]
            st_ = {"pend": None, "iS": 0}

            def run_items(items):
                for item in items:
                    if item[0] == "it":
                        sb_ = st_["iS"] % 2
                        st_["iS"] += 1
                        item[1](sb_)
                        if st_["pend"] is not None:
                            st_["pend"][0](st_["pend"][1])
                        item[2](sb_)
                        st_["pend"] = (item[3], sb_)
                    elif item[0] == "flush":
                        if st_["pend"] is not None:
                            st_["pend"][0](st_["pend"][1])
                            st_["pend"] = None
                        item[1]()
                    else:
                        item[1]()

            def start_n(n):
                dma(c, "sp", xt2[n % 2][:, :], x_dram[n * 128:(n + 1) * 128, :], [rx_tiles[n]], [rxt2[n % 2]])
                dma(c, "sp", addm_t[n % 2][:, :], addm_in[:, n * 64:(n + 1) * 64], [], [raddm[n % 2]])

            def make_k(n, g, kk):
                qs = slice(n * 128, (n + 1) * 128)
                R = R_[kk % 2]
                rRr = rR[kk % 2]
                Rq = R[0:64, :, :]
                Rf = R[:, :, :]
                oacc_ = oacc2[kk % 2]
                roacc_ = roacc2[kk % 2]
                ntile = 1 if (8 * n + 6) < 128 else 2

                def load():
                    dma(c, "sp", R[0:64, :, :], qS[:, 4 * g:4 * g + 4, qs], [rqS], [rRr])

                def mk_iter(kind, kt, last_kt, first_kt):
                    def S_(sb_):
                        if kind == "c":
                            mm(c, pS[sb_][:, :].rearrange("p (h q) -> p h q", h=4),
                               KcT[0:64, g, kt * 128:(kt + 1) * 128], Rq, True, True, [rKc, rRr], [rpS[sb_]])
                        elif kind == "s":
                            mm(c, pS[sb_][:, :].rearrange("p (h q) -> p h q", h=4),
                               Lslc[:, g, kt * 128:(kt + 1) * 128], Rf, True, True, [rL, rRr], [rpS[sb_]])
                        else:
                            mm(c, pS[sb_][:, :].rearrange("p (h q) -> p h q", h=4),
                               Lwin[0:64, g, kt * 128:(kt + 1) * 128], Rq, True, True, [rL, rRr], [rpS[sb_]])

                    def E_(sb_):
                        act(c, Es[sb_][:, :], pS[sb_][:, :], AF.Exp, [rpS[sb_]], [rEs[sb_]], scale=0.125)
                        e3 = Es[sb_][:, :].rearrange("p (h q) -> p h q", h=4)
                        if kind == "c":
                            mi = min(n, 16) if kt == 0 else n - 16
                            if not (kt == 0 and n >= 17):
                                tt(c, "dve", e3, e3, bc_mid(maskc[:, mi * 128:(mi + 1) * 128], 4), ALU.mult,
                                   [rEs[sb_], rconst], [rEs[sb_]])
                        else:
                            if kt == n:
                                tt(c, "dve", e3, e3, bc_mid(tri, 4), ALU.mult, [rEs[sb_], rconst], [rEs[sb_]])
                            if kind == "w" and kt == n - 4:
                                tt(c, "dve", e3, e3, bc_mid(upper, 4), ALU.mult, [rEs[sb_], rconst], [rEs[sb_]])

                    def PV_(sb_):
                        for h in range(4):
                            s0 = (kt == first_kt and h == 0)
                            s1 = (kt == last_kt and h == 3)
                            lw = Es[sb_][:, h * 128:(h + 1) * 128]
                            if kind == "c":
                                mm(c, pOc[:, h * 65:(h + 1) * 65], lw, Vc[:, kt, g, :], s0, s1,
                                   [rEs[sb_], rKc], [rpOc], skip_group_check=True)
                                mm(c, pI[:, h * 64:(h + 1) * 64], lw, ovl[:, kt * 64:(kt + 1) * 64], s0, s1,
                                   [rEs[sb_], rconst], [rpI], skip_group_check=True)
                            elif kind == "s":
                                mm(c, pOs[:, h * 65:(h + 1) * 65], lw, Vslc[:, kt, g, :], s0, s1,
                                   [rEs[sb_], rV], [rpOs], skip_group_check=True)
                            else:
                                mm(c, pOw[:, h * 65:(h + 1) * 65], lw, Vwin[:, kt, g, :], s0, s1,
                                   [rEs[sb_], rV], [rpOw], skip_group_check=True)
                    return ("it", S_, E_, PV_)

                def sel_dve():
                    ts(c, "dve", den[:, 0:4], heads3(pOc)[:, :, 64], 1e-30, None, ALU.max, None, [rpOc], [rden])
                    c.P.op("dve", lambda e: e.reciprocal(den[:, 4:8], den[:, 0:4]), [rden], [rden])
                    stt(c, imp[:, :], pI[:, 0:64], den[:, 4:5], addm_t[n % 2][:, :], ALU.mult, ALU.add,
                        [rpI, rden, raddm[n % 2]], [rimp])
                    for h in range(1, 4):
                        stt(c, imp[:, :], pI[:, h * 64:(h + 1) * 64], den[:, 4 + h:5 + h], imp[:, :], ALU.mult,
                            ALU.add, [rpI, rden, rimp], [rimp])
                    c.P.op("dve", lambda e: e.max(m8[:, 0:8], imp[:, :]), [rimp], [rimp])
                    c.P.op("dve", lambda e: e.match_replace(imp2[:, :], m8[:, 0:8], imp[:, :], -3.0e38), [rimp],
                           [rimp])
                    c.P.op("dve", lambda e: e.max(m8[:, 8:16], imp2[:, :]), [rimp], [rimp])
                    ts(c, "dve", bsel[:, 64:128], imp[:, :], m8[:, 15:16], NEGB, ALU.is_lt, ALU.mult, [rimp],
                       [rbsel])
                    tt(c, "dve", coef[:, 0:4], den[:, 4:8], gates[:, n, (4 * g) * 3:(4 * g + 4) * 3:3], ALU.mult,
                       [rden, rG], [rcoef])
                    for h in range(4):
                        ts(c, "dve", oacc_[:, h, :], heads3(pOc)[:, h, 0:64], coef[:, h:h + 1], None, ALU.mult,
                           None, [rpOc, rcoef], [roacc_])

                def sel_pe():
                    tr(c, pT[:, :].bitcast(BF16)[:, 0:128], bsel[:, :], ident[:, :], [rbsel, rconst], [rpT])
                    cp(c, "act", R[64:128, :, :], bc_mid(pT[:, :].bitcast(BF16)[64:128, 0:128], 4), [rpT], [rRr])

                def combine():
                    for br, (po, rpo) in ((2, (pOw, rpOw)), (1, (pOs, rpOs))):
                        d0 = 8 if br == 1 else 0
                        ts(c, "dve", den2[:, d0:d0 + 4], heads3(po)[:, :, 64], 1e-30, None, ALU.max, None, [rpo],
                           [rden2])
                        c.P.op("dve", lambda e, d0=d0: e.reciprocal(den2[:, d0 + 4:d0 + 8], den2[:, d0:d0 + 4]),
                               [rden2], [rden2])
                        gsl = gates[:, n, (4 * g) * 3 + br:(4 * g + 4) * 3:3]
                        tt(c, "dve", coef2[:, br * 4:(br + 1) * 4], den2[:, d0 + 4:d0 + 8], gsl, ALU.mult,
                           [rden2, rG], [rcoef2])
                    for h in range(4):
                        stt(c, oacc_[:, h, :], heads3(pOw)[:, h, 0:64], coef2[:, 8 + h:9 + h], oacc_[:, h, :],
                            ALU.mult, ALU.add, [rpOw, rcoef2, roacc_], [roacc_])
                    for h in range(4):
                        stt(c, Otok[:, (4 * g + h) * 64:(4 * g + h + 1) * 64], heads3(pOs)[:, h, 0:64],
                            coef2[:, 4 + h:5 + h], oacc_[:, h, :], ALU.mult, ALU.add, [rpOs, rcoef2, roacc_],
                            [rOtok])

                k0 = max(0, n - 4)
                return {
                    "load": load,
                    "cmp": [mk_iter("c", it, ntile - 1, 0) for it in range(ntile)] + [("flush", sel_dve)],
                    "win": [mk_iter("w", kt, n, k0) for kt in range(k0, n + 1)] + [("noflush", sel_pe)],
                    "slc": [mk_iter("s", kt, n, 0) for kt in range(n + 1)] + [("flush", combine)],
                }

            def out_proj(n):
                tpb = pT[:, :].bitcast(BF16)
                for f in range(8):
                    tr(c, tpb[:, f * 128:(f + 1) * 128], Otok[:, f * 128:(f + 1) * 128], ident[:, :], [rOtok, rconst],
                       [rpT])
                cp(c, "act", OT[:, :, :], tpb[:, :].rearrange("p (f t) -> p f t", f=8), [rpT], [rOT])
                for nh in range(2):
                    for f in range(8):
                        mm(c, pY[:, :], OT[:, f, :], wo[:, f, nh * 512:(nh + 1) * 512], f == 0, f == 7, [rOT, rwo],
                           [rpY])
                    tt(c, "dve", xo[nh][:, :], pY[:, :], xt2[n % 2][:, nh * 512:(nh + 1) * 512], ALU.add,
                       [rpY, rxt2[n % 2]], [rxo[nh]])
                    dma(c, "sp", x_dram[n * 128:(n + 1) * 128, nh * 512:(nh + 1) * 512], xo[nh][:, :], [rxo[nh]],
                        [rx_tiles[n]])

            den2 = c.sb([128, 16], F32, es, "den2")
            rden2 = Res()
            coef2 = c.sb([128, 12], F32, es, "coef2")
            rcoef2 = Res()
            ks = [(n, g) for n in range(NQ) for g in range(NG)]
            start_n(0)
            cur = make_k(0, 0, 0)
            cur["load"]()
            run_items(cur["cmp"])
            run_items(cur["win"])
            for kk in range(len(ks)):
                n, g = ks[kk]
                nxt = None
                if kk + 1 < len(ks):
                    n1, g1 = ks[kk + 1]
                    if g1 == 0:
                        start_n(n1)
                    nxt = make_k(n1, g1, kk + 1)
                    nxt["load"]()
                    run_items(nxt["cmp"])
                run_items(cur["slc"])
                if g == NG - 1:
                    out_proj(n)
                if nxt is not None:
                    run_items(nxt["win"])
                bg.step(c)
                cur = nxt
            bg.flush(c)


def lay_gain(g):
    return np.ascontiguousarray(g.reshape(KC, 128).T)


def lay_wgu(w):
    a = w.reshape(KC, 128, 2, FC, 128)
    return np.ascontiguousarray(a.transpose(1, 3, 0, 2, 4).reshape(128, -1))


def lay_wd(w):
    return np.ascontiguousarray(w.reshape(FC, 128, D).transpose(1, 0, 2).reshape(128, -1))


def build(T, plan):
    nc = bass.Bass("TRN2", target_bir_lowering=False)
    nlay = 4
    x_in = nc.dram_tensor("x", [T, D], F32, kind="ExternalInput")
    gains_in = nc.dram_tensor("gains", [128, 8 * KC], F32, kind="ExternalInput")
    gfin_in = nc.dram_tensor("gfin", [D], F32, kind="ExternalInput")
    wgu_in = nc.dram_tensor("wgu", [nlay, 128, FC * 2048], F32, kind="ExternalInput")
    wd_in = nc.dram_tensor("wd", [nlay, 128, FC * D], F32, kind="ExternalInput")
    ident_in = nc.dram_tensor("ident", [128, 128], F32, kind="ExternalInput")
    pos_in = nc.dram_tensor("pos", [T], I32, kind="ExternalInput")
    rwqk_in = nc.dram_tensor("rwqk", [2, 128, 16 * 1024], F32, kind="ExternalInput")
    rwvg_in = nc.dram_tensor("rwvg", [2, 128, 8 * 4096], F32, kind="ExternalInput")
    rwo_in = nc.dram_tensor("rwo", [2, 128, 16 * D], F32, kind="ExternalInput")
    rc_in = nc.dram_tensor("rc", [128, 521], F32, kind="ExternalInput")
    nsaw_in = nc.dram_tensor("nsaw", [2, 128, NSA_WTOT], F32, kind="ExternalInput")
    ncs_in = nc.dram_tensor("ncs", [128, NSMALL], F32, kind="ExternalInput")
    eall_in = nc.dram_tensor("eall", [128, T], F32, kind="ExternalInput")
    addm_in = nc.dram_tensor("addm", [128, (T // 128) * 64], F32, kind="ExternalInput")
    out = nc.dram_tensor("out", [T, D], F32, kind="ExternalOutput")

    with ExitStack() as es:
        c = Ctx(nc, es)
        xs = c.dram([T, D], F32, "xres")
        rx_tiles = [Res() for _ in range(T // 128)]
        consts = {}
        rconst = Res()
        consts["r"] = rconst
        identf = c.sb([128, 128], F32, None, "identf")
        ident = c.sb([128, 128], BF16, None, "ident")
        gains = c.sb([128, 8 * KC], F32, None, "gains")
        eps = c.sb([128, 1], F32, None, "eps")
        consts["ident"] = ident
        consts["eps"] = eps
        rtmp = Res()
        dma(c, "sp", identf[:, :], ident_in[:, :], [], [rtmp])
        cp(c, "dve", ident[:, :], identf[:, :], [rtmp], [rconst])
        dma(c, "sp", gains[:, :], gains_in[:, :], [], [rconst])
        c.P.op("dve", lambda e: e.memset(eps[:, :], EPS), [], [rconst])
        rc_sb = c.sb([128, 521], F32, None, "rc_sb")
        dma(c, "sp", rc_sb[:, :], rc_in[:, :], [], [rconst])
        _, cdec = ret_consts_host()
        qS = c.dram([64, 16, T], BF16, "qS")
        for i in range(T // 128):
            dma(c, "sp", xs[i * 128:(i + 1) * 128, :], x_in[i * 128:(i + 1) * 128, :], [], [rx_tiles[i]])
        wgu_s = {}
        wd_s = {}
        rw_s = {}
        nw_s = {}
        r_w = {}
        wsets = {}
        for p in plan:
            key = (p[0], p[1]) if p[0] != "final" else None
            if key is None or key in wsets:
                continue
            r_w[key] = Res()
            if p[0] == "ffn":
                l = p[1]
                wgu_s[l] = c.dram([128, FC * 2048], BF16, "wgus")
                wd_s[l] = c.dram([128, FC * D], BF16, "wds")
                wsets[key] = [
                    (lambda lo, hi, l=l: wgu_in[l, :, lo:hi], lambda lo, hi, l=l: wgu_s[l][:, lo:hi], FC * 2048),
                    (lambda lo, hi, l=l: wd_in[l, :, lo:hi], lambda lo, hi, l=l: wd_s[l][:, lo:hi], FC * D)]
            elif p[0] == "ret":
                j = p[1]
                rw_s[j] = (c.dram([128, 16 * 1024], BF16, "rwqks"), c.dram([128, 8 * 4096], BF16, "rwvgs"),
                           c.dram([128, 16 * D], BF16, "rwos"))
                wsets[key] = [(lambda lo, hi, j=j, src_t=src_t: src_t[j, :, lo:hi],
                               lambda lo, hi, dst_t=dst_t: dst_t[:, lo:hi], N)
                              for (src_t, dst_t, N) in ((rwqk_in, rw_s[j][0], 16 * 1024),
                                                        (rwvg_in, rw_s[j][1], 8 * 4096),
                                                        (rwo_in, rw_s[j][2], 16 * D))]
            elif p[0] == "nsa":
                j = p[1]
                nw_s[j] = c.dram([128, NSA_WTOT], BF16, "nsaws")
                wsets[key] = [(lambda lo, hi, j=j: nsaw_in[j, :, lo:hi], lambda lo, hi, j=j: nw_s[j][:, lo:hi],
                               NSA_WTOT)]
        keys = [((p[0], p[1]) if p[0] != "final" else None) for p in plan]
        done = set()
        upfront = []
        for k in keys[:2]:
            if k is not None and k not in done:
                upfront += wsets[k]
                done.add(k)
        bgs = {}
        for i, p in enumerate(plan):
            todo = []
            if p[0] == "ffn":
                if i + 1 < len(plan) and keys[i + 1] is not None and keys[i + 1] not in done:
                    todo.append(keys[i + 1])
            elif p[0] == "nsa":
                for i2 in range(i + 1, len(plan)):
                    if plan[i2][0] == "nsa":
                        break
                    if keys[i2] is not None and keys[i2] not in done:
                        todo.append(keys[i2])
            its = []
            for k in todo:
                if k not in done:
                    its += wsets[k]
                    done.add(k)
            bgs[i] = BgConv(its)
        for k in keys:
            assert k is None or k in done, k
        convert_weights(c, upfront)
        for i, p in enumerate(plan):
            c.P.barrier()
            if p[0] == "nsa":
                j, li = p[1], p[2]
                nsa_layer(c, T, xs, rx_tiles, pos_in.ap(), nw_s[j], r_w[("nsa", j)], gains[:, li * KC:(li + 1) * KC],
                          consts, ncs_in, eall_in, addm_in, qS, bg=bgs[i])
                continue
            if p[0] == "ret":
                j, li = p[1], p[2]
                ret_pass(c, T, xs, rx_tiles, pos_in.ap(), rw_s[j][0], rw_s[j][1], rw_s[j][2], r_w[("ret", j)],
                         gains[:, li * KC:(li + 1) * KC], consts, rc_sb, cdec)
                continue
            if p[0] == "ffn":
                l = p[1]
                ffn_pass(c, T, xs, rx_tiles, wgu_s[l], r_w[("ffn", l)], wd_s[l], r_w[("ffn", l)],
                         gains[:, (4 + l) * KC:(5 + l) * KC], consts, bg=bgs[i])
            elif p[0] == "final":
                final_pass(c, T, xs, rx_tiles, out, gfin_in, rconst, consts)
        c.P.emit()
    return nc


def prep_common(inputs):
    g = np.concatenate([lay_gain(inputs["norm_mix"][i]) for i in range(4)] +
                       [lay_gain(inputs["norm_ffn"][i]) for i in range(4)], axis=1)
    m = {
        "gains": np.ascontiguousarray(g, dtype=np.float32),
        "gfin": np.ascontiguousarray(inputs["norm_final"], dtype=np.float32),
        "wgu": np.stack([lay_wgu(inputs["ffn_w_gu"][i]) for i in range(4)]),
        "wd": np.stack([lay_wd(inputs["ffn_w_down"][i]) for i in range(4)]),
        "ident": np.eye(128, dtype=np.float32),
    }
    lr = [lay_ret(inputs["ret_w_in"][j], inputs["ret_w_out"][j]) for j in range(2)]
    m["rwqk"] = np.stack([l[0] for l in lr])
    m["rwvg"] = np.stack([l[1] for l in lr])
    m["rwo"] = np.stack([l[2] for l in lr])
    m["rc"] = ret_consts_host()[0]
    m["nsaw"] = np.stack([lay_nsa(inputs["nsa_w_in"][j], inputs["nsa_cmp_pos"][j], inputs["nsa_cmp_w1"][j],
                                  inputs["nsa_cmp_w2"][j], inputs["nsa_w_out"][j]) for j in range(2)])
    return m


def run(inputs, T, plan, ncores, trace=False):
    nc = build(T, plan)
    common = prep_common(inputs)
    common["ncs"], common["eall"], common["addm"] = nsa_consts_host(T)
    in_maps = []
    for b in range(ncores):
        m = dict(common)
        m["x"] = np.ascontiguousarray(inputs["x"][b, :T], dtype=np.float32)
        m["pos"] = np.ascontiguousarray(inputs["positions"][b, :T], dtype=np.int32)
        in_maps.append(m)
    res = run_bass_kernel_spmd(nc, in_maps, core_ids=list(range(ncores)), trace=trace)
    return np.stack([r["out"] for r in res.results], axis=0), res


def kernel(**inputs):
    inputs = {k: np.asarray(v) for k, v in inputs.items()}
    plan = [("ret", 0, 0), ("ffn", 0), ("nsa", 0, 1), ("ffn", 1), ("ret", 1, 2), ("ffn", 2), ("nsa", 1, 3),
            ("ffn", 3), ("final",)]
    out, _ = run(inputs, 4096, plan, 8)
    return out.astype(np.float32)
```

```python
import numpy as np
import concourse.bass as bass
import concourse.mybir as mybir
from concourse.bass_utils import run_bass_kernel_spmd
from contextlib import ExitStack

F32 = mybir.dt.float32
BF16 = mybir.dt.bfloat16
I32 = mybir.dt.int32
AF = mybir.ActivationFunctionType
ALU = mybir.AluOpType

D = 1024
KC = 8
FH = 2816
FC = 22
EPS = 1e-6
TB = 512
NDMA_SEM = 8

ENGS = ("pe", "act", "dve", "pool", "sp")


class Res:
    __slots__ = ("w", "r")

    def __init__(self):
        self.w = None
        self.r = {}


class Prog:
    def __init__(self, nc):
        self.nc = nc
        self.eng = {"pe": nc.tensor, "act": nc.scalar, "dve": nc.vector, "pool": nc.gpsimd, "sp": nc.sync}
        self.q = {e: [] for e in ENGS}
        self.cnt = {e: 0 for e in ENGS}
        self.dcnt = {e: 0 for e in ENGS}
        self.waited = {e: {} for e in ENGS}
        self.final = []
        self.pend = {e: {} for e in ENGS}
        self.ep = 0
        self.maxval = {}

    def barrier(self):
        cur = {}
        for e in ENGS:
            if self.cnt[e]:
                cur[(e, "c", self.ep)] = self.cnt[e]
            for j in range(min(self.dcnt[e], NDMA_SEM)):
                cur[(e, "d", j)] = 16 * ((self.dcnt[e] - 1 - j) // NDMA_SEM + 1)
        for e in ENGS:
            for k2, v in cur.items():
                if self.pend[e].get(k2, 0) < v:
                    self.pend[e][k2] = v
        self.ep += 1
        self.cnt = {e: 0 for e in ENGS}

    def op(self, eng, fn, reads=(), writes=(), dma=False):
        deps = []
        for r in reads:
            if r.w is not None:
                deps.append(r.w)
        for w in writes:
            if w.w is not None:
                deps.append(w.w)
            deps.extend(w.r.items())
        if dma:
            k = self.dcnt[eng]
            self.dcnt[eng] += 1
            sk = (eng, "d", k % NDMA_SEM)
            val = 16 * (k // NDMA_SEM + 1)
            if k >= NDMA_SEM:
                deps.append((sk, val - 16))
        else:
            self.cnt[eng] += 1
            sk = (eng, "c", self.ep)
            val = self.cnt[eng]
        waits = {}
        wd = self.waited[eng]
        if self.pend[eng]:
            deps.extend(self.pend[eng].items())
            self.pend[eng] = {}
        for (k2, v) in deps:
            if eng == "pe" and k2[0] == "pe" and k2[1] == "c":
                continue
            if wd.get(k2, 0) >= v:
                continue
            if waits.get(k2, 0) < v:
                waits[k2] = v
        for k2, v in waits.items():
            wd[k2] = v
        self.q[eng].append((tuple(waits.items()), fn, sk, 16 if dma else 1))
        ev = (sk, val)
        if self.maxval.get(sk, 0) < val:
            self.maxval[sk] = val
        for r in reads:
            if r.r.get(sk, 0) < val:
                r.r[sk] = val
        for w in writes:
            w.w = ev
            w.r = {}
        return ev

    def emit(self):
        nc = self.nc
        keys = set()
        for e in ENGS:
            for (waits, fn, sk, inc) in self.q[e]:
                keys.add(sk)
        with ExitStack() as es:
            sems = {}
            for sk in sorted(keys):
                sems[sk] = es.enter_context(nc.semaphore("s_%s_%s_%d" % sk))
            block = es.enter_context(nc.Block())
            fin = [(sems[k2], v) for k2, v in sorted(self.maxval.items())]

            def mk(e):
                def body(engine):
                    for (waits, fn, sk, inc) in self.q[e]:
                        for (k2, v) in waits:
                            engine.wait_ge(sems[k2], v)
                        fn(engine).then_inc(sems[sk], inc)
                    if e == "sp":
                        for (s, v) in fin:
                            engine.wait_ge(s, v)
                return body

            block.tensor(mk("pe"))
            block.scalar(mk("act"))
            block.vector(mk("dve"))
            block.gpsimd(mk("pool"))
            block.sync(mk("sp"))


def ap_of(t, offset, pat):
    return bass.AP(tensor=t, offset=offset, ap=[list(p) for p in pat])


class Ctx:
    def __init__(self, nc, es):
        self.nc = nc
        self.es = es
        self.P = Prog(nc)
        self.n = 0

    def sb(self, shape, dt, es=None, name=None):
        self.n += 1
        t = (es or self.es).enter_context(self.nc.sbuf_tensor("%s%d" % (name or "sb", self.n), list(shape), dt))
        return t

    def ps(self, es=None):
        self.n += 1
        t = (es or self.es).enter_context(self.nc.psum_tensor("ps%d" % self.n, [128, 512], F32))
        return t

    def dram(self, shape, dt, name=None):
        self.n += 1
        return self.nc.dram_tensor("%s%d" % (name or "scr", self.n), list(shape), dt, kind="Internal")


def dma(c, eng, out, in_, reads, writes):
    return c.P.op(eng, lambda e: e.dma_start(out=out, in_=in_), reads, writes, dma=True)


def mm(c, out, lhsT, rhs, start, stop, reads, writes, **kw):
    return c.P.op("pe", lambda e: e.matmul(out, lhsT, rhs, start=start, stop=stop, **kw), reads, writes)


def tr(c, out, in_, ident, reads, writes):
    return c.P.op("pe", lambda e: e.transpose(out, in_, ident), reads, writes)


def act(c, out, in_, func, reads, writes, **kw):
    return c.P.op("act", lambda e: e.activation(out, in_, func, **kw), reads, writes)


def tt(c, eng, out, in0, in1, op, reads, writes):
    return c.P.op(eng, lambda e: e.tensor_tensor(out, in0, in1, op), reads, writes)


def ts(c, eng, out, in0, s1, s2, op0, op1, reads, writes):
    if op1 is None:
        return c.P.op(eng, lambda e: e.tensor_scalar(out, in0, s1, None, op0), reads, writes)
    return c.P.op(eng, lambda e: e.tensor_scalar(out, in0, s1, s2, op0, op1), reads, writes)


def stt(c, out, in0, scalar, in1, op0, op1, reads, writes):
    return c.P.op("dve", lambda e: e.scalar_tensor_tensor(out, in0, scalar, in1, op0, op1), reads, writes)


def cp(c, eng, out, in_, reads, writes):
    if eng == "act":
        return c.P.op("act", lambda e: e.copy(out, in_), reads, writes)
    return c.P.op(eng, lambda e: e.tensor_copy(out, in_), reads, writes)


def conv_chunks(items, CH):
    out = []
    for (src, dst, N) in items:
        lo = 0
        while lo < N:
            n = min(CH, N - lo)
            out.append((src(lo, lo + n), dst(lo, lo + n), n))
            lo += n
    return out


def convert_weights(c, items):
    CH = 4096
    chunks = conv_chunks(items, CH)
    with ExitStack() as es:
        NB = 3
        stg = [c.sb([128, CH], F32, es, "cvf") for _ in range(NB)]
        out = [c.sb([128, CH], BF16, es, "cvb") for _ in range(NB)]
        rs = [Res() for _ in range(NB)]
        ro = [Res() for _ in range(NB)]
        engs = ["dve", "act"]

        def fin(i):
            src, dst, n = chunks[i]
            b = i % NB
            cp(c, engs[i % 2], out[b][:, :n], stg[b][:, :n], [rs[b]], [ro[b]])
            dma(c, "pool", dst, out[b][:, :n], [ro[b]], [Res()])

        for i, (src, dst, n) in enumerate(chunks):
            b = i % NB
            dma(c, "sp", stg[b][:, :n], src, [], [rs[b]])
            if i >= 1:
                fin(i - 1)
        if chunks:
            fin(len(chunks) - 1)


class BgConv:
    CH = 2048

    def __init__(self, items):
        self.chunks = conv_chunks(items, self.CH)
        self.i = 0
        self.pending = None
        self.on = False

    def attach(self, c, es):
        self.on = len(self.chunks) > 0
        if not self.on:
            return
        self.stg = [c.sb([128, self.CH], F32, es, "bgf") for _ in range(2)]
        self.out = [c.sb([128, self.CH], BF16, es, "bgb") for _ in range(2)]
        self.rs = [Res() for _ in range(2)]
        self.ro = [Res() for _ in range(2)]

    def step(self, c):
        if not self.on:
            return
        nxt = None
        if self.i < len(self.chunks):
            src, dst, n = self.chunks[self.i]
            b = self.i % 2
            dma(c, "pool", self.stg[b][:, :n], src, [], [self.rs[b]])
            nxt = (b, dst, n)
            self.i += 1
        if self.pending is not None:
            b, dst, n = self.pending
            cp(c, "pool", self.out[b][:, :n], self.stg[b][:, :n], [self.rs[b]], [self.ro[b]])
            dma(c, "pool", dst, self.out[b][:, :n], [self.ro[b]], [Res()])
        self.pending = nxt

    def flush(self, c):
        while self.on and (self.pending is not None or self.i < len(self.chunks)):
            self.step(c)


NOBG = BgConv([])


class NormBufs:
    def __init__(self, c, es, nx=4, nh=1):
        self.nx = nx
        self.xt = [c.sb([128, D], F32, es, "xt") for _ in range(nx)]
        self.rxt = [Res() for _ in range(nx)]
        self.xs = c.sb([128, D], BF16, es, "xs")
        self.rxs = Res()
        self.junk = c.sb([128, D], BF16, es, "junk")
        self.rjunk = Res()
        self.ss = c.sb([128, 4], F32, es, "ss")
        self.rs = c.sb([128, 4], F32, es, "rs")
        self.rstd = c.sb([128, 4], F32, es, "rstd")
        self.rss = [Res() for _ in range(4)]
        self.hTs = [c.sb([128, KC, TB], BF16, es, "hT") for _ in range(nh)]
        self.rhTs = [Res() for _ in range(nh)]
        self.hT = self.hTs[0]
        self.rhT = self.rhTs[0]
        self.tp = c.ps(es)
        self.rtp = Res()


def norm_block(c, nb, x_dram, rx_tiles, tok0, gain_ap, consts, hsel=0, xoff=0):
    ident, rconst = consts["ident"], consts["r"]
    for t4 in range(4):
        r0 = tok0 + t4 * 128
        dma(c, "sp", nb.xt[(xoff + t4) % nb.nx][:, :], x_dram[r0:r0 + 128, :], [rx_tiles[r0 // 128]], [nb.rxt[(xoff + t4) % nb.nx]])
        act(c, nb.junk[:, :], nb.xt[(xoff + t4) % nb.nx][:, :], AF.Square, [nb.rxt[(xoff + t4) % nb.nx]], [nb.rjunk, nb.rss[t4]],
            accum_out=nb.ss[:, t4:t4 + 1])
        act(c, nb.rs[:, t4:t4 + 1], nb.ss[:, t4:t4 + 1], AF.Sqrt, [nb.rss[t4]], [nb.rss[t4]],
            scale=1.0 / D, bias=consts["eps"][:, 0:1])
        c.P.op("dve", lambda e, t4=t4: e.reciprocal(nb.rstd[:, t4:t4 + 1], nb.rs[:, t4:t4 + 1]),
               [nb.rss[t4]], [nb.rss[t4]])
        ts(c, "dve", nb.xs[:, :], nb.xt[(xoff + t4) % nb.nx][:, :], nb.rstd[:, t4:t4 + 1], None, ALU.mult, None,
           [nb.rxt[(xoff + t4) % nb.nx], nb.rss[t4]], [nb.rxs])
        tpb = nb.tp[:, :].bitcast(BF16)
        for kc in range(KC):
            tr(c, tpb[:, kc * 128:(kc + 1) * 128], nb.xs[:, kc * 128:(kc + 1) * 128], ident[:, :],
               [nb.rxs, rconst], [nb.rtp])
        for kc in range(KC):
            ts(c, "dve", nb.hTs[hsel][:, kc, t4 * 128:(t4 + 1) * 128],
               tpb[:, kc * 128:(kc + 1) * 128], gain_ap[:, kc:kc + 1], None, ALU.mult, None,
               [nb.rtp, rconst], [nb.rhTs[hsel]])


def ffn_pass(c, T, x_dram, rx_tiles, wgu_s, r_wgu, wd_s, r_wd, gain_ap, consts, bg=NOBG):
    with ExitStack() as es:
        nb = NormBufs(c, es, nx=8, nh=2)
        bg.attach(c, es)
        wd = c.sb([128, FC, D], BF16, es, "wd")
        rwd = Res()
        NWB = 3
        wg = [c.sb([128, KC, 2, 128], BF16, es, "wg") for _ in range(NWB)]
        rwg = [Res() for _ in range(NWB)]
        actT = c.sb([128, FC, TB], BF16, es, "actT")
        ractT = Res()
        sa = [c.sb([128, TB], F32, es, "sa") for _ in range(2)]
        rsa = [Res() for _ in range(2)]
        pa = [c.ps(es) for _ in range(2)]
        pb = [c.ps(es) for _ in range(2)]
        rpa = [Res() for _ in range(2)]
        rpb = [Res() for _ in range(2)]
        py = [c.ps(es) for _ in range(2)]
        rpy = [Res() for _ in range(2)]
        xo = [c.sb([128, TB], F32, es, "xo") for _ in range(2)]
        rxo = [Res() for _ in range(2)]
        half = FC // 2
        dma(c, "sp", wd[:, 0:half, :], wd_s[:, 0:half * D].rearrange("p (c n) -> p c n", n=D), [r_wd], [rwd])
        dma(c, "sp", wd[:, half:FC, :], wd_s[:, half * D:FC * D].rearrange("p (c n) -> p c n", n=D), [r_wd], [rwd])
        it = 0
        nblk = T // TB
        norm_block(c, nb, x_dram, rx_tiles, 0, gain_ap, consts, hsel=0, xoff=0)
        for blk in range(nblk):
            tok0 = blk * TB
            par = blk % 2
            hT = nb.hTs[par]
            rhT = nb.rhTs[par]
            for cc in range(FC):
                b = it % NWB
                pp = it % 2
                it += 1
                dma(c, "sp", wg[b][:, :, :, :],
                    wgu_s[:, cc * 2048:(cc + 1) * 2048].rearrange("p (k a j) -> p k a j", k=KC, a=2),
                    [r_wgu], [rwg[b]])
                for kc in range(KC):
                    mm(c, pa[pp][:, :], wg[b][:, kc, 0, :], hT[:, kc, :], kc == 0, kc == KC - 1,
                       [rwg[b], rhT], [rpa[pp]])
                for kc in range(KC):
                    mm(c, pb[pp][:, :], wg[b][:, kc, 1, :], hT[:, kc, :], kc == 0, kc == KC - 1,
                       [rwg[b], rhT], [rpb[pp]])
                act(c, sa[pp][:, :], pa[pp][:, :], AF.Silu, [rpa[pp]], [rsa[pp]])
                tt(c, "dve", actT[:, cc, :], sa[pp][:, :], pb[pp][:, :], ALU.mult, [rsa[pp], rpb[pp]], [ractT])
                if cc % 3 == 0:
                    bg.step(c)
                if cc == 10 and blk + 1 < nblk:
                    norm_block(c, nb, x_dram, rx_tiles, tok0 + TB, gain_ap, consts, hsel=1 - par, xoff=(1 - par) * 4)
            j = 0
            for t4 in range(4):
                xt = nb.xt[par * 4 + t4]
                rxt = nb.rxt[par * 4 + t4]
                for nh in range(2):
                    pp = j % 2
                    j += 1
                    for cc in range(FC):
                        mm(c, py[pp][:, :], actT[:, cc, t4 * 128:(t4 + 1) * 128], wd[:, cc, nh * 512:(nh + 1) * 512],
                           cc == 0, cc == FC - 1, [ractT, rwd], [rpy[pp]])
                    tt(c, "dve", xo[pp][:, :], py[pp][:, :], xt[:, nh * 512:(nh + 1) * 512], ALU.add,
                       [rpy[pp], rxt], [rxo[pp]])
                    r0 = tok0 + t4 * 128
                    dma(c, "sp", x_dram[r0:r0 + 128, nh * 512:(nh + 1) * 512], xo[pp][:, :], [rxo[pp]],
                        [rx_tiles[r0 // 128]])
        bg.flush(c)


def final_pass(c, T, x_dram, rx_tiles, out_dram, gfin_in, rconst, consts):
    with ExitStack() as es:
        gain_rep = c.sb([128, D], F32, es, "gfin")
        rconst = Res()
        dma(c, "sp", gain_rep[:, :], gfin_in.ap().partition_broadcast(128), [], [rconst])
        NBF = 2
        xt = [c.sb([128, D], F32, es, "fx") for _ in range(NBF)]
        rxt = [Res() for _ in range(NBF)]
        yo = [c.sb([128, D], F32, es, "fy") for _ in range(NBF)]
        ryo = [Res() for _ in range(NBF)]
        junk = c.sb([128, D], BF16, es, "fj")
        rj = Res()
        st = [c.sb([128, 4], F32, es, "fs") for _ in range(NBF)]
        rst = [Res() for _ in range(NBF)]
        rout = Res()
        for i in range(T // 128):
            b = i % NBF
            dma(c, "sp", xt[b][:, :], x_dram[i * 128:(i + 1) * 128, :], [rx_tiles[i]], [rxt[b]])
            act(c, junk[:, :], xt[b][:, :], AF.Square, [rxt[b]], [rj, rst[b]], accum_out=st[b][:, 0:1])
            act(c, st[b][:, 1:2], st[b][:, 0:1], AF.Sqrt, [rst[b]], [rst[b]], scale=1.0 / D,
                bias=consts["eps"][:, 0:1])
            c.P.op("dve", lambda e, b=b: e.reciprocal(st[b][:, 2:3], st[b][:, 1:2]), [rst[b]], [rst[b]])
            stt(c, yo[b][:, :], xt[b][:, :], st[b][:, 2:3], gain_rep[:, :], ALU.mult, ALU.mult,
                [rxt[b], rst[b], rconst], [ryo[b]])
            dma(c, "pool", out_dram[i * 128:(i + 1) * 128, :], yo[b][:, :], [ryo[b]], [rout])


TWO_PI = 6.283185307179586
C1 = 6.28125
C2 = TWO_PI - C1
PI_SAFE = 3.1415925


class TrigBufs:
    def __init__(self, c, es, n):
        self.u = c.sb([128, n], F32, es, "tg_u")
        self.ki = c.sb([128, n], I32, es, "tg_k")
        self.kf = c.sb([128, n], F32, es, "tg_kf")
        self.r = Res()


def trig(c, tb, ang, rang, out, rout, phase, scale=None):
    n = ang.shape[-1]
    ts(c, "dve", tb.u[:, :n], ang, 1.0 / TWO_PI, 0.5 + phase / TWO_PI, ALU.mult, ALU.add, [rang], [tb.r])
    cp(c, "dve", tb.ki[:, :n], tb.u[:, :n], [tb.r], [tb.r])
    cp(c, "dve", tb.kf[:, :n], tb.ki[:, :n], [tb.r], [tb.r])
    stt(c, tb.u[:, :n], tb.kf[:, :n], -C1, ang, ALU.mult, ALU.add, [tb.r, rang], [tb.r])
    stt(c, tb.u[:, :n], tb.kf[:, :n], -C2, tb.u[:, :n], ALU.mult, ALU.add, [tb.r], [tb.r])
    if phase != 0.0:
        ts(c, "dve", tb.u[:, :n], tb.u[:, :n], float(phase), None, ALU.add, None, [tb.r], [tb.r])
    ts(c, "dve", tb.kf[:, :n], tb.u[:, :n], -PI_SAFE, TWO_PI, ALU.is_lt, ALU.mult, [tb.r], [tb.r])
    tt(c, "dve", tb.u[:, :n], tb.u[:, :n], tb.kf[:, :n], ALU.add, [tb.r], [tb.r])
    ts(c, "dve", tb.u[:, :n], tb.u[:, :n], PI_SAFE, -PI_SAFE, ALU.min, ALU.max, [tb.r], [tb.r])
    if scale is None:
        act(c, out, tb.u[:, :n], AF.Sin, [tb.r], [rout])
    else:
        act(c, out, tb.u[:, :n], AF.Sin, [tb.r], [rout], scale=scale)


RH = 4
RC = 128


def ret_consts_host():
    h = np.arange(RH, dtype=np.float64)
    log_g = np.log(1.0 - 2.0 ** (-5.0 - h))
    idx = np.arange(RC, dtype=np.float64)
    diff = idx[None, :] - idx[:, None]
    m = np.where(diff[:, None, :] >= 0, np.exp(log_g[None, :, None] * np.maximum(diff[:, None, :], 0.0)), 0.0)
    qdec = np.exp(log_g[None, :] * (idx[:, None] + 1.0))
    kdec = np.exp(log_g[None, :] * (RC - 1.0 - idx[:, None]))
    cdec = np.exp(log_g * RC)
    half = 128
    invf = (np.float32(10000.0) ** (np.float32(-2.0) * np.arange(half, dtype=np.float32) / np.float32(256.0))).astype(np.float32)
    arr = np.concatenate([m.reshape(128, RH * 128), qdec, kdec, invf[:, None]], axis=1).astype(np.float32)
    return arr, [float(x) for x in cdec]


def ret_pass(c, T, x_dram, rx_tiles, pos_dram, wqk_s, wvg_s, wo_s, r_w, gain_ap, consts, rc_sb, cdec):
    rconst = consts["r"]
    ident = consts["ident"]
    maskT = rc_sb[:, 0:512]
    qdec = rc_sb[:, 512:516]
    kdec = rc_sb[:, 516:520]
    invf = rc_sb[:, 520:521]
    with ExitStack() as es:
        nb = NormBufs(c, es)
        wo = c.sb([128, 16, D], BF16, es, "wo")
        rwo = Res()
        st32 = c.sb([128, 2, RH, 512], F32, es, "st32")
        stbf = c.sb([128, 2, RH, 512], BF16, es, "stbf")
        rst32 = [[Res() for _ in range(RH)] for _ in range(2)]
        rstbf = [[Res() for _ in range(RH)] for _ in range(2)]
        qT = c.sb([128, KC, TB], BF16, es, "qT")
        kT = c.sb([128, KC, TB], BF16, es, "kT")
        rqT = [Res() for _ in range(KC)]
        rkT = [Res() for _ in range(KC)]
        ktok = [c.sb([128, D], BF16, es, "ktok") for _ in range(4)]
        rktok = [Res() for _ in range(4)]
        v = [c.sb([128, 2048], BF16, es, "v") for _ in range(4)]
        rv = [Res() for _ in range(4)]
        sg = [c.sb([128, 2048], BF16, es, "sg") for _ in range(4)]
        rsg = [Res() for _ in range(4)]
        og = c.sb([128, 2048], BF16, es, "og")
        rog = Res()
        ogT = c.sb([128, 16, 128], BF16, es, "ogT")
        rogT = Res()
        wqk = [c.sb([128, KC, 128], BF16, es, "wqk") for _ in range(2)]
        rwqk = [Res() for _ in range(2)]
        wvg = [c.sb([128, KC, 512], BF16, es, "wvg") for _ in range(2)]
        rwvg = [Res() for _ in range(2)]
        posi = c.sb([128, TB], I32, es, "posi")
        ang = c.sb([128, TB], F32, es, "ang")
        rang = Res()
        cosT = c.sb([128, TB], F32, es, "cosT")
        sinT = c.sb([128, TB], F32, es, "sinT")
        rcos = Res()
        rsin = Res()
        tb = TrigBufs(c, es, TB)
        t1 = [c.sb([128, TB], F32, es, "t1") for _ in range(2)]
        t2 = [c.sb([128, TB], F32, es, "t2") for _ in range(2)]
        rt1 = [Res() for _ in range(2)]
        rt2 = [Res() for _ in range(2)]
        sm2 = [c.sb([128, 128], BF16, es, "sm") for _ in range(2)]
        rsm2 = [Res() for _ in range(2)]
        ross2 = [Res() for _ in range(2)]
        icore = 0
        ocs = c.sb([128, 512], F32, es, "ocs")
        rocs = Res()
        ofl = c.sb([128, 512], F32, es, "ofl")
        rofl = Res()
        oss = c.sb([128, 8], F32, es, "oss")
        ross = Res()
        xo = [c.sb([128, 512], F32, es, "rxo") for _ in range(2)]
        rxo = [Res() for _ in range(2)]
        ojunk = nb.junk
        rojunk = nb.rjunk
        pA = c.ps(es)
        pB = c.ps(es)
        rpA = Res()
        rpB = Res()
        pS = c.ps(es)
        rpS = Res()
        pO = c.ps(es)
        rpO = Res()
        pC = c.ps(es)
        rpC = Res()
        pU = [c.ps(es) for _ in range(2)]
        rpU = [Res() for _ in range(2)]
        tp = nb.tp
        rtp = nb.rtp
        rpS2 = [rpS, rtp]
        pS2 = [pS, tp]

        c.P.op("pool", lambda e: e.memset(st32[:, :, :, :], 0.0), [], [r for rr in rst32 for r in rr])
        c.P.op("pool", lambda e: e.memset(stbf[:, :, :, :], 0.0), [], [r for rr in rstbf for r in rr])
        dma(c, "sp", wo[:, 0:8, :], wo_s[:, 0:8 * D].rearrange("p (c n) -> p c n", n=D), [r_w], [rwo])
        dma(c, "sp", wo[:, 8:16, :], wo_s[:, 8 * D:16 * D].rearrange("p (c n) -> p c n", n=D), [r_w], [rwo])
        iq = 0
        ivg = 0
        for blk in range(T // TB):
            tok0 = blk * TB
            norm_block(c, nb, x_dram, rx_tiles, tok0, gain_ap, consts)
            dma(c, "sp", posi[:, :], pos_dram[tok0:tok0 + TB].partition_broadcast(128), [], [rang])
            cp(c, "dve", ang[:, :], posi[:, :], [rang], [rang])
            ts(c, "dve", ang[:, :], ang[:, :], invf, None, ALU.mult, None, [rang, rconst], [rang])
            trig(c, tb, ang[:, :], rang, sinT[:, :], rsin, 0.0)
            trig(c, tb, ang[:, :], rang, cosT[:, :], rcos, np.pi / 2)
            for qk in range(2):
                dstT = qT if qk == 0 else kT
                rdst = rqT if qk == 0 else rkT
                for h in range(RH):
                    PA, rPA, PB, rPB = (pA, [rpA], pB, [rpB]) if h % 2 == 0 else (pS, [rpS], pO, [rpO])
                    for a, (pp, rpp) in enumerate(((PA, rPA), (PB, rPB))):
                        m = qk * 8 + 2 * h + a
                        b = iq % 2
                        iq += 1
                        dma(c, "sp", wqk[b][:, :, :],
                            wqk_s[:, m * 1024:(m + 1) * 1024].rearrange("p (k j) -> p k j", k=KC), [r_w], [rwqk[b]])
                        for kc in range(KC):
                            mm(c, pp[:, :], wqk[b][:, kc, :], nb.hT[:, kc, :], kc == 0, kc == KC - 1,
                               [rwqk[b], nb.rhT], rpp)
                    sc = 1.0 if qk == 0 else 1.0 / 16.0
                    stt(c, t1[0][:, :], PA[:, :], sc, cosT[:, :], ALU.mult, ALU.mult, rPA + [rcos], [rt1[0]])
                    stt(c, t2[0][:, :], PB[:, :], sc, sinT[:, :], ALU.mult, ALU.mult, rPB + [rsin], [rt2[0]])
                    tt(c, "pool", dstT[:, 2 * h, :], t1[0][:, :], t2[0][:, :], ALU.subtract, [rt1[0], rt2[0]],
                       [rdst[2 * h]])
                    stt(c, t1[1][:, :], PB[:, :], sc, cosT[:, :], ALU.mult, ALU.mult, rPB + [rcos], [rt1[1]])
                    stt(c, t2[1][:, :], PA[:, :], sc, sinT[:, :], ALU.mult, ALU.mult, rPA + [rsin], [rt2[1]])
                    tt(c, "pool", dstT[:, 2 * h + 1, :], t1[1][:, :], t2[1][:, :], ALU.add, [rt1[1], rt2[1]],
                       [rdst[2 * h + 1]])
            tpb = tp[:, :].bitcast(BF16)
            for t4 in range(4):
                for m in range(KC):
                    tr(c, tpb[:, m * 128:(m + 1) * 128], kT[:, m, t4 * 128:(t4 + 1) * 128], ident[:, :],
                       [rkT[m], rconst], [rtp])
                for h in range(RH):
                    ts(c, "dve", ktok[t4][:, h * 256:(h + 1) * 256], tpb[:, h * 256:(h + 1) * 256],
                       kdec[:, h:h + 1], None, ALU.mult, None, [rtp, rconst], [rktok[t4]])
            for grp in range(8):
                b = ivg % 2
                ivg += 1
                dma(c, "sp", wvg[b][:, :, :],
                    wvg_s[:, grp * 4096:(grp + 1) * 4096].rearrange("p (k j) -> p k j", k=KC), [r_w], [rwvg[b]])
                for t4 in range(4):
                    pp, rpp = ((pA, rpA), (pB, rpB), (pC, rpC), (pU[0], rpU[0]))[t4]
                    for kc in range(KC):
                        mm(c, pp[:, :], nb.hT[:, kc, t4 * 128:(t4 + 1) * 128], wvg[b][:, kc, :], kc == 0,
                           kc == KC - 1, [nb.rhT, rwvg[b]], [rpp])
                    if grp < 4:
                        cp(c, "act", v[t4][:, grp * 512:(grp + 1) * 512], pp[:, :], [rpp], [rv[t4]])
                    else:
                        act(c, sg[t4][:, (grp - 4) * 512:(grp - 3) * 512], pp[:, :], AF.Silu, [rpp], [rsg[t4]])
            def stage1(t4, h, par):
                cs = slice(t4 * 128, (t4 + 1) * 128)
                for a in range(2):
                    mm(c, pS2[par][:, 0:128], kT[:, 2 * h + a, cs], qT[:, 2 * h + a, cs], a == 0, a == 1,
                       [rkT[2 * h + a], rqT[2 * h + a]], [rpS2[par]], skip_group_check=True)
                tt(c, "dve", sm2[par][:, :], pS2[par][:, 0:128], maskT[:, h * 128:(h + 1) * 128],
                   ALU.mult, [rpS2[par], rconst], [rsm2[par]])

            def stage2(t4, h, par):
                cs = slice(t4 * 128, (t4 + 1) * 128)
                hs = slice(h * 512, (h + 1) * 512)
                PO, rPO, PC, rPC = (pO, rpO, pC, rpC) if par == 0 else (pA, rpA, pB, rpB)
                ocs_, rocs_ = (ocs, rocs) if par == 0 else (t1[0], rt1[0])
                ofl_, rofl_ = (ofl, rofl) if par == 0 else (t2[0], rt2[0])
                o0 = par * 4
                mm(c, PO[:, :], sm2[par][:, :], v[t4][:, hs], True, True, [rsm2[par], rv[t4]], [rPO])
                for a in range(2):
                    mm(c, PC[:, :], qT[:, 2 * h + a, cs], stbf[:, a, h, :], a == 0, a == 1,
                       [rqT[2 * h + a], rstbf[a][h]], [rPC])
                for a in range(2):
                    mm(c, pU[a][:, :], ktok[t4][:, (2 * h + a) * 128:(2 * h + a + 1) * 128], v[t4][:, hs],
                       True, True, [rktok[t4], rv[t4]], [rpU[a]])
                act(c, ocs_[:, :], PC[:, :], AF.Identity, [rPC, rconst], [rocs_], scale=qdec[:, h:h + 1])
                tt(c, "dve", ofl_[:, :], PO[:, :], ocs_[:, :], ALU.add, [rPO, rocs_], [rofl_])
                for a in range(2):
                    stt(c, st32[:, a, h, :], st32[:, a, h, :], cdec[h], pU[a][:, :], ALU.mult, ALU.add,
                        [rst32[a][h], rpU[a]], [rst32[a][h]])
                    cp(c, "pool", stbf[:, a, h, :], st32[:, a, h, :], [rst32[a][h]], [rstbf[a][h]])
                act(c, ojunk[:, 0:512], ofl_[:, :], AF.Square, [rofl_], [rojunk, ross2[par]],
                    accum_out=oss[:, o0:o0 + 1])
                act(c, oss[:, o0 + 1:o0 + 2], oss[:, o0:o0 + 1], AF.Sqrt, [ross2[par]], [ross2[par]],
                    scale=1.0 / 512.0, bias=consts["eps"][:, 0:1])
                c.P.op("dve", lambda e: e.reciprocal(oss[:, o0 + 2:o0 + 3], oss[:, o0 + 1:o0 + 2]), [ross2[par]],
                       [ross2[par]])
                stt(c, og[:, hs], ofl_[:, :], oss[:, o0 + 2:o0 + 3], sg[t4][:, hs], ALU.mult, ALU.mult,
                    [rofl_, ross2[par], rsg[t4]], [rog])

            for t4 in range(4):
                pend = None
                for h in range(RH):
                    par = icore % 2
                    icore += 1
                    stage1(t4, h, par)
                    if pend is not None:
                        stage2(*pend)
                    pend = (t4, h, par)
                stage2(*pend)
                for half in range(2):
                    for f in range(8):
                        fc = half * 8 + f
                        tr(c, tpb[:, f * 128:(f + 1) * 128], og[:, fc * 128:(fc + 1) * 128], ident[:, :],
                           [rog, rconst], [rtp])
                    cp(c, "act", ogT[:, half * 8:(half + 1) * 8, :],
                       tpb[:, :].rearrange("p (f t) -> p f t", f=8), [rtp], [rogT])
                for nh in range(2):
                    pp, rpp = (pA, rpA) if nh == 0 else (pB, rpB)
                    for fc in range(16):
                        mm(c, pp[:, :], ogT[:, fc, :], wo[:, fc, nh * 512:(nh + 1) * 512], fc == 0, fc == 15,
                           [rogT, rwo], [rpp])
                    tt(c, "dve", xo[nh][:, :], pp[:, :], nb.xt[t4][:, nh * 512:(nh + 1) * 512], ALU.add,
                       [rpp, nb.rxt[t4]], [rxo[nh]])
                    r0 = tok0 + t4 * 128
                    dma(c, "pool", x_dram[r0:r0 + 128, nh * 512:(nh + 1) * 512], xo[nh][:, :], [rxo[nh]],
                        [rx_tiles[r0 // 128]])


def lay_ret(w_in, w_out):
    a = w_in[:, :2048].reshape(KC, 128, 16, 128)
    wqk = np.ascontiguousarray(a.transpose(1, 2, 0, 3).reshape(128, -1))
    b = w_in[:, 2048:].reshape(KC, 128, 8, 512)
    wvg = np.ascontiguousarray(b.transpose(1, 2, 0, 3).reshape(128, -1))
    wo = np.ascontiguousarray(w_out.reshape(16, 128, D).transpose(1, 0, 2).reshape(128, -1))
    return wqk, wvg, wo


NG = 4
HD = 64
NEGB = -30000.0


def nsa_consts_host(T):
    NQ = T // 128
    NB = T // 64
    n_cmp = (T - 32) // 16 + 1
    p = np.arange(128)
    pm = p % 64
    f = (pm % 8).astype(np.float32)
    invf = np.where(pm < 16, np.float32(500000.0) ** (np.float32(-2.0) * f / np.float32(16.0)), 0.0).astype(np.float32)
    sgn = np.where(pm < 8, -1.0, 1.0).astype(np.float32)
    partner = np.where(pm < 8, p + 8, np.where(pm < 16, p - 8, p))
    Pm = np.zeros((128, 128), np.float32)
    Pm[partner, p] = 1.0
    j = np.arange(128)[:, None]
    q = np.arange(128)[None, :]
    tri = (j <= q).astype(np.float32)
    upper = (j > q).astype(np.float32)
    ci = np.arange(256)
    nb = np.arange(64)
    ovl = ((ci[:, None] * 16 < nb[None, :] * 64 + 64) & (ci[:, None] * 16 + 32 > nb[None, :] * 64)).astype(np.float32)
    ovl[n_cmp:] = 0.0
    ovl2 = ovl.reshape(2, 128, 64).transpose(1, 0, 2).reshape(128, 128)
    maskc = np.zeros((128, 17, 128), np.float32)
    for n in range(17):
        maskc[:, n, :] = (16 * j + 31 - q <= 128 * n)
    small = np.concatenate([invf[:, None], sgn[:, None], Pm, tri, upper, ovl2, maskc.reshape(128, -1)], axis=1)
    eall = np.zeros((128, T), np.float32)
    key = np.arange(T)
    for b in range(min(64, NB)):
        eall[64 + b] = (key // 64 == b)
    addm = np.zeros((128, NQ, 64), np.float32)
    for n in range(NQ):
        t = 128 * n + np.arange(128)
        cur = t // 64
        blk = np.arange(64)[None, :]
        forced = (blk == 0) | (blk == cur[:, None]) | (blk == cur[:, None] - 1)
        future = blk > cur[:, None]
        addm[:, n, :] = np.where(future, -1e30, np.where(forced, 1e4, 0.0))
    return (np.ascontiguousarray(small, np.float32), eall, np.ascontiguousarray(addm.reshape(128, -1), np.float32))


NSMALL = 2 + 128 * 4 + 17 * 128


def lay_nsa(w_in, cmp_pos, cmp_w1, cmp_w2, w_out):
    cols = []
    for m in range(8):
        cols.append(np.arange(m * 128, (m + 1) * 128))
    base = 1024
    for (br, kv) in ((0, 0), (0, 1), (1, 0), (2, 0)):
        for gp in range(2):
            cols.append(base + br * 512 + kv * 256 + gp * 128 + np.arange(128))
    cols = np.concatenate(cols)
    a = w_in[:, cols].reshape(KC, 128, 16, 128)
    wfm = np.ascontiguousarray(a.transpose(1, 2, 0, 3).reshape(128, -1))
    tcols = np.concatenate([base + 1 * 512 + 256 + np.arange(256), base + 2 * 512 + 256 + np.arange(256),
                            2560 + np.arange(48)])
    b = w_in[:, tcols].reshape(KC, 128, 560)
    wtm = np.ascontiguousarray(b.transpose(1, 0, 2).reshape(128, -1))
    w1 = cmp_w1.reshape(2, 32, 64, 256).transpose(0, 2, 1, 3).reshape(128, 32 * 256)
    posc = cmp_pos.transpose(0, 2, 1).reshape(128, 32)
    w2 = cmp_w2.reshape(2, 2, 128, 64).transpose(2, 0, 1, 3).reshape(128, 256)
    wo = w_out.reshape(8, 128, D).transpose(1, 0, 2).reshape(128, -1)
    pack = np.concatenate([wfm, wtm, w1, posc, w2, wo], axis=1)
    return np.ascontiguousarray(pack, np.float32)


NSA_OFF = {}
_o = 0
for _k, _n in (("wfm", 16 * 1024), ("wtm", KC * 560), ("w1", 32 * 256), ("posc", 32), ("w2", 256), ("wo", 8 * D)):
    NSA_OFF[_k] = (_o, _o + _n)
    _o += _n
NSA_WTOT = _o


def bc_mid(ap2d, n):
    return ap2d.unsqueeze(1).broadcast_to([ap2d.shape[0], n, ap2d.shape[1]])


def nsa_layer(c, T, x_dram, rx_tiles, pos_dram, w_s, r_w, gain_ap, consts, ncs_in, eall_in, addm_in, qS,
              bg=NOBG):
    rconst = consts["r"]
    ident = consts["ident"]
    NQ = T // 128
    n_cmp = (T - 32) // 16 + 1

    def W(k):
        return w_s[:, NSA_OFF[k][0]:NSA_OFF[k][1]]

    with ExitStack() as eo:
        Lslc = c.sb([128, NG, T], BF16, eo, "Lslc")
        Lwin = c.sb([64, NG, T], BF16, eo, "Lwin")
        Vslc = c.sb([128, NQ, NG, 65], BF16, eo, "Vslc")
        Vwin = c.sb([128, NQ, NG, 65], BF16, eo, "Vwin")
        KcT = c.sb([64, NG, 256], BF16, eo, "KcT")
        Vc = c.sb([128, 2, NG, 65], BF16, eo, "Vc")
        gates = c.sb([128, NQ, 48], F32, eo, "gates")
        cb = c.sb([128, 128 * 4 + 17 * 128], BF16, eo, "cb")
        rL = Res()
        rV = Res()
        rKc = Res()
        rG = Res()
        rqS = Res()
        Pm = cb[:, 0:128]
        tri = cb[:, 128:256]
        upper = cb[:, 256:384]
        ovl = cb[:, 384:512]
        maskc = cb[:, 512:512 + 17 * 128]
        isg = c.sb([128, 2], F32, eo, "isg")
        invf = isg[:, 0:1]
        sgn = isg[:, 1:2]
        with ExitStack() as et:
            ncs = c.sb([128, NSMALL], F32, et, "ncs")
            rn = Res()
            dma(c, "sp", ncs[:, :], ncs_in[:, :], [], [rn])
            cp(c, "dve", cb[:, :], ncs[:, 2:2 + 128 * 4 + 17 * 128], [rn], [rconst])
            cp(c, "dve", isg[:, :], ncs[:, 0:2], [rn], [rconst])
        c.P.barrier()
        c.P.op("pool", lambda e: e.memset(Vslc[:, :, :, :], 1.0), [], [rV])
        c.P.op("pool", lambda e: e.memset(Vwin[:, :, :, :], 1.0), [], [rV])
        c.P.op("pool", lambda e: e.memset(Vc[:, :, :, :], 0.0), [], [rKc])
        c.P.op("pool", lambda e: e.memset(Vc[:, :, :, 64:65], 1.0), [], [rKc])
        c.P.op("pool", lambda e: e.memset(KcT[:, :, :], 0.0), [], [rKc])

        with ExitStack() as ex:
            Acmp = c.sb([128, NG, T], BF16, ex, "Acmp")
            rA = Res()
            with ExitStack() as es:
                nb = NormBufs(c, es, nx=2)
                est = c.sb([128, 512], F32, es, "est")
                rest = Res()
                for ec in range(T // 512):
                    dma(c, "sp", est[64:128, :], eall_in[64:128, ec * 512:(ec + 1) * 512], [], [rest])
                    for g in range(NG):
                        cp(c, "pool", Lslc[64:128, g, ec * 512:(ec + 1) * 512], est[64:128, :], [rest], [rL])
                wtm = c.sb([128, KC, 560], BF16, es, "wtm")
                rwtm = Res()
                dma(c, "sp", wtm[:, :, :], W("wtm").rearrange("p (k j) -> p k j", k=KC), [r_w], [rwtm])
                wfm = [c.sb([128, KC, 128], BF16, es, "wfm") for _ in range(2)]
                rwfm = [Res() for _ in range(2)]
                posi = c.sb([128, TB], I32, es, "nposi")
                ang = c.sb([128, TB], F32, es, "nang")
                rang = Res()
                Ct = c.sb([128, TB], F32, es, "nC")
                St = c.sb([128, TB], F32, es, "nS")
                rC = Res()
                rS = Res()
                tb = TrigBufs(c, es, TB)
                xb = [c.sb([128, TB], BF16, es, "xb") for _ in range(2)]
                rxb = [Res() for _ in range(2)]
                t1 = [c.sb([128, TB], F32, es, "nt1") for _ in range(2)]
                t2 = [c.sb([128, TB], F32, es, "nt2") for _ in range(2)]
                rt1 = [Res() for _ in range(2)]
                rt2 = [Res() for _ in range(2)]
                kr = [c.sb([128, TB], BF16, es, "kr") for _ in range(2)]
                rkr = [Res() for _ in range(2)]
                pX = [c.ps(es) for _ in range(2)]
                rpX = [Res() for _ in range(2)]
                pP = [c.ps(es) for _ in range(2)]
                rpP = [Res() for _ in range(2)]
                ik = 0
                for blk in range(T // TB):
                    tok0 = blk * TB
                    norm_block(c, nb, x_dram, rx_tiles, tok0, gain_ap, consts)
                    dma(c, "sp", posi[:, :], pos_dram[tok0:tok0 + TB].partition_broadcast(128), [], [rang])
                    cp(c, "dve", ang[:, :], posi[:, :], [rang], [rang])
                    ts(c, "dve", ang[:, :], ang[:, :], invf, None, ALU.mult, None, [rang, rconst], [rang])
                    trig(c, tb, ang[:, :], rang, St[:, :], rS, 0.0, scale=sgn)
                    trig(c, tb, ang[:, :], rang, Ct[:, :], rC, np.pi / 2)
                    def post(m, b):
                        rot = not (m in (10, 11))
                        if rot:
                            cp(c, "act", xb[b][:, :], pX[b][:, :], [rpX[b]], [rxb[b]])
                            mm(c, pP[b][:, :], Pm, xb[b][:, :], True, True, [rconst, rxb[b]], [rpP[b]])
                            tt(c, "dve", t1[b][:, :], pP[b][:, :], St[:, :], ALU.mult, [rpP[b], rS], [rt1[b]])
                            tt(c, "pool", t2[b][:, :], xb[b][:, :], Ct[:, :], ALU.mult, [rxb[b], rC], [rt2[b]])
                            tt(c, "pool", kr[b][:, :], t1[b][:, :], t2[b][:, :], ALU.add, [rt1[b], rt2[b]], [rkr[b]])
                        else:
                            cp(c, "act", kr[b][:, :], pX[b][:, :], [rpX[b]], [rkr[b]])
                        ts_ = slice(tok0, tok0 + TB)
                        if m < 8:
                            dma(c, "pool", qS[:, 2 * m, ts_], kr[b][0:64, :], [rkr[b]], [rqS])
                            dma(c, "pool", qS[:, 2 * m + 1, ts_], kr[b][64:128, :], [rkr[b]], [rqS])
                        else:
                            gp = (m - 8) % 2
                            kind = (m - 8) // 2
                            for hh in range(2):
                                g = gp * 2 + hh
                                src = kr[b][hh * 64:(hh + 1) * 64, :]
                                if kind == 0:
                                    dma(c, "pool", Acmp[0:64, g, ts_], src, [rkr[b]], [rA])
                                elif kind == 1:
                                    dma(c, "pool", Acmp[64:128, g, ts_], src, [rkr[b]], [rA])
                                elif kind == 2:
                                    dma(c, "pool", Lslc[0:64, g, ts_], src, [rkr[b]], [rL])
                                else:
                                    dma(c, "pool", Lwin[0:64, g, ts_], src, [rkr[b]], [rL])

                    prev = None
                    for m in range(16):
                        b = ik % 2
                        ik += 1
                        dma(c, "sp", wfm[b][:, :, :],
                            W("wfm")[:, m * 1024:(m + 1) * 1024].rearrange("p (k j) -> p k j", k=KC), [r_w], [rwfm[b]])
                        for kc in range(KC):
                            mm(c, pX[b][:, :], wfm[b][:, kc, :], nb.hT[:, kc, :], kc == 0, kc == KC - 1,
                               [rwfm[b], nb.rhT], [rpX[b]])
                        if prev is not None:
                            post(*prev)
                        prev = (m, b)
                    post(*prev)
                    for t4 in range(4):
                        n = (tok0 // 128) + t4
                        b = t4 % 2
                        for kc in range(KC):
                            mm(c, pX[b][:, :], nb.hT[:, kc, t4 * 128:(t4 + 1) * 128], wtm[:, kc, 0:512], kc == 0,
                               kc == KC - 1, [nb.rhT, rwtm], [rpX[b]])
                        cp(c, "act", Vslc[:, n, :, 0:64], pX[b][:, 0:256].rearrange("p (g d) -> p g d", d=64),
                           [rpX[b]], [rV])
                        cp(c, "act", Vwin[:, n, :, 0:64], pX[b][:, 256:512].rearrange("p (g d) -> p g d", d=64),
                           [rpX[b]], [rV])
                        for kc in range(KC):
                            mm(c, pP[b][:, 0:48], nb.hT[:, kc, t4 * 128:(t4 + 1) * 128], wtm[:, kc, 512:560], kc == 0,
                               kc == KC - 1, [nb.rhT, rwtm], [rpP[b]])
                        act(c, gates[:, n, :], pP[b][:, 0:48], AF.Sigmoid, [rpP[b]], [rG])
            c.P.barrier()
            with ExitStack() as es:
                w1 = c.sb([128, 32, 256], BF16, es, "w1")
                posc = c.sb([128, 32], BF16, es, "posc")
                w2 = c.sb([128, 2, 2, 64], BF16, es, "w2")
                rw = Res()
                dma(c, "sp", w1[:, :, :], W("w1").rearrange("p (l m) -> p l m", m=256), [r_w], [rw])
                dma(c, "sp", posc[:, :], W("posc"), [r_w], [rw])
                dma(c, "sp", w2[:, :, :, :], W("w2").rearrange("p (a b d) -> p a b d", a=2, b=2), [r_w], [rw])
                hid = c.sb([128, 2, 1024], BF16, es, "hid")
                rhid = Res()
                c.P.op("pool", lambda e: e.memset(hid[:, :, :], 0.0), [], [rhid])
                bias = c.sb([128, 4], F32, es, "cbias")
                rb = Res()
                pH = [c.ps(es) for _ in range(2)]
                rpH = [Res() for _ in range(2)]
                pb = c.ps(es)
                rpb = Res()
                i2 = 0
                for kv in range(2):
                    rows = slice(kv * 64, kv * 64 + 64)
                    for mc in range(2):
                        for l in range(32):
                            mm(c, pb[:, 0:1], w1[rows, l, mc * 128:(mc + 1) * 128], posc[rows, l:l + 1], l == 0, l == 31,
                               [rw], [rpb])
                        cp(c, "dve", bias[:, kv * 2 + mc:kv * 2 + mc + 1], pb[:, 0:1], [rpb], [rb])
                    for mc in range(2):
                        for gp in range(2):
                            b = i2 % 2
                            i2 += 1
                            for l in range(32):
                                rhs = Acmp[rows, 2 * gp:2 * gp + 2, l:l + 16 * (n_cmp - 1) + 1:16]
                                mm(c, pH[b][:, 0:2 * n_cmp].rearrange("p (g i) -> p g i", g=2),
                                   w1[rows, l, mc * 128:(mc + 1) * 128], rhs, l == 0, l == 31, [rw, rA], [rpH[b]])
                            act(c, hid[:, mc, gp * 512:gp * 512 + 512].rearrange("p (g i) -> p g i", g=2)[:, :, 0:n_cmp],
                                pH[b][:, 0:2 * n_cmp].rearrange("p (g i) -> p g i", g=2), AF.Silu, [rpH[b], rb],
                                [rhid], bias=bias[:, kv * 2 + mc:kv * 2 + mc + 1])
                    if kv == 0:
                        for gp in range(2):
                            b = i2 % 2
                            i2 += 1
                            for mc in range(2):
                                mm(c, pH[b][0:64, :], w2[:, 0, mc, :], hid[:, mc, gp * 512:(gp + 1) * 512], mc == 0,
                                   mc == 1, [rw, rhid], [rpH[b]])
                            cp(c, "dve", KcT[:, 2 * gp:2 * gp + 2, :],
                               pH[b][0:64, :].rearrange("p (g i) -> p g i", g=2), [rpH[b]], [rKc])
                        c.P.op("dve", lambda e: e.memset(KcT[:, :, n_cmp:256], 0.0), [rKc], [rKc])
                    else:
                        for g in range(NG):
                            for it in range(2):
                                b = i2 % 2
                                i2 += 1
                                for mc in range(2):
                                    mm(c, pH[b][:, 0:64], hid[:, mc, g * 256 + it * 128:g * 256 + (it + 1) * 128],
                                       w2[:, 1, mc, :], mc == 0, mc == 1, [rhid, rw], [rpH[b]])
                                cp(c, "dve", Vc[:, it, g, 0:64], pH[b][:, 0:64], [rpH[b]], [rKc])
            c.P.barrier()
        with ExitStack() as es:
            bg.attach(c, es)
            wo = c.sb([128, 8, D], BF16, es, "nwo")
            rwo = Res()
            dma(c, "sp", wo[:, :, :], W("wo").rearrange("p (f n) -> p f n", n=D), [r_w], [rwo])
            R_ = [c.sb([128, 4, 128], BF16, es, "R") for _ in range(2)]
            rR = [Res() for _ in range(2)]
            Ec = [c.sb([128, 512], BF16, es, "Ec") for _ in range(2)]
            rEc = [Res() for _ in range(2)]
            Es = [c.sb([128, 512], BF16, es, "Es") for _ in range(2)]
            rEs = [Res() for _ in range(2)]
            imp = c.sb([128, 64], F32, es, "imp")
            imp2 = c.sb([128, 64], F32, es, "imp2")
            m8 = c.sb([128, 16], F32, es, "m8")
            rimp = Res()
            bsel = c.sb([128, 128], BF16, es, "bsel")
            rbsel = Res()
            c.P.op("pool", lambda e: e.memset(bsel[:, :], 0.0), [], [rbsel])
            den = c.sb([128, 16], F32, es, "den")
            rden = Res()
            coef = c.sb([128, 12], F32, es, "coef")
            rcoef = Res()
            oacc = c.sb([128, 4, 64], F32, es, "oacc")
            roacc = Res()
            Otok = c.sb([128, D], BF16, es, "Otok")
            rOtok = Res()
            OT = c.sb([128, 8, 128], BF16, es, "OT")
            rOT = Res()
            xt = c.sb([128, D], F32, es, "nxt")
            rxt = Res()
            xo = [c.sb([128, 512], F32, es, "nxo") for _ in range(2)]
            rxo = [Res() for _ in range(2)]
            pS = [c.ps(es) for _ in range(2)]
            rpS = [Res() for _ in range(2)]
            pOc = c.ps(es)
            pI = c.ps(es)
            pOs = c.ps(es)
            pOw = c.ps(es)
            pT = c.ps(es)
            pY = c.ps(es)
            rpOc, rpI, rpOs, rpOw, rpT, rpY = Res(), Res(), Res(), Res(), Res(), Res()
            iS = 0
            iR = 0

            def heads3(ps):
                return ps[:, 0:260].rearrange("p (h e) -> p h e", e=65)

            addm_t = [c.sb([128, 64], F32, es, "addm") for _ in range(2)]
            raddm = [Res() for _ in range(2)]
            xt2 = [xt, c.sb([128, D], F32, es, "nxt2")]
            rxt2 = [rxt, Res()]
            oacc2 = [oacc, c.sb([128, 4, 64], F32, es, "oacc2")]
            roacc2 = [roacc, Res()]
            st_ = {"pend": None, "iS": 0}

            def run_items(items):
                for item in items:
                    if item[0] == "it":
                        sb_ = st_["iS"] % 2
                        st_["iS"] += 1
                        item[1](sb_)
                        if st_["pend"] is not None:
                            st_["pend"][0](st_["pend"][1])
                        item[2](sb_)
                        st_["pend"] = (item[3], sb_)
                    elif item[0] == "flush":
                        if st_["pend"] is not None:
                            st_["pend"][0](st_["pend"][1])
                            st_["pend"] = None
                        item[1]()
                    else:
                        item[1]()

            def start_n(n):
                dma(c, "sp", xt2[n % 2][:, :], x_dram[n * 128:(n + 1) * 128, :], [rx_tiles[n]], [rxt2[n % 2]])
                dma(c, "sp", addm_t[n % 2][:, :], addm_in[:, n * 64:(n + 1) * 64], [], [raddm[n % 2]])

            def make_k(n, g, kk):
                qs = slice(n * 128, (n + 1) * 128)
                R = R_[kk % 2]
                rRr = rR[kk % 2]
                Rq = R[0:64, :, :]
                Rf = R[:, :, :]
                oacc_ = oacc2[kk % 2]
                roacc_ = roacc2[kk % 2]
                ntile = 1 if (8 * n + 6) < 128 else 2

                def load():
                    dma(c, "sp", R[0:64, :, :], qS[:, 4 * g:4 * g + 4, qs], [rqS], [rRr])

                def mk_iter(kind, kt, last_kt, first_kt):
                    def S_(sb_):
                        if kind == "c":
                            mm(c, pS[sb_][:, :].rearrange("p (h q) -> p h q", h=4),
                               KcT[0:64, g, kt * 128:(kt + 1) * 128], Rq, True, True, [rKc, rRr], [rpS[sb_]])
                        elif kind == "s":
                            mm(c, pS[sb_][:, :].rearrange("p (h q) -> p h q", h=4),
                               Lslc[:, g, kt * 128:(kt + 1) * 128], Rf, True, True, [rL, rRr], [rpS[sb_]])
                        else:
                            mm(c, pS[sb_][:, :].rearrange("p (h q) -> p h q", h=4),
                               Lwin[0:64, g, kt * 128:(kt + 1) * 128], Rq, True, True, [rL, rRr], [rpS[sb_]])

                    def E_(sb_):
                        act(c, Es[sb_][:, :], pS[sb_][:, :], AF.Exp, [rpS[sb_]], [rEs[sb_]], scale=0.125)
                        e3 = Es[sb_][:, :].rearrange("p (h q) -> p h q", h=4)
                        if kind == "c":
                            mi = min(n, 16) if kt == 0 else n - 16
                            if not (kt == 0 and n >= 17):
                                tt(c, "dve", e3, e3, bc_mid(maskc[:, mi * 128:(mi + 1) * 128], 4), ALU.mult,
                                   [rEs[sb_], rconst], [rEs[sb_]])
                        else:
                            if kt == n:
                                tt(c, "dve", e3, e3, bc_mid(tri, 4), ALU.mult, [rEs[sb_], rconst], [rEs[sb_]])
                            if kind == "w" and kt == n - 4:
                                tt(c, "dve", e3, e3, bc_mid(upper, 4), ALU.mult, [rEs[sb_], rconst], [rEs[sb_]])

                    def PV_(sb_):
                        for h in range(4):
                            s0 = (kt == first_kt and h == 0)
                            s1 = (kt == last_kt and h == 3)
                            lw = Es[sb_][:, h * 128:(h + 1) * 128]
                            if kind == "c":
                                mm(c, pOc[:, h * 65:(h + 1) * 65], lw, Vc[:, kt, g, :], s0, s1,
                                   [rEs[sb_], rKc], [rpOc], skip_group_check=True)
                                mm(c, pI[:, h * 64:(h + 1) * 64], lw, ovl[:, kt * 64:(kt + 1) * 64], s0, s1,
                                   [rEs[sb_], rconst], [rpI], skip_group_check=True)
                            elif kind == "s":
                                mm(c, pOs[:, h * 65:(h + 1) * 65], lw, Vslc[:, kt, g, :], s0, s1,
                                   [rEs[sb_], rV], [rpOs], skip_group_check=True)
                            else:
                                mm(c, pOw[:, h * 65:(h + 1) * 65], lw, Vwin[:, kt, g, :], s0, s1,
                                   [rEs[sb_], rV], [rpOw], skip_group_check=True)
                    return ("it", S_, E_, PV_)

                def sel_dve():
                    ts(c, "dve", den[:, 0:4], heads3(pOc)[:, :, 64], 1e-30, None, ALU.max, None, [rpOc], [rden])
                    c.P.op("dve", lambda e: e.reciprocal(den[:, 4:8], den[:, 0:4]), [rden], [rden])
                    stt(c, imp[:, :], pI[:, 0:64], den[:, 4:5], addm_t[n % 2][:, :], ALU.mult, ALU.add,
                        [rpI, rden, raddm[n % 2]], [rimp])
                    for h in range(1, 4):
                        stt(c, imp[:, :], pI[:, h * 64:(h + 1) * 64], den[:, 4 + h:5 + h], imp[:, :], ALU.mult,
                            ALU.add, [rpI, rden, rimp], [rimp])
                    c.P.op("dve", lambda e: e.max(m8[:, 0:8], imp[:, :]), [rimp], [rimp])
                    c.P.op("dve", lambda e: e.match_replace(imp2[:, :], m8[:, 0:8], imp[:, :], -3.0e38), [rimp],
                           [rimp])
                    c.P.op("dve", lambda e: e.max(m8[:, 8:16], imp2[:, :]), [rimp], [rimp])
                    ts(c, "dve", bsel[:, 64:128], imp[:, :], m8[:, 15:16], NEGB, ALU.is_lt, ALU.mult, [rimp],
                       [rbsel])
                    tt(c, "dve", coef[:, 0:4], den[:, 4:8], gates[:, n, (4 * g) * 3:(4 * g + 4) * 3:3], ALU.mult,
                       [rden, rG], [rcoef])
                    for h in range(4):
                        ts(c, "dve", oacc_[:, h, :], heads3(pOc)[:, h, 0:64], coef[:, h:h + 1], None, ALU.mult,
                           None, [rpOc, rcoef], [roacc_])

                def sel_pe():
                    tr(c, pT[:, :].bitcast(BF16)[:, 0:128], bsel[:, :], ident[:, :], [rbsel, rconst], [rpT])
                    cp(c, "act", R[64:128, :, :], bc_mid(pT[:, :].bitcast(BF16)[64:128, 0:128], 4), [rpT], [rRr])

                def combine():
                    for br, (po, rpo) in ((2, (pOw, rpOw)), (1, (pOs, rpOs))):
                        d0 = 8 if br == 1 else 0
                        ts(c, "dve", den2[:, d0:d0 + 4], heads3(po)[:, :, 64], 1e-30, None, ALU.max, None, [rpo],
                           [rden2])
                        c.P.op("dve", lambda e, d0=d0: e.reciprocal(den2[:, d0 + 4:d0 + 8], den2[:, d0:d0 + 4]),
                               [rden2], [rden2])
                        gsl = gates[:, n, (4 * g) * 3 + br:(4 * g + 4) * 3:3]
                        tt(c, "dve", coef2[:, br * 4:(br + 1) * 4], den2[:, d0 + 4:d0 + 8], gsl, ALU.mult,
                           [rden2, rG], [rcoef2])
                    for h in range(4):
                        stt(c, oacc_[:, h, :], heads3(pOw)[:, h, 0:64], coef2[:, 8 + h:9 + h], oacc_[:, h, :],
                            ALU.mult, ALU.add, [rpOw, rcoef2, roacc_], [roacc_])
                    for h in range(4):
                        stt(c, Otok[:, (4 * g + h) * 64:(4 * g + h + 1) * 64], heads3(pOs)[:, h, 0:64],
                            coef2[:, 4 + h:5 + h], oacc_[:, h, :], ALU.mult, ALU.add, [rpOs, rcoef2, roacc_],
                            [rOtok])

                k0 = max(0, n - 4)
                return {
                    "load": load,
                    "cmp": [mk_iter("c", it, ntile - 1, 0) for it in range(ntile)] + [("flush", sel_dve)],
                    "win": [mk_iter("w", kt, n, k0) for kt in range(k0, n + 1)] + [("noflush", sel_pe)],
                    "slc": [mk_iter("s", kt, n, 0) for kt in range(n + 1)] + [("flush", combine)],
                }

            def out_proj(n):
                tpb = pT[:, :].bitcast(BF16)
                for f in range(8):
                    tr(c, tpb[:, f * 128:(f + 1) * 128], Otok[:, f * 128:(f + 1) * 128], ident[:, :], [rOtok, rconst],
                       [rpT])
                cp(c, "act", OT[:, :, :], tpb[:, :].rearrange("p (f t) -> p f t", f=8), [rpT], [rOT])
                for nh in range(2):
                    for f in range(8):
                        mm(c, pY[:, :], OT[:, f, :], wo[:, f, nh * 512:(nh + 1) * 512], f == 0, f == 7, [rOT, rwo],
                           [rpY])
                    tt(c, "dve", xo[nh][:, :], pY[:, :], xt2[n % 2][:, nh * 512:(nh + 1) * 512], ALU.add,
                       [rpY, rxt2[n % 2]], [rxo[nh]])
                    dma(c, "sp", x_dram[n * 128:(n + 1) * 128, nh * 512:(nh + 1) * 512], xo[nh][:, :], [rxo[nh]],
                        [rx_tiles[n]])

            den2 = c.sb([128, 16], F32, es, "den2")
            rden2 = Res()
            coef2 = c.sb([128, 12], F32, es, "coef2")
            rcoef2 = Res()
            ks = [(n, g) for n in range(NQ) for g in range(NG)]
            start_n(0)
            cur = make_k(0, 0, 0)
            cur["load"]()
            run_items(cur["cmp"])
            run_items(cur["win"])
            for kk in range(len(ks)):
                n, g = ks[kk]
                nxt = None
                if kk + 1 < len(ks):
                    n1, g1 = ks[kk + 1]
                    if g1 == 0:
                        start_n(n1)
                    nxt = make_k(n1, g1, kk + 1)
                    nxt["load"]()
                    run_items(nxt["cmp"])
                run_items(cur["slc"])
                if g == NG - 1:
                    out_proj(n)
                if nxt is not None:
                    run_items(nxt["win"])
                bg.step(c)
                cur = nxt
            bg.flush(c)


def lay_gain(g):
    return np.ascontiguousarray(g.reshape(KC, 128).T)


def lay_wgu(w):
    a = w.reshape(KC, 128, 2, FC, 128)
    return np.ascontiguousarray(a.transpose(1, 3, 0, 2, 4).reshape(128, -1))


def lay_wd(w):
    return np.ascontiguousarray(w.reshape(FC, 128, D).transpose(1, 0, 2).reshape(128, -1))


def build(T, plan):
    nc = bass.Bass("TRN2", target_bir_lowering=False)
    nlay = 4
    x_in = nc.dram_tensor("x", [T, D], F32, kind="ExternalInput")
    gains_in = nc.dram_tensor("gains", [128, 8 * KC], F32, kind="ExternalInput")
    gfin_in = nc.dram_tensor("gfin", [D], F32, kind="ExternalInput")
    wgu_in = nc.dram_tensor("wgu", [nlay, 128, FC * 2048], F32, kind="ExternalInput")
    wd_in = nc.dram_tensor("wd", [nlay, 128, FC * D], F32, kind="ExternalInput")
    ident_in = nc.dram_tensor("ident", [128, 128], F32, kind="ExternalInput")
    pos_in = nc.dram_tensor("pos", [T], I32, kind="ExternalInput")
    rwqk_in = nc.dram_tensor("rwqk", [2, 128, 16 * 1024], F32, kind="ExternalInput")
    rwvg_in = nc.dram_tensor("rwvg", [2, 128, 8 * 4096], F32, kind="ExternalInput")
    rwo_in = nc.dram_tensor("rwo", [2, 128, 16 * D], F32, kind="ExternalInput")
    rc_in = nc.dram_tensor("rc", [128, 521], F32, kind="ExternalInput")
    nsaw_in = nc.dram_tensor("nsaw", [2, 128, NSA_WTOT], F32, kind="ExternalInput")
    ncs_in = nc.dram_tensor("ncs", [128, NSMALL], F32, kind="ExternalInput")
    eall_in = nc.dram_tensor("eall", [128, T], F32, kind="ExternalInput")
    addm_in = nc.dram_tensor("addm", [128, (T // 128) * 64], F32, kind="ExternalInput")
    out = nc.dram_tensor("out", [T, D], F32, kind="ExternalOutput")

    with ExitStack() as es:
        c = Ctx(nc, es)
        xs = c.dram([T, D], F32, "xres")
        rx_tiles = [Res() for _ in range(T // 128)]
        consts = {}
        rconst = Res()
        consts["r"] = rconst
        identf = c.sb([128, 128], F32, None, "identf")
        ident = c.sb([128, 128], BF16, None, "ident")
        gains = c.sb([128, 8 * KC], F32, None, "gains")
        eps = c.sb([128, 1], F32, None, "eps")
        consts["ident"] = ident
        consts["eps"] = eps
        rtmp = Res()
        dma(c, "sp", identf[:, :], ident_in[:, :], [], [rtmp])
        cp(c, "dve", ident[:, :], identf[:, :], [rtmp], [rconst])
        dma(c, "sp", gains[:, :], gains_in[:, :], [], [rconst])
        c.P.op("dve", lambda e: e.memset(eps[:, :], EPS), [], [rconst])
        rc_sb = c.sb([128, 521], F32, None, "rc_sb")
        dma(c, "sp", rc_sb[:, :], rc_in[:, :], [], [rconst])
        _, cdec = ret_consts_host()
        qS = c.dram([64, 16, T], BF16, "qS")
        for i in range(T // 128):
            dma(c, "sp", xs[i * 128:(i + 1) * 128, :], x_in[i * 128:(i + 1) * 128, :], [], [rx_tiles[i]])
        wgu_s = {}
        wd_s = {}
        rw_s = {}
        nw_s = {}
        r_w = {}
        wsets = {}
        for p in plan:
            key = (p[0], p[1]) if p[0] != "final" else None
            if key is None or key in wsets:
                continue
            r_w[key] = Res()
            if p[0] == "ffn":
                l = p[1]
                wgu_s[l] = c.dram([128, FC * 2048], BF16, "wgus")
                wd_s[l] = c.dram([128, FC * D], BF16, "wds")
                wsets[key] = [
                    (lambda lo, hi, l=l: wgu_in[l, :, lo:hi], lambda lo, hi, l=l: wgu_s[l][:, lo:hi], FC * 2048),
                    (lambda lo, hi, l=l: wd_in[l, :, lo:hi], lambda lo, hi, l=l: wd_s[l][:, lo:hi], FC * D)]
            elif p[0] == "ret":
                j = p[1]
                rw_s[j] = (c.dram([128, 16 * 1024], BF16, "rwqks"), c.dram([128, 8 * 4096], BF16, "rwvgs"),
                           c.dram([128, 16 * D], BF16, "rwos"))
                wsets[key] = [(lambda lo, hi, j=j, src_t=src_t: src_t[j, :, lo:hi],
                               lambda lo, hi, dst_t=dst_t: dst_t[:, lo:hi], N)
                              for (src_t, dst_t, N) in ((rwqk_in, rw_s[j][0], 16 * 1024),
                                                        (rwvg_in, rw_s[j][1], 8 * 4096),
                                                        (rwo_in, rw_s[j][2], 16 * D))]
            elif p[0] == "nsa":
                j = p[1]
                nw_s[j] = c.dram([128, NSA_WTOT], BF16, "nsaws")
                wsets[key] = [(lambda lo, hi, j=j: nsaw_in[j, :, lo:hi], lambda lo, hi, j=j: nw_s[j][:, lo:hi],
                               NSA_WTOT)]
        keys = [((p[0], p[1]) if p[0] != "final" else None) for p in plan]
        done = set()
        upfront = []
        for k in keys[:2]:
            if k is not None and k not in done:
                upfront += wsets[k]
                done.add(k)
        bgs = {}
        for i, p in enumerate(plan):
            todo = []
            if p[0] == "ffn":
                if i + 1 < len(plan) and keys[i + 1] is not None and keys[i + 1] not in done:
                    todo.append(keys[i + 1])
            elif p[0] == "nsa":
                for i2 in range(i + 1, len(plan)):
                    if plan[i2][0] == "nsa":
                        break
                    if keys[i2] is not None and keys[i2] not in done:
                        todo.append(keys[i2])
            its = []
            for k in todo:
                if k not in done:
                    its += wsets[k]
                    done.add(k)
            bgs[i] = BgConv(its)
        for k in keys:
            assert k is None or k in done, k
        convert_weights(c, upfront)
        for i, p in enumerate(plan):
            c.P.barrier()
            if p[0] == "nsa":
                j, li = p[1], p[2]
                nsa_layer(c, T, xs, rx_tiles, pos_in.ap(), nw_s[j], r_w[("nsa", j)], gains[:, li * KC:(li + 1) * KC],
                          consts, ncs_in, eall_in, addm_in, qS, bg=bgs[i])
                continue
            if p[0] == "ret":
                j, li = p[1], p[2]
                ret_pass(c, T, xs, rx_tiles, pos_in.ap(), rw_s[j][0], rw_s[j][1], rw_s[j][2], r_w[("ret", j)],
                         gains[:, li * KC:(li + 1) * KC], consts, rc_sb, cdec)
                continue
            if p[0] == "ffn":
                l = p[1]
                ffn_pass(c, T, xs, rx_tiles, wgu_s[l], r_w[("ffn", l)], wd_s[l], r_w[("ffn", l)],
                         gains[:, (4 + l) * KC:(5 + l) * KC], consts, bg=bgs[i])
            elif p[0] == "final":
                final_pass(c, T, xs, rx_tiles, out, gfin_in, rconst, consts)
        c.P.emit()
    return nc


def prep_common(inputs):
    g = np.concatenate([lay_gain(inputs["norm_mix"][i]) for i in range(4)] +
                       [lay_gain(inputs["norm_ffn"][i]) for i in range(4)], axis=1)
    m = {
        "gains": np.ascontiguousarray(g, dtype=np.float32),
        "gfin": np.ascontiguousarray(inputs["norm_final"], dtype=np.float32),
        "wgu": np.stack([lay_wgu(inputs["ffn_w_gu"][i]) for i in range(4)]),
        "wd": np.stack([lay_wd(inputs["ffn_w_down"][i]) for i in range(4)]),
        "ident": np.eye(128, dtype=np.float32),
    }
    lr = [lay_ret(inputs["ret_w_in"][j], inputs["ret_w_out"][j]) for j in range(2)]
    m["rwqk"] = np.stack([l[0] for l in lr])
    m["rwvg"] = np.stack([l[1] for l in lr])
    m["rwo"] = np.stack([l[2] for l in lr])
    m["rc"] = ret_consts_host()[0]
    m["nsaw"] = np.stack([lay_nsa(inputs["nsa_w_in"][j], inputs["nsa_cmp_pos"][j], inputs["nsa_cmp_w1"][j],
                                  inputs["nsa_cmp_w2"][j], inputs["nsa_w_out"][j]) for j in range(2)])
    return m


def run(inputs, T, plan, ncores, trace=False):
    nc = build(T, plan)
    common = prep_common(inputs)
    common["ncs"], common["eall"], common["addm"] = nsa_consts_host(T)
    in_maps = []
    for b in range(ncores):
        m = dict(common)
        m["x"] = np.ascontiguousarray(inputs["x"][b, :T], dtype=np.float32)
        m["pos"] = np.ascontiguousarray(inputs["positions"][b, :T], dtype=np.int32)
        in_maps.append(m)
    res = run_bass_kernel_spmd(nc, in_maps, core_ids=list(range(ncores)), trace=trace)
    return np.stack([r["out"] for r in res.results], axis=0), res


def kernel(**inputs):
    inputs = {k: np.asarray(v) for k, v in inputs.items()}
    plan = [("ret", 0, 0), ("ffn", 0), ("nsa", 0, 1), ("ffn", 1), ("ret", 1, 2), ("ffn", 2), ("nsa", 1, 3),
            ("ffn", 3), ("final",)]
    out, _ = run(inputs, 4096, plan, 8)
    return out.astype(np.float32)
```
